# Optimizing a Trainium2 kernel written in Bass

```python
import math
import jax, jax.numpy as jnp
from jax import lax
import numpy as np

D_MODEL = 1024
BATCH = 8
SEQ = 4096
DEPTH = 4

ATTN_GROUPS = ((128, 1), (512, 4), (2048, 16))
N_ATTN_GROUPS = len(ATTN_GROUPS)
HEADS_PER_GROUP = 6
ATTN_HEADS = N_ATTN_GROUPS * HEADS_PER_GROUP
HEAD_DIM = 64
ATTN_WIDTH = ATTN_HEADS * HEAD_DIM
ATTN_OUT_WIDTH = HEADS_PER_GROUP * HEAD_DIM
REL_BUCKETS = 32
REL_MAX_DISTANCE = 2048
POOL_WINDOWS = (2, 4, 8, 16)
POOL_WIDTH = D_MODEL
POOL_GROUP = POOL_WIDTH // len(POOL_WINDOWS)
SSD_INNER = D_MODEL
SSD_HEAD_DIM = 64
SSD_HEADS = SSD_INNER // SSD_HEAD_DIM
SSD_GROUPS = 2
SSD_STATE = 128
SSD_CONV = 4
SSD_CHUNK = 128
SSD_XBC = SSD_INNER + 2 * SSD_GROUPS * SSD_STATE
D_FF = 2816
FFN_CONV = 3
N_BRANCH = 3
NORM_EPS = 1e-6
IN_SPLIT_SIZES = (ATTN_WIDTH, ATTN_WIDTH, ATTN_WIDTH, POOL_WIDTH, SSD_INNER, SSD_XBC, SSD_HEADS, N_BRANCH * D_MODEL)
IN_WIDTH = sum(IN_SPLIT_SIZES)

kernel_name = "hybrid_gated_dilattn_pool_ssd_trunk"


def rmsnorm(x, g):
    xf = x.astype(jnp.float32)
    y = xf * lax.rsqrt(jnp.mean(xf * xf, axis=-1, keepdims=True) + NORM_EPS)
    return (y * g.astype(jnp.float32)).astype(x.dtype)


def causal_dwconv(x, w, b):
    K = w.shape[0]
    s = x.shape[1]
    xp = jnp.pad(x, ((0, 0), (K - 1, 0), (0, 0)))
    y = xp[:, 0:s] * w[0]
    for k in range(1, K):
        y = y + xp[:, k:k + s] * w[k]
    return y + b


def t5_bucket(dist):
    max_exact = REL_BUCKETS // 2
    is_small = dist < max_exact
    nf = jnp.maximum(dist, 1).astype(jnp.float32)
    large = max_exact + (jnp.log(nf / max_exact) / math.log(REL_MAX_DISTANCE / max_exact)
                         * (REL_BUCKETS - max_exact)).astype(jnp.int32)
    large = jnp.minimum(large, REL_BUCKETS - 1)
    return jnp.where(is_small, dist, large)


def dilated_group_attention(q, k, v, bias_table, dilation, steps):
    b, s, h, dh = q.shape
    L = s // dilation
    W = steps
    nb = -(-L // W)
    Lp = nb * W
    bd = b * dilation

    def to_res(t):
        return t.reshape(b, L, dilation, h, dh).transpose(0, 2, 3, 1, 4).reshape(bd, h, L, dh)

    qb = jnp.pad(to_res(q), ((0, 0), (0, 0), (0, Lp - L), (0, 0))).reshape(bd, h, nb, W, dh)
    kr = jnp.pad(to_res(k), ((0, 0), (0, 0), (W, Lp - L), (0, 0))).reshape(bd, h, nb + 1, W, dh)
    vr = jnp.pad(to_res(v), ((0, 0), (0, 0), (W, Lp - L), (0, 0))).reshape(bd, h, nb + 1, W, dh)
    kb = jnp.concatenate([kr[:, :, :-1], kr[:, :, 1:]], axis=3)
    vb = jnp.concatenate([vr[:, :, :-1], vr[:, :, 1:]], axis=3)

    qi = jnp.arange(W)[:, None]
    kk = jnp.arange(2 * W)[None, :]
    rel = qi + W - kk
    band = (rel >= 0) & (rel <= W)
    kabs = jnp.arange(nb)[:, None, None] * W + kk[None] - W
    valid = band[None] & (kabs >= 0)
    bucket = t5_bucket(jnp.clip(rel, 0, None) * dilation)
    bias = bias_table[bucket].transpose(2, 0, 1).astype(jnp.float32)

    scale = 1.0 / math.sqrt(dh)
    scores = jnp.einsum("bhnqd,bhnkd->bhnqk", qb, kb).astype(jnp.float32) * scale + bias[:, None]
    scores = jnp.where(valid[None, None], scores, -jnp.inf)
    m = jnp.max(scores, axis=-1, keepdims=True)
    p = jnp.exp(scores - m)
    l = jnp.sum(p, axis=-1)
    o = jnp.einsum("bhnqk,bhnkd->bhnqd", p, vb.astype(jnp.float32)) / l[..., None]
    lse = m[..., 0] + jnp.log(l)

    o = o.reshape(bd, h, Lp, dh)[:, :, :L].reshape(b, dilation, h, L, dh)
    o = o.transpose(0, 3, 1, 2, 4).reshape(b, s, h, dh)
    lse = lse.reshape(bd, h, Lp)[:, :, :L].reshape(b, dilation, h, L)
    lse = lse.transpose(0, 3, 1, 2).reshape(b, s, h)
    return o, lse


def dilated_attention_mixer(q, k, v, rel_bias):
    b, s = q.shape[:2]
    outs, lses = [], []
    for gi, (window, dilation) in enumerate(ATTN_GROUPS):
        hs = slice(gi * HEADS_PER_GROUP, (gi + 1) * HEADS_PER_GROUP)
        o, lse = dilated_group_attention(q[:, :, hs], k[:, :, hs], v[:, :, hs],
                                         rel_bias[:, hs], dilation, window // dilation)
        outs.append(o)
        lses.append(lse)
    o = jnp.stack(outs, axis=0)
    alpha = jax.nn.softmax(jnp.stack(lses, axis=0), axis=0)
    o = jnp.sum(alpha[..., None] * o, axis=0)
    return o.reshape(b, s, ATTN_OUT_WIDTH).astype(q.dtype)


def pool_mixer(u, w_grp, scale):
    b, s, _ = u.shape
    uf = u.astype(jnp.float32)
    cs = jnp.cumsum(uf, axis=1)
    pos = (jnp.arange(s) + 1)[None, :, None]
    outs = []
    for gi, w in enumerate(POOL_WINDOWS):
        sl = slice(gi * POOL_GROUP, (gi + 1) * POOL_GROUP)
        c = cs[..., sl]
        shifted = jnp.pad(c, ((0, 0), (w, 0), (0, 0)))[:, :s]
        cnt = jnp.minimum(pos, w).astype(jnp.float32)
        outs.append((c - shifted) / cnt - uf[..., sl])
    d = jnp.stack(outs, axis=2).astype(u.dtype)
    y = jnp.einsum("bsgc,gcd->bsgd", d, w_grp).reshape(b, s, POOL_WIDTH)
    return y * scale


def ssd_scan(x, dt, A, Bm, Cm):
    b, s, h, p = x.shape
    g, n = Bm.shape[2:]
    e = h // g
    l = SSD_CHUNK
    c = s // l
    xc = (x * dt[..., None]).reshape(b, c, l, g, e, p)
    a = (dt * A).reshape(b, c, l, h).transpose(0, 3, 1, 2)
    a_cs = jnp.cumsum(a, axis=-1)
    Bc = Bm.reshape(b, c, l, g, n)
    Cc = Cm.reshape(b, c, l, g, n)
    causal = jnp.tril(jnp.ones((l, l), dtype=bool))
    seg = a_cs[..., :, None] - a_cs[..., None, :]
    Lmat = jnp.exp(jnp.where(causal, seg, -jnp.inf)).reshape(b, g, e, c, l, l)
    cb = jnp.einsum("bclgn,bcsgn->bcgls", Cc, Bc)
    y_diag = jnp.einsum("bcgls,bgecls,bcsgep->bclgep", cb, Lmat, xc)
    decay = jnp.exp(a_cs[..., -1:] - a_cs).reshape(b, g, e, c, l)
    states = jnp.einsum("bclgn,bgecl,bclgep->bcgepn", Bc, decay, xc)
    chunk_decay = jnp.exp(a_cs[..., -1]).reshape(b, g, e, c)

    def step(carry, inp):
        st, dec = inp
        return carry * dec[..., None, None] + st, carry

    init = jnp.zeros((b, g, e, p, n), dtype=x.dtype)
    _, prev = lax.scan(step, init, (states.transpose(1, 0, 2, 3, 4, 5), chunk_decay.transpose(3, 0, 1, 2)))
    prev = prev.transpose(1, 0, 2, 3, 4, 5)
    out_decay = jnp.exp(a_cs).reshape(b, g, e, c, l)
    y_off = jnp.einsum("bclgn,bcgepn,bgecl->bclgep", Cc, prev, out_decay)
    return (y_diag + y_off).reshape(b, s, h, p)


def ssd_mixer(z, xbc, dt_raw, conv_w, conv_b, dt_bias, a_log, d_skip, norm_w):
    b, s, _ = z.shape
    xbc = jax.nn.silu(causal_dwconv(xbc, conv_w, conv_b))
    xs = xbc[..., :SSD_INNER].reshape(b, s, SSD_HEADS, SSD_HEAD_DIM).astype(jnp.float32)
    Bm = xbc[..., SSD_INNER:SSD_INNER + SSD_GROUPS * SSD_STATE].reshape(b, s, SSD_GROUPS, SSD_STATE).astype(jnp.float32)
    Cm = xbc[..., SSD_INNER + SSD_GROUPS * SSD_STATE:].reshape(b, s, SSD_GROUPS, SSD_STATE).astype(jnp.float32)
    dt = jax.nn.softplus(dt_raw.astype(jnp.float32) + dt_bias.astype(jnp.float32))
    A = -jnp.exp(a_log.astype(jnp.float32))
    y = ssd_scan(xs, dt, A, Bm, Cm) + d_skip.astype(jnp.float32)[:, None] * xs
    y = y.reshape(b, s, SSD_INNER) * jax.nn.silu(z.astype(jnp.float32))
    yg = y.reshape(b, s, SSD_GROUPS, SSD_INNER // SSD_GROUPS)
    yg = yg * lax.rsqrt(jnp.mean(yg * yg, axis=-1, keepdims=True) + NORM_EPS)
    y = yg.reshape(b, s, SSD_INNER) * norm_w.astype(jnp.float32)
    return y.astype(z.dtype)


def conv_ffn(u, w_up, conv_w, conv_b, w_down):
    h = causal_dwconv(u @ w_up, conv_w, conv_b)
    a, v = h[..., :D_FF], h[..., D_FF:]
    return (jax.nn.silu(a) * v) @ w_down


def setup_inputs(seed: int = 0) -> dict:
    key = jax.random.key(seed)
    ks = jax.random.split(key, 24)
    f32 = jnp.float32

    def nrm(k, shape, scale):
        return jax.random.normal(k, shape, dtype=f32) * scale

    dt0 = jnp.exp(jax.random.uniform(ks[10], (DEPTH, SSD_HEADS), dtype=f32)
                  * (math.log(0.1) - math.log(0.001)) + math.log(0.001))
    return {
        "x": nrm(ks[0], (BATCH, SEQ, D_MODEL), 1.0),
        "rel_bias": nrm(ks[1], (REL_BUCKETS, ATTN_HEADS), 0.1),
        "ln1_g": 1.0 + nrm(ks[2], (DEPTH, D_MODEL), 0.02),
        "w_in": nrm(ks[3], (DEPTH, D_MODEL, IN_WIDTH), D_MODEL ** -0.5),
        "b_gate": nrm(ks[4], (DEPTH, N_BRANCH * D_MODEL), 0.02),
        "w_a": nrm(ks[5], (DEPTH, ATTN_OUT_WIDTH, D_MODEL), ATTN_OUT_WIDTH ** -0.5),
        "pool_w": nrm(ks[6], (DEPTH, len(POOL_WINDOWS), POOL_GROUP, POOL_GROUP), POOL_GROUP ** -0.5),
        "pool_scale": 1.0 + nrm(ks[7], (DEPTH, POOL_WIDTH), 0.1),
        "w_b": nrm(ks[8], (DEPTH, POOL_WIDTH, D_MODEL), POOL_WIDTH ** -0.5),
        "ssd_conv_w": nrm(ks[9], (DEPTH, SSD_CONV, SSD_XBC), SSD_CONV ** -0.5),
        "ssd_conv_b": nrm(ks[11], (DEPTH, SSD_XBC), 0.02),
        "ssd_dt_bias": dt0 + jnp.log(-jnp.expm1(-dt0)),
        "ssd_a_log": jnp.log(jax.random.uniform(ks[12], (DEPTH, SSD_HEADS), dtype=f32, minval=1.0, maxval=16.0)),
        "ssd_d": 1.0 + nrm(ks[13], (DEPTH, SSD_HEADS), 0.1),
        "ssd_norm_w": 1.0 + nrm(ks[14], (DEPTH, SSD_INNER), 0.02),
        "w_c": nrm(ks[15], (DEPTH, SSD_INNER, D_MODEL), SSD_INNER ** -0.5),
        "w_o": nrm(ks[16], (DEPTH, D_MODEL, D_MODEL), D_MODEL ** -0.5),
        "ln2_g": 1.0 + nrm(ks[17], (DEPTH, D_MODEL), 0.02),
        "ffn_w_up": nrm(ks[18], (DEPTH, D_MODEL, 2 * D_FF), D_MODEL ** -0.5),
        "ffn_conv_w": nrm(ks[19], (DEPTH, FFN_CONV, 2 * D_FF), FFN_CONV ** -0.5),
        "ffn_conv_b": nrm(ks[20], (DEPTH, 2 * D_FF), 0.02),
        "ffn_w_down": nrm(ks[21], (DEPTH, D_FF, D_MODEL), D_FF ** -0.5),
        "final_g": 1.0 + nrm(ks[22], (D_MODEL,), 0.02),
    }


def reference(x, rel_bias, ln1_g, w_in, b_gate, w_a, pool_w, pool_scale, w_b,
              ssd_conv_w, ssd_conv_b, ssd_dt_bias, ssd_a_log, ssd_d, ssd_norm_w, w_c,
              w_o, ln2_g, ffn_w_up, ffn_conv_w, ffn_conv_b, ffn_w_down, final_g):
    b, s, _ = x.shape
    split_idx = []
    acc = 0
    for sz in IN_SPLIT_SIZES[:-1]:
        acc += sz
        split_idx.append(acc)
    for i in range(DEPTH):
        u = rmsnorm(x, ln1_g[i])
        proj = u @ w_in[i]
        q, k, v, pool_in, z, xbc, dt_raw, gate_pre = jnp.split(proj, split_idx, axis=-1)
        q = q.reshape(b, s, ATTN_HEADS, HEAD_DIM)
        k = k.reshape(b, s, ATTN_HEADS, HEAD_DIM)
        v = v.reshape(b, s, ATTN_HEADS, HEAD_DIM)
        y_a = dilated_attention_mixer(q, k, v, rel_bias) @ w_a[i]
        y_b = pool_mixer(pool_in, pool_w[i], pool_scale[i]) @ w_b[i]
        y_c = ssd_mixer(z, xbc, dt_raw, ssd_conv_w[i], ssd_conv_b[i], ssd_dt_bias[i],
                        ssd_a_log[i], ssd_d[i], ssd_norm_w[i]) @ w_c[i]
        gates = jax.nn.sigmoid(gate_pre + b_gate[i]).reshape(b, s, N_BRANCH, D_MODEL)
        merged = gates[:, :, 0] * y_a + gates[:, :, 1] * y_b + gates[:, :, 2] * y_c
        x = x + merged @ w_o[i]
        x = x + conv_ffn(rmsnorm(x, ln2_g[i]), ffn_w_up[i], ffn_conv_w[i], ffn_conv_b[i], ffn_w_down[i])
    return rmsnorm(x, final_g)
```

```python
import math
import numpy as np
from contextlib import ExitStack
import concourse.bass as bass
import concourse.mybir as mybir
from concourse.bass_utils import run_bass_kernel_spmd

F32 = mybir.dt.float32
BF16 = mybir.dt.bfloat16
AF = mybir.ActivationFunctionType
ALU = mybir.AluOpType

D = 1024
SEQ = 4096
DEPTH = 4
NCORES = 8
IN_W = 10128
Q0, K0, V0, P0, Z0, X0, DT0, G0 = 0, 1152, 2304, 3456, 4480, 5504, 7040, 7056
DFF = 2816
EPS = 1e-6
GROUPS = ((128, 1), (512, 4), (2048, 16))
POOLW = (2, 4, 8, 16)
NEG = -30000.0
import os
SSD_LEVEL = int(os.environ.get('SSD_LEVEL', '9'))
SKIP = os.environ.get('SKIP', '').split(',')
FFN_ACT_TAP = int(os.environ.get('FFN_ACT_TAP', '1'))
OPLIMIT = int(os.environ.get('OPLIMIT', '1000000000'))

PP_LN1, PP_LN2, PP_BG, PP_PSC, PP_SCW, PP_SCB, PP_NW, PP_DSK, PP_FCW, PP_FCB, PP_DTB, PP_ALOG, PP_FG = (
    0, 8, 16, 40, 48, 96, 108, 116, 124, 256, 300, 316, 332)
NPP = 340
C_ID, C_ONE, C_TRI, C_BAND = 0, 128, 256, 384
NCST = 384 + 12 * 128

ENGS = ("pe", "act", "dve", "pool", "sp")


class Buf:
    __slots__ = ("name", "w", "rs")

    def __init__(self, name=""):
        self.name = name
        self.w = None
        self.rs = []


class Op:
    __slots__ = ("eng", "idx", "fn", "deps", "dma", "sem", "val", "inc", "waits", "presem")

    def __init__(self, eng, idx, fn, dma):
        self.eng = eng
        self.idx = idx
        self.fn = fn
        self.deps = {}
        self.dma = dma
        self.sem = None
        self.val = 0
        self.inc = False
        self.waits = []
        self.presem = None


class Sched:
    def __init__(self, nc, n_dma_sems=40, raw_gap=3):
        self.nc = nc
        self.streams = {e: [] for e in ENGS}
        self.n_dma_sems = n_dma_sems
        self.raw_gap = raw_gap
        self.bar = None
        self.bar_seen = set()
        self.region = False
        self.rcount = 0

    def _add_dep(self, op, d, raw):
        if d is None or d is op:
            return
        if d.eng == op.eng and not d.dma:
            if op.eng == "pe" or not raw or op.dma:
                return
            if op.idx - d.idx >= self.raw_gap:
                return
        if d.dma:
            op.deps[("dma", d.eng, d.idx)] = d
        else:
            cur = op.deps.get(d.eng)
            if cur is None or cur.idx < d.idx:
                op.deps[d.eng] = d

    def barrier(self):
        self.bar = [st[-1] for st in self.streams.values() if st]
        self.bar_seen = set()

    def op(self, eng, fn, reads=(), writes=(), dma=False):
        if self.region:
            self.rcount += 1
            if self.rcount > OPLIMIT:
                return None
        st = self.streams[eng]
        o = Op(eng, len(st), fn, dma)
        if self.bar is not None and eng not in self.bar_seen:
            self.bar_seen.add(eng)
            for d in self.bar:
                if d.eng != eng or d.dma:
                    self._add_dep(o, d, False)
        for b in reads:
            self._add_dep(o, b.w, True)
        for b in writes:
            self._add_dep(o, b.w, False)
            for r in b.rs:
                self._add_dep(o, r, False)
        for b in writes:
            b.w = o
            b.rs = []
        for b in reads:
            if b.w is not o:
                b.rs.append(o)
        st.append(o)
        return o

    def dma(self, eng, out, in_, reads=(), writes=(), **kw):
        return self.op(eng, lambda e: e.dma_start(out=out, in_=in_, **kw), reads, writes, dma=True)

    def emit(self, sem_ctx):
        nc = self.nc
        esem = {e: sem_ctx("c_" + e) for e in ENGS if e != "sp"}
        dsem = {}
        for e in ENGS:
            if any(o.dma for o in self.streams[e]):
                dsem[e] = [sem_ctx("d_%s_%d" % (e, i)) for i in range(self.n_dma_sems)]
        for e in ENGS:
            for o in self.streams[e]:
                for d in o.deps.values():
                    if not d.dma:
                        d.inc = True
        for e in ENGS:
            cnt = 0
            dcnt = [0] * self.n_dma_sems
            k = 0
            for o in self.streams[e]:
                if o.dma:
                    s = k % self.n_dma_sems
                    k += 1
                    o.presem = (dsem[e][s], dcnt[s])
                    dcnt[s] += 16
                    o.sem = dsem[e][s]
                    o.val = dcnt[s]
                elif o.inc:
                    cnt += 1
                    o.sem = esem[e]
                    o.val = cnt
        nwait = 0
        for e in ENGS:
            known = {}
            for o in self.streams[e]:
                ws = []
                if o.dma and o.presem[1] > 0:
                    s, v = o.presem
                    if known.get(id(s), 0) < v:
                        known[id(s)] = v
                        ws.append((s, v))
                for d in o.deps.values():
                    if known.get(id(d.sem), 0) < d.val:
                        known[id(d.sem)] = d.val
                        ws.append((d.sem, d.val))
                o.waits = ws
                nwait += len(ws)
        self.nwait = nwait

        def run(eng_name, eh):
            final = {}
            for o in self.streams[eng_name]:
                for s, v in o.waits:
                    eh.wait_ge(s, v)
                ins = o.fn(eh)
                if o.dma:
                    ins.then_inc(o.sem, 16)
                    final[id(o.sem)] = (o.sem, o.val)
                elif o.inc:
                    ins.then_inc(o.sem, 1)
            for s, v in final.values():
                eh.wait_ge(s, v)

        with nc.Block() as block:
            if self.streams["pe"]:
                @block.tensor
                def _(eh):
                    run("pe", eh)
            if self.streams["act"]:
                @block.scalar
                def _(eh):
                    run("act", eh)
            if self.streams["dve"]:
                @block.vector
                def _(eh):
                    run("dve", eh)
            if self.streams["pool"]:
                @block.gpsimd
                def _(eh):
                    run("pool", eh)
            if self.streams["sp"]:
                @block.sync
                def _(eh):
                    run("sp", eh)


def build_program(depth=DEPTH, debug=False, upto=None):
    nc = bass.Bass("TRN2", target_bir_lowering=False)
    S = Sched(nc)
    NT = SEQ // 512
    dbg_kind = "ExternalOutput" if debug else "Internal"

    def dram_in(name, shape, dt=F32):
        return nc.dram_tensor(name, list(shape), dt, kind="ExternalInput").ap()

    xT = dram_in("xT", [D, SEQ])
    w_in = dram_in("w_in", [depth, D, IN_W])
    w_a = dram_in("w_a", [depth, 384, D])
    pool_w = dram_in("pool_w", [depth, 4, 256, 256])
    w_b = dram_in("w_b", [depth, D, D])
    w_c = dram_in("w_c", [depth, D, D])
    w_o = dram_in("w_o", [depth, D, D])
    w_up = dram_in("w_up", [depth, D, 2 * DFF])
    w_down = dram_in("w_down", [depth, DFF, D])
    pp_d = dram_in("pp", [128, depth, NPP])
    cst_d = dram_in("cst", [128, NCST])
    bias_d = dram_in("biasT", [128, 18, 256])
    outT = nc.dram_tensor("outT", [D, SEQ], F32, kind="ExternalOutput").ap()

    xr = nc.dram_tensor("xr", [D, SEQ], F32, kind=dbg_kind).ap()
    ao_d = nc.dram_tensor("ao_d", [3, 128, SEQ], BF16, kind=dbg_kind).ap()
    zs_d = nc.dram_tensor("zs_d", [NT, 128, 8, 512], BF16, kind=dbg_kind).ap()
    mac_d = nc.dram_tensor("mac_d", [NT, 128, 8, 512], BF16, kind=dbg_kind).ap()
    yn_d = nc.dram_tensor("yn_d", [NT, 128, 8, 512], BF16, kind=dbg_kind).ap()
    act_d = nc.dram_tensor("act_d", [NT, 128, 22, 512], BF16, kind=dbg_kind).ap()

    xT_v = xT.rearrange("(c p) t -> p c t", p=128)
    xr_v = xr.rearrange("(c p) t -> p c t", p=128)
    outT_v = outT.rearrange("(c p) t -> p c t", p=128)

    Bxr = [Buf() for _ in range(16)]
    Bao = Buf()
    Bzs = [Buf() for _ in range(NT)]
    Bmac = [Buf() for _ in range(NT)]
    Byn = [Buf() for _ in range(NT)]
    Bact = [Buf() for _ in range(NT)]
    Bout = Buf()

    es = ExitStack()
    with es:
        uid = {"n": 0}

        def sb(name, shape, dt, stack=es):
            uid["n"] += 1
            return stack.enter_context(nc.sbuf_tensor("s%d_%s" % (uid["n"], name), list(shape), dt))

        uT = sb("uT", [128, 8, SEQ], BF16)
        BuT = [Buf() for _ in range(16)]
        pp = sb("pp", [128, depth, NPP], F32)
        Bpp = Buf()
        cstf = sb("cstf", [128, 256], F32)
        cstb = sb("cstb", [128, NCST], BF16)
        Bcst = Buf()
        banks = [es.enter_context(nc.psum_tensor("bank%d" % i, [128, 512], F32)) for i in range(8)]
        Bbank = [Buf() for _ in range(8)]

        def uR(t0, t1):
            return BuT[t0 // 256:(t1 + 255) // 256]

        def mm(out, lhsT, rhs, start, stop, R, W):
            S.op("pe", lambda e: e.matmul(out, lhsT=lhsT, rhs=rhs, start=start, stop=stop), R, W)

        def transp(out, in_, R, W):
            S.op("pe", lambda e: e.transpose(out, in_, cstb[:, C_ID:C_ID + 128]), R, W)

        def act(out, in_, func, R, W, bias=None, scale=None):
            kw = {}
            if bias is not None:
                kw["bias"] = bias
            if scale is not None:
                kw["scale"] = scale
            S.op("act", lambda e: e.activation(out=out, in_=in_, func=func, **kw), R, W)

        def tcopy(eng, out, in_, R, W):
            if eng == "act":
                act(out, in_, AF.Copy, R, W)
            else:
                S.op(eng, lambda e: e.tensor_copy(out=out, in_=in_), R, W)

        def tt(eng, out, in0, in1, op, R, W):
            S.op(eng, lambda e: e.tensor_tensor(out=out, in0=in0, in1=in1, op=op), R, W)

        def ts(eng, out, in0, s1, s2, op0, op1, R, W):
            if op1 is None:
                S.op(eng, lambda e: e.tensor_scalar(out=out, in0=in0, scalar1=s1, scalar2=None, op0=op0), R, W)
            else:
                S.op(eng, lambda e: e.tensor_scalar(out=out, in0=in0, scalar1=s1, scalar2=s2, op0=op0, op1=op1), R, W)

        def stt(out, in0, scalar, in1, op0, op1, R, W):
            S.op("dve", lambda e: e.scalar_tensor_tensor(out=out, in0=in0, scalar=scalar, in1=in1, op0=op0, op1=op1), R, W)

        def memset(eng, ap, val, W):
            S.op(eng, lambda e: e.memset(ap, val), (), W)

        def wload(dst, src, B):
            S.dma("pool", dst, src, (), [B])

        rr = {"ev": 0, "bank": 0}

        def ev_eng():
            rr["ev"] += 1
            return "act" if rr["ev"] % 2 else "dve"

        S.dma("sp", pp[:], pp_d, (), [Bpp])
        S.dma("sp", cstf[:], cst_d[:, C_ONE:C_ONE + 256], (), [Bcst])
        S.dma("pool", cstb[:], cst_d, (), [Bcst])
        ident = cstb[:, C_ID:C_ID + 128]
        ones_b = cstb[:, C_ONE:C_ONE + 128]
        ones_f = cstf[:, 0:128]
        tri_f = cstf[:, 128:256]

        def band(kind, gi):
            o = C_BAND + (kind * 4 + gi) * 128
            return cstb[:, o:o + 128]

        def ppc(l, col, n=1):
            return pp[:, l, col:col + n]

        def norm_tile(st, xt, Bx, gcol0, l, dst_fn, Bdst, TT, nb, Bnb, sq, Bsq, rs, Brs):
            for c in range(8):
                act(sq[c % 2][:, 0:TT], xt[:, c, :], AF.Square, [Bx], [Bsq[c % 2]])
                mm(nb[:, 0:TT], ones_b, sq[c % 2][:, 0:TT], c == 0, c == 7, [Bsq[c % 2], Bcst], [Bnb])
            act(rs[:, 0:TT], nb[:, 0:TT], AF.Ln, [Bnb], [Brs], bias=EPS, scale=1.0 / D)
            act(rs[:, 0:TT], rs[:, 0:TT], AF.Exp, [Brs], [Brs], scale=-0.5)
            for c in range(8):
                stt(dst_fn(c), xt[:, c, :], ppc(l, gcol0 + c), rs[:, 0:TT], ALU.mult, ALU.mult,
                    [Bx, Brs, Bpp], Bdst)

        stop = {"flag": False}

        def done(l, name):
            if upto is not None and (l, name) == tuple(upto):
                stop["flag"] = True

        def phase_norm0():
            with ExitStack() as ps:
                xt = [sb("n0_xt%d" % i, [128, 8, 512], F32, ps) for i in range(2)]
                Bxt = [Buf() for _ in range(2)]
                sq = [sb("n0_sq%d" % i, [128, 512], BF16, ps) for i in range(2)]
                Bsq = [Buf(), Buf()]
                rs = sb("n0_rs", [128, 512], F32, ps)
                Brs = Buf()
                for t in range(NT):
                    t0 = t * 512
                    S.dma("sp", xt[t % 2][:], xT_v[:, :, t0:t0 + 512], (), [Bxt[t % 2]])
                    norm_tile(None, xt[t % 2], Bxt[t % 2], PP_LN1, 0,
                              lambda c: uT[:, c, t0:t0 + 512], uR(t0, t0 + 512), 512,
                              banks[0], Bbank[0], sq, Bsq, rs, Brs)
            S.barrier()

        def phase_attn(l):
            with ExitStack() as ps:
                bias_sb = sb("at_bias", [128, 18, 256], F32, ps)
                Bbias = Buf()
                S.dma("sp", bias_sb[:], bias_d, (), [Bbias])
                acc = sb("at_acc", [128, 2, SEQ], F32, ps)
                Bacc = Buf()
                qT = [sb("at_q%d" % i, [128, SEQ], BF16, ps) for i in range(2)]
                kT = [sb("at_k%d" % i, [128, SEQ], BF16, ps) for i in range(2)]
                vS = [sb("at_v%d" % i, [128, 32, 128], BF16, ps) for i in range(2)]
                wq = [sb("at_w%d" % i, [128, 8, 3, 128], BF16, ps) for i in range(2)]
                Bq = [Buf(), Buf()]
                Bk = [Buf(), Buf()]
                Bv = [Buf(), Buf()]
                Bw = [Buf(), Buf()]
                tmp = [sb("at_tmp%d" % i, [128, 256], F32, ps) for i in range(4)]
                pT = [sb("at_p%d" % i, [128, 256], BF16, ps) for i in range(4)]
                Btmp = [Buf() for _ in range(4)]
                BpT = [Buf() for _ in range(4)]
                osb = sb("at_o", [128, SEQ], BF16, ps)
                Bosb = Buf()
                w_l = w_in[l].rearrange("(c p) f -> p c f", p=128)
                it = 0
                combos = [(j, g) for j in range(3) for g in range(3)]

                def load_w(idx):
                    j, g = combos[idx]
                    sl = idx % 2
                    fo = (g * 6 + 2 * j) * 64
                    for wi, base in enumerate((Q0, K0, V0)):
                        wload(wq[sl][:, :, wi, :], w_l[:, :, base + fo:base + fo + 128], Bw[sl])

                load_w(0)
                for idx, (j, g) in enumerate(combos):
                    sl = idx % 2
                    if idx + 1 < len(combos):
                        load_w(idx + 1)
                    win, d = GROUPS[g]
                    L = SEQ // d
                    nbr = L // 128
                    for t in range(NT):
                        t0 = t * 512
                        for wi in range(2):
                            bk = rr["bank"] % 2
                            rr["bank"] += 1
                            for kc in range(8):
                                mm(banks[bk][:, :], wq[sl][:, kc, wi, :], uT[:, kc, t0:t0 + 512],
                                   kc == 0, kc == 7, [Bw[sl]] + uR(t0, t0 + 512), [Bbank[bk]])
                            dst_t = qT[sl] if wi == 0 else kT[sl]
                            Bd = Bq[sl] if wi == 0 else Bk[sl]
                            if d == 1:
                                src = banks[bk][:, :]
                                dst = dst_t[:, t0:t0 + 512]
                            else:
                                src = banks[bk][:, :].rearrange("p (m r) -> p r m", r=d)
                                dst = dst_t[:, :].rearrange("p (r l) -> p r l", r=d)[:, :, t0 // d:(t0 + 512) // d]
                            if wi == 0:
                                act(dst, src, AF.Copy, [Bbank[bk]], [Bd], scale=0.125)
                            else:
                                tcopy("dve", dst, src, [Bbank[bk]], [Bd])
                    uv = None
                    if d > 1:
                        uv = [uT[:, kc, :].rearrange("p (n i r) -> p r n i", r=d, i=128) for kc in range(8)]
                    for b4 in range(8):
                        bk = rr["bank"] % 2
                        rr["bank"] += 1
                        for bb in range(4):
                            b = b4 * 4 + bb
                            r, n = divmod(b, nbr)
                            for kc in range(8):
                                lhs = uT[:, kc, b * 128:(b + 1) * 128] if d == 1 else uv[kc][:, r, n, :]
                                mm(banks[bk][:, bb * 128:(bb + 1) * 128], lhs, wq[sl][:, kc, 2, :],
                                   kc == 0, kc == 7, [Bw[sl]] + BuT, [Bbank[bk]])
                        tcopy(ev_eng(), vS[sl][:, b4 * 4:(b4 + 1) * 4, :],
                              banks[bk][:, :].rearrange("p (b f) -> p b f", b=4), [Bbank[bk]], [Bv[sl]])
                    if d > 1:
                        accv = acc[:, :, :].rearrange("p a (n i r) -> p a r n i", r=d, i=128)
                    def emit_S(b):
                        r, n = divmod(b, nbr)
                        hp = n > 0
                        for hh in range(2):
                            sbk = 2 + (b % 2) * 2 + hh
                            ps_ = slice(hh * 64, (hh + 1) * 64)
                            mm(banks[sbk][:, 0:128], kT[sl][ps_, b * 128:(b + 1) * 128],
                               qT[sl][ps_, b * 128:(b + 1) * 128], True, True, [Bq[sl], Bk[sl]], [Bbank[sbk]])
                            if hp:
                                mm(banks[sbk][:, 128:256], kT[sl][ps_, (b - 1) * 128:b * 128],
                                   qT[sl][ps_, b * 128:(b + 1) * 128], True, True, [Bq[sl], Bk[sl]], [Bbank[sbk]])

                    def emit_add(b):
                        r, n = divmod(b, nbr)
                        ncol = 256 if n > 0 else 128
                        for hh in range(2):
                            sbk = 2 + (b % 2) * 2 + hh
                            bi = (b % 2) * 2 + hh
                            hg = g * 6 + 2 * j + hh
                            tt("dve", tmp[bi][:, 0:ncol], banks[sbk][:, 0:ncol], bias_sb[:, hg, 0:ncol], ALU.add,
                               [Bbank[sbk], Bbias], [Btmp[bi]])

                    def emit_exp(b):
                        r, n = divmod(b, nbr)
                        ncol = 256 if n > 0 else 128
                        for hh in range(2):
                            bi = (b % 2) * 2 + hh
                            act(pT[bi][:, 0:ncol], tmp[bi][:, 0:ncol], AF.Exp, [Btmp[bi]], [BpT[bi]])

                    def emit_PV(b):
                        r, n = divmod(b, nbr)
                        hp = n > 0
                        nzb = 6 + (b % 2)
                        for hh in range(2):
                            bi = (b % 2) * 2 + hh
                            ps_ = slice(hh * 64, (hh + 1) * 64)
                            mm(banks[nzb][ps_, 0:128], vS[sl][:, b, ps_], pT[bi][:, 0:128], True, not hp,
                               [Bv[sl], BpT[bi]], [Bbank[nzb]])
                            if hp:
                                mm(banks[nzb][ps_, 0:128], vS[sl][:, b - 1, ps_], pT[bi][:, 128:256], False, True,
                                   [Bv[sl], BpT[bi]], [Bbank[nzb]])
                            mm(banks[nzb][ps_, 128:256], ones_b[:, 0:64], pT[bi][:, 0:128], True, not hp,
                               [Bcst, BpT[bi]], [Bbank[nzb]])
                            if hp:
                                mm(banks[nzb][ps_, 128:256], ones_b[:, 0:64], pT[bi][:, 128:256], False, True,
                                   [Bcst, BpT[bi]], [Bbank[nzb]])

                    def emit_evac(b):
                        r, n = divmod(b, nbr)
                        nzb = 6 + (b % 2)
                        src = banks[nzb][:, 0:256].rearrange("p (a q) -> p a q", a=2)
                        if d == 1:
                            dst = acc[:, :, b * 128:(b + 1) * 128]
                        else:
                            dst = accv[:, :, r, n, :]
                        if g == 0:
                            tcopy("act", dst, src, [Bbank[nzb]], [Bacc])
                        else:
                            tt("dve", dst, src, dst, ALU.add, [Bbank[nzb], Bacc], [Bacc])

                    for k in range(32 + 4):
                        if k < 32:
                            emit_S(k)
                        if 0 <= k - 1 < 32:
                            emit_add(k - 1)
                        if 0 <= k - 2 < 32:
                            emit_exp(k - 2)
                        if 0 <= k - 3 < 32:
                            emit_PV(k - 3)
                        if 0 <= k - 4 < 32:
                            emit_evac(k - 4)
                    if g == 2:
                        for hf in range(4):
                            cs = slice(hf * 1024, (hf + 1) * 1024)
                            S.op("dve", lambda e, cs=cs: e.reciprocal(out=acc[:, 1, cs], in_=acc[:, 1, cs]), [Bacc], [Bacc])
                            tt("dve", osb[:, cs], acc[:, 0, cs], acc[:, 1, cs], ALU.mult, [Bacc], [Bosb])
                        S.dma("sp", ao_d[j], osb[:], [Bosb], [Bao])
            S.barrier()

        def phase_poolz(l):
            with ExitStack() as ps:
                Wp = sb("pz_wp", [128, 8, 1024], BF16, ps)
                Wz = sb("pz_wz", [128, 8, 1024], BF16, ps)
                pw = sb("pz_pw", [128, 4, 2, 256], BF16, ps)
                wb = sb("pz_wb", [128, 8, 1024], BF16, ps)
                Wg = sb("pz_wg", [128, 8, 1024], BF16, ps)
                BWp, BWz, Bpw, Bwb, BWg = [Buf() for _ in range(5)]
                w_l = w_in[l].rearrange("(c p) f -> p c f", p=128)
                wload(Wp[:], w_l[:, :, P0:P0 + 1024], BWp)
                for g in range(4):
                    wload(pw[:, g, :, :], pool_w[l, g].rearrange("(cc p) d -> p cc d", p=128), Bpw)
                wload(wb[:], w_b[l].rearrange("(c p) f -> p c f", p=128), Bwb)
                wload(Wg[:], w_l[:, :, G0 + 1024:G0 + 2048], BWg)
                wload(Wz[:], w_l[:, :, Z0:Z0 + 1024], BWz)
                pin = [sb("pz_pin%d" % i, [128, 1024], BF16, ps) for i in range(2)]
                Bpin = [Buf(), Buf()]
                dT = sb("pz_dT", [128, 8, 512], BF16, ps)
                BdT = Buf()
                yp = sb("pz_yp", [128, 8, 512], BF16, ps)
                Byp = Buf()
                g1 = [sb("pz_g%d" % i, [128, 512], F32, ps) for i in range(2)]
                Bg1 = [Buf(), Buf()]
                mac = [sb("pz_mac%d" % i, [128, 8, 512], BF16, ps) for i in range(2)]
                Bm = [Buf(), Buf()]
                zst = [sb("pz_zs%d" % i, [128, 8, 512], BF16, ps) for i in range(2)]
                Bz = [Buf(), Buf()]

                def nb4():
                    bk = rr["bank"] % 4
                    rr["bank"] += 1
                    return bk

                for t in range(NT):
                    t0 = t * 512
                    uRt = uR(t0, t0 + 512)
                    for blk in range(4):
                        b = t * 4 + blk
                        tok = b * 128
                        for half in range(2):
                            bk = nb4()
                            for kc in range(8):
                                mm(banks[bk][:, :], uT[:, kc, tok:tok + 128], Wp[:, kc, half * 512:(half + 1) * 512],
                                   kc == 0, kc == 7, [BWp] + uRt, [Bbank[bk]])
                            tcopy(ev_eng(), pin[b % 2][:, half * 512:(half + 1) * 512], banks[bk][:, :],
                                  [Bbank[bk]], [Bpin[b % 2]])
                        for half in range(2):
                            dbk = 4 + half
                            for c4 in range(4):
                                c = half * 4 + c4
                                gi = c // 2
                                mm(banks[dbk][:, c4 * 128:(c4 + 1) * 128], pin[b % 2][:, c * 128:(c + 1) * 128],
                                   band(2 if b == 0 else 0, gi), True, b == 0, [Bpin[b % 2], Bcst], [Bbank[dbk]])
                                if b > 0:
                                    mm(banks[dbk][:, c4 * 128:(c4 + 1) * 128], pin[(b - 1) % 2][:, c * 128:(c + 1) * 128],
                                       band(1, gi), False, True, [Bpin[(b - 1) % 2], Bcst], [Bbank[dbk]])
                            tcopy(ev_eng(), dT[:, half * 4:(half + 1) * 4, blk * 128:(blk + 1) * 128],
                                  banks[dbk][:, :].rearrange("p (c q) -> p c q", c=4), [Bbank[dbk]], [BdT])
                    for oc in range(8):
                        g, dc = divmod(oc, 2)
                        bk = nb4()
                        for cc in range(2):
                            mm(banks[bk][:, :], pw[:, g, cc, dc * 128:(dc + 1) * 128], dT[:, g * 2 + cc, :],
                               cc == 0, cc == 1, [Bpw, BdT], [Bbank[bk]])
                        act(yp[:, oc, :], banks[bk][:, :], AF.Identity, [Bbank[bk], Bpp], [Byp], scale=ppc(l, PP_PSC + oc))
                    for oc in range(8):
                        bka = nb4()
                        for kc in range(8):
                            mm(banks[bka][:, :], wb[:, kc, oc * 128:(oc + 1) * 128], yp[:, kc, :], kc == 0, kc == 7,
                               [Bwb, Byp], [Bbank[bka]])
                        bkg = nb4()
                        for kc in range(8):
                            mm(banks[bkg][:, :], Wg[:, kc, oc * 128:(oc + 1) * 128], uT[:, kc, t0:t0 + 512], kc == 0, kc == 7,
                               [BWg] + uRt, [Bbank[bkg]])
                        act(g1[oc % 2][:, :], banks[bkg][:, :], AF.Sigmoid, [Bbank[bkg], Bpp], [Bg1[oc % 2]],
                            bias=ppc(l, PP_BG + 8 + oc))
                        tt("dve", mac[t % 2][:, oc, :], banks[bka][:, :], g1[oc % 2][:, :], ALU.mult,
                           [Bbank[bka], Bg1[oc % 2]], [Bm[t % 2]])
                    S.dma("sp", mac_d[t], mac[t % 2][:], [Bm[t % 2]], [Bmac[t]])
                    for oc in range(8):
                        bk = nb4()
                        for kc in range(8):
                            mm(banks[bk][:, :], Wz[:, kc, oc * 128:(oc + 1) * 128], uT[:, kc, t0:t0 + 512], kc == 0, kc == 7,
                               [BWz] + uRt, [Bbank[bk]])
                        act(zst[t % 2][:, oc, :], banks[bk][:, :], AF.Silu, [Bbank[bk]], [Bz[t % 2]])
                    S.dma("sp", zs_d[t], zst[t % 2][:], [Bz[t % 2]], [Bzs[t]])
            S.barrier()

        def phase_ssd(l):
            with ExitStack() as ps:
                Wx = sb("sd_wx", [128, 8, 1536], BF16, ps)
                Wd = sb("sd_wd", [128, 8, 16], BF16, ps)
                BWx, BWd = Buf(), Buf()
                w_l = w_in[l].rearrange("(c p) f -> p c f", p=128)
                wload(Wx[:], w_l[:, :, X0:X0 + 1536], BWx)
                wload(Wd[:], w_l[:, :, DT0:DT0 + 16], BWd)
                zt = sb("sd_z", [128, 8, 512], BF16, ps)
                Bzt = Buf()
                xraw = sb("sd_xr", [128, 12, 515], BF16, ps)
                hal = sb("sd_hal", [128, 12, 3], BF16, ps)
                Bxraw = Buf()
                Bhalo = Buf()
                Bhal = Buf()
                ctmp = [sb("sd_ct%d" % i, [128, 512], F32, ps) for i in range(2)]
                Bct = [Buf(), Buf()]
                xcT2 = [sb("sd_xc%d" % i, [128, 12, 512], BF16, ps) for i in range(2)]
                BxcT2 = [Buf(), Buf()]
                yT = sb("sd_y", [128, 8, 512], BF16, ps)
                ByT = Buf()
                sq = [sb("sd_sq%d" % i, [128, 512], BF16, ps) for i in range(2)]
                Bsq = [Buf(), Buf()]
                rstd = sb("sd_rs", [128, 2, 512], F32, ps)
                Brs = Buf()
                A_bc = sb("sd_A", [128, 16], F32, ps)
                BA = Buf()

                def two(name, shape, dt):
                    return [sb("%s%d" % (name, i), shape, dt, ps) for i in range(2)]

                dtp = two("sd_dtp", [128, 4, 16], F32)
                dt_ = two("sd_dt", [128, 4, 16], F32)
                da = two("sd_da", [128, 4, 16], F32)
                acs = two("sd_acs", [128, 4, 16], F32)
                nacs = two("sd_nacs", [128, 4, 16], F32)
                dec = two("sd_dec", [128, 4, 16], F32)
                w2 = two("sd_w2", [128, 4, 16], F32)
                cd = two("sd_cd", [128, 4, 16], F32)
                Bsm = [Buf(), Buf()]
                da_bc = two("sd_dabc", [128, 4, 128], F32)
                Bdabc = [Buf(), Buf()]
                xs_tok = sb("sd_xst", [128, 1024], BF16, ps)
                Bxst = Buf()
                xc_tok = two("sd_xct", [128, 1024], BF16)
                xdec = two("sd_xdec", [128, 1024], BF16)
                Bxct, Bxdec = [Buf(), Buf()], [Buf(), Buf()]
                Btok = two("sd_btok", [128, 256], BF16)
                BBtok = [Buf(), Buf()]
                cbm = [two("sd_cbm%d_" % i, [128, 128], F32) for i in range(2)]
                Bcbm = [[Buf(), Buf()], [Buf(), Buf()]]
                E1 = [sb("sd_e1%d" % i, [128, 512], F32, ps) for i in range(3)]
                BE1 = [Buf() for _ in range(3)]
                OD = two("sd_od", [128, 512], F32)
                Gm = two("sd_g", [128, 512], BF16)
                Cod = two("sd_cod", [128, 512], BF16)
                BOD, BG, BCod = [[Buf(), Buf()] for _ in range(3)]
                prev_f = sb("sd_prevf", [128, 1024], F32, ps)
                prev_b = sb("sd_prevb", [128, 1024], BF16, ps)
                Bpf, Bpb = Buf(), Buf()

                act(A_bc[:], ppc(l, PP_ALOG, 16), AF.Exp, [Bpp], [BA])
                ts("dve", A_bc[:], A_bc[:], -1.0, None, ALU.mult, None, [BA], [BA])
                memset("dve", prev_f[:], 0.0, [Bpf])
                memset("dve", prev_b[:], 0.0, [Bpb])
                memset("dve", hal[:], 0.0, [Bhal])

                tb16 = banks[7][:, :].bitcast(BF16)
                misc = banks[2]
                mb16 = banks[2][:, :].bitcast(BF16)

                def conv(t):
                    t0 = t * 512
                    uRt = uR(t0, t0 + 512)
                    xcT = xcT2[t % 2]
                    BxcT = BxcT2[t % 2]
                    tcopy("pool", xraw[:, :, 0:3], hal[:], [Bhal], [Bhalo])
                    for c in range(12):
                        bk = rr["bank"] % 2
                        rr["bank"] += 1
                        for kc in range(8):
                            mm(banks[bk][:, :], Wx[:, kc, c * 128:(c + 1) * 128], uT[:, kc, t0:t0 + 512], kc == 0, kc == 7,
                               [BWx] + uRt, [Bbank[bk]])
                        tcopy("act", xraw[:, c, 3:515], banks[bk][:, :], [Bbank[bk]], [Bxraw])
                        tcopy("act", hal[:, c, :], banks[bk][:, 509:512], [Bbank[bk]], [Bhal])
                        ci = c % 2
                        wcol = PP_SCW + c * 4
                        ts("dve", ctmp[ci][:, :], xraw[:, c, 3:515], ppc(l, wcol + 3), ppc(l, PP_SCB + c),
                           ALU.mult, ALU.add, [Bxraw, Bpp], [Bct[ci]])
                        for k in (2, 1, 0):
                            stt(ctmp[ci][:, :], xraw[:, c, k:k + 512], ppc(l, wcol + k), ctmp[ci][:, :],
                                ALU.mult, ALU.add, [Bxraw, Bhalo, Bct[ci], Bpp], [Bct[ci]])
                        act(xcT[:, c, :], ctmp[ci][:, :], AF.Silu, [Bct[ci]], [BxcT])

                conv(0)
                for t in range(NT):
                    t0 = t * 512
                    uRt = uR(t0, t0 + 512)
                    xcT = xcT2[t % 2]
                    BxcT = BxcT2[t % 2]
                    S.dma("sp", zt[:], zs_d[t], [Bzs[t]], [Bzt])

                    tp = t % 2
                    A0 = banks[0]

                    def stageA():
                        for ci in range(4):
                            tk0 = t0 + ci * 128
                            for kc in range(8):
                                mm(A0[:, ci * 16:(ci + 1) * 16], uT[:, kc, tk0:tk0 + 128], Wd[:, kc, :], kc == 0, kc == 7,
                                   [BWd] + uRt, [Bbank[0]])
                        v3 = lambda ap: ap.rearrange("p (c h) -> p c h", c=4)
                        tt("dve", dtp[tp][:], v3(A0[:, 0:64]), ppc(l, PP_DTB, 16).unsqueeze(1).to_broadcast([128, 4, 16]),
                           ALU.add, [Bbank[0], Bpp], [Bsm[tp]])
                        act(dtp[tp][:], dtp[tp][:], AF.Exp, [Bsm[tp]], [Bsm[tp]])
                        act(dt_[tp][:], dtp[tp][:], AF.Ln, [Bsm[tp]], [Bsm[tp]], bias=1.0)
                        tt("dve", da[tp][:], dt_[tp][:], A_bc[:].unsqueeze(1).to_broadcast([128, 4, 16]), ALU.mult,
                           [Bsm[tp], BA], [Bsm[tp]])
                        for ci in range(4):
                            mm(A0[:, 64 + ci * 16:64 + (ci + 1) * 16], tri_f, da[tp][:, ci, :], True, True, [Bcst, Bsm[tp]], [Bbank[0]])
                        for ci in range(4):
                            mm(A0[:, 128 + ci * 16:128 + (ci + 1) * 16], ones_f, da[tp][:, ci, :], True, True, [Bcst, Bsm[tp]], [Bbank[0]])
                        tcopy("dve", acs[tp][:], v3(A0[:, 64:128]), [Bbank[0]], [Bsm[tp]])
                        ts("dve", nacs[tp][:], v3(A0[:, 64:128]), -1.0, None, ALU.mult, None, [Bbank[0]], [Bsm[tp]])
                        tt("dve", dec[tp][:], v3(A0[:, 128:192]), acs[tp][:], ALU.subtract, [Bbank[0], Bsm[tp]], [Bsm[tp]])
                        act(dec[tp][:], dec[tp][:], AF.Exp, [Bsm[tp]], [Bsm[tp]])
                        act(cd[tp][:], v3(A0[:, 128:192]), AF.Exp, [Bbank[0]], [Bsm[tp]])
                        tt("dve", w2[tp][:], dt_[tp][:], dec[tp][:], ALU.mult, [Bsm[tp]], [Bsm[tp]])

                    def stageB(ci):
                        sl = ci % 2
                        cs = slice(ci * 128, ci * 128 + 128)
                        for c in range(8):
                            transp(tb16[:, c * 128:(c + 1) * 128], xcT[:, c, cs], [BxcT, Bcst], [Bbank[7]])
                        for g2 in range(2):
                            transp(mb16[:, 768 + g2 * 128:768 + (g2 + 1) * 128], xcT[:, 8 + g2, cs], [BxcT, Bcst], [Bbank[2]])
                        tcopy("act", Btok[sl][:], mb16[:, 768:1024], [Bbank[2]], [BBtok[sl]])
                        tcopy("act", xs_tok[:], tb16[:, :], [Bbank[7]], [Bxst])
                        tt("dve", xc_tok[sl][:].rearrange("p (h q) -> p h q", h=16), xs_tok[:].rearrange("p (h q) -> p h q", h=16),
                           dt_[tp][:, ci, :].unsqueeze(2).to_broadcast([128, 16, 64]), ALU.mult, [Bxst, Bsm[tp]], [Bxct[sl]])
                        tt("pool", xdec[sl][:].rearrange("p (h q) -> p h q", h=16), xs_tok[:].rearrange("p (h q) -> p h q", h=16),
                           w2[tp][:, ci, :].unsqueeze(2).to_broadcast([128, 16, 64]), ALU.mult, [Bxst, Bsm[tp]], [Bxdec[sl]])
                        for g2 in range(2):
                            mm(misc[:, g2 * 128:(g2 + 1) * 128], xcT[:, 8 + g2, cs], xcT[:, 10 + g2, cs], True, True,
                               [BxcT], [Bbank[2]])
                            tt("dve", cbm[sl][g2][:, :], misc[:, g2 * 128:(g2 + 1) * 128], tri_f, ALU.mult,
                               [Bbank[2], Bcst], [Bcbm[sl][g2]])

                    def states(ci, g2):
                        sl = ci % 2
                        mm(banks[1][:, :], Btok[sl][:, g2 * 128:(g2 + 1) * 128], xdec[sl][:, g2 * 512:(g2 + 1) * 512],
                           True, True, [BBtok[sl], Bxdec[sl]], [Bbank[1]])

                    def stage3a(ci):
                        for g2 in range(2):
                            states(ci, g2)
                            pg = prev_f[:, g2 * 512:(g2 + 1) * 512]
                            tt("dve", pg.rearrange("p (h q) -> p h q", h=8), pg.rearrange("p (h q) -> p h q", h=8),
                               cd[tp][:, ci, g2 * 8:(g2 + 1) * 8].unsqueeze(2).to_broadcast([128, 8, 64]), ALU.mult,
                               [Bpf, Bsm[tp]], [Bpf])
                            tt("dve", pg, banks[1][:, :], pg, ALU.add, [Bbank[1], Bpf], [Bpf])

                    def stage3b(ci):
                        tcopy("pool", prev_b[:], prev_f[:], [Bpf], [Bpb])

                    def s0(k):
                        ci, q4 = divmod(k, 4)
                        rb = k % 2
                        rbk = 3 + k % 3
                        tcopy("dve", da_bc[rb][:], da[tp][:, ci, q4 * 4:(q4 + 1) * 4].unsqueeze(2).to_broadcast([128, 4, 128]),
                              [Bsm[tp]], [Bdabc[rb]])
                        for h4 in range(4):
                            mm(banks[rbk][:, h4 * 128:(h4 + 1) * 128], da_bc[rb][:, h4, :], tri_f, True, True,
                               [Bdabc[rb], Bcst], [Bbank[rbk]])

                    def s1(k):
                        ci, q4 = divmod(k, 4)
                        rbk = 3 + k % 3
                        e = k % 3
                        for h4 in range(4):
                            h = q4 * 4 + h4
                            act(E1[e][:, h4 * 128:(h4 + 1) * 128], banks[rbk][:, h4 * 128:(h4 + 1) * 128], AF.Exp,
                                [Bbank[rbk], Bsm[tp]], [BE1[e]], bias=nacs[tp][:, ci, h:h + 1])

                    def s2(k):
                        rbk = 3 + k % 3
                        act(OD[k % 2][:, :], banks[rbk][:, :], AF.Exp, [Bbank[rbk]], [BOD[k % 2]])

                    def s3(k):
                        ci, q4 = divmod(k, 4)
                        sl = ci % 2
                        e = k % 3
                        rb = k % 2
                        g2 = q4 // 2
                        cs = slice(ci * 128, ci * 128 + 128)
                        stt(Gm[rb][:].rearrange("p (h q) -> p h q", h=4), E1[e][:].rearrange("p (h q) -> p h q", h=4), 1.0,
                            cbm[sl][g2][:, :].unsqueeze(1).to_broadcast([128, 4, 128]), ALU.min, ALU.mult,
                            [BE1[e], Bcbm[sl][g2]], [BG[rb]])
                        tt("dve", Cod[rb][:].rearrange("p (h q) -> p h q", h=4), OD[rb][:].rearrange("p (h q) -> p h q", h=4),
                           xcT[:, 10 + g2, cs].unsqueeze(1).to_broadcast([128, 4, 128]), ALU.mult,
                           [BOD[rb], BxcT], [BCod[rb]])

                    def s4(k):
                        ci, q4 = divmod(k, 4)
                        sl = ci % 2
                        rb = k % 2
                        for h4 in range(4):
                            h = q4 * 4 + h4
                            hs = slice(h * 64, (h + 1) * 64)
                            yc0 = rb * 256 + (h4 // 2) * 128
                            yo = banks[6][(h % 2) * 64:(h % 2 + 1) * 64, yc0:yc0 + 128]
                            mm(yo, xc_tok[sl][:, hs], Gm[rb][:, h4 * 128:(h4 + 1) * 128], True, False, [Bxct[sl], BG[rb]], [Bbank[6]])
                            mm(yo, prev_b[:, hs], Cod[rb][:, h4 * 128:(h4 + 1) * 128], False, True, [Bpb, BCod[rb]], [Bbank[6]])

                    def s5(k):
                        ci, q4 = divmod(k, 4)
                        rb = k % 2
                        cs = slice(ci * 128, ci * 128 + 128)
                        for pi in range(2):
                            pair = q4 * 2 + pi
                            yc0 = rb * 256 + pi * 128
                            stt(yT[:, pair, cs], xcT[:, pair, cs], ppc(l, PP_DSK + pair), banks[6][:, yc0:yc0 + 128],
                                ALU.mult, ALU.add, [BxcT, Bbank[6], Bpp], [ByT])

                    stageA()
                    stageB(0)
                    for k in range(16 + 5):
                        if k < 16:
                            s0(k)
                        if 0 <= k - 1 < 16:
                            s1(k - 1)
                        if 0 <= k - 2 < 16:
                            s2(k - 2)
                        if 0 <= k - 3 < 16:
                            s3(k - 3)
                        if 0 <= k - 4 < 16:
                            s4(k - 4)
                            if (k - 4) % 4 == 3:
                                stage3b((k - 4) // 4)
                        if 0 <= k - 5 < 16:
                            s5(k - 5)
                        if k % 4 == 3 and k // 4 < 4:
                            c = k // 4
                            stage3a(c)
                            if c + 1 < 4:
                                stageB(c + 1)
                        if k == 13 and t + 1 < NT:
                            conv(t + 1)
                    tt("pool", yT[:, :, :], yT[:, :, :], zt[:, :, :], ALU.mult, [ByT, Bzt], [ByT])
                    for g2 in range(2):
                        nbk = g2
                        for i in range(4):
                            c = g2 * 4 + i
                            act(sq[c % 2][:, :], yT[:, c, :], AF.Square, [ByT], [Bsq[c % 2]])
                            mm(banks[nbk][:, :], ones_b, sq[c % 2][:, :], i == 0, i == 3, [Bsq[c % 2], Bcst], [Bbank[nbk]])
                        act(rstd[:, g2, :], banks[nbk][:, :], AF.Ln, [Bbank[nbk]], [Brs], bias=EPS, scale=1.0 / 512)
                        act(rstd[:, g2, :], rstd[:, g2, :], AF.Exp, [Brs], [Brs], scale=-0.5)
                    for c in range(8):
                        stt(yT[:, c, :], yT[:, c, :], ppc(l, PP_NW + c), rstd[:, c // 4, :], ALU.mult, ALU.mult,
                            [ByT, Brs, Bpp], [ByT])
                    S.dma("sp", yn_d[t], yT[:], [ByT], [Byn[t]])
            S.barrier()

        def phase_merge(l):
            TT = 256
            with ExitStack() as ps:
                wa = sb("mg_wa", [128, 3, 1024], BF16, ps)
                Wg0 = sb("mg_wg0", [128, 8, 1024], BF16, ps)
                Wg2 = sb("mg_wg2", [128, 8, 1024], BF16, ps)
                wc = sb("mg_wc", [128, 8, 1024], BF16, ps)
                wo = sb("mg_wo", [128, 8, 1024], BF16, ps)
                Bwa, BWg0, BWg2, Bwc, Bwo = [Buf() for _ in range(5)]
                w_l = w_in[l].rearrange("(c p) f -> p c f", p=128)
                wload(wa[:], w_a[l].rearrange("(c p) f -> p c f", p=128), Bwa)
                wload(Wg0[:], w_l[:, :, G0:G0 + 1024], BWg0)
                wload(wc[:], w_c[l].rearrange("(c p) f -> p c f", p=128), Bwc)
                wload(Wg2[:], w_l[:, :, G0 + 2048:G0 + 3072], BWg2)
                wload(wo[:], w_o[l].rearrange("(c p) f -> p c f", p=128), Bwo)
                ao = [sb("mg_ao%d" % i, [128, 3, TT], BF16, ps) for i in range(2)]
                yn = [sb("mg_yn%d" % i, [128, 8, TT], BF16, ps) for i in range(2)]
                mc = [sb("mg_mc%d" % i, [128, 8, TT], BF16, ps) for i in range(2)]
                xt = [sb("mg_xt%d" % i, [128, 8, TT], F32, ps) for i in range(2)]
                Bao_t, Byn_t, Bmc_t, Bxt = [[Buf(), Buf()] for _ in range(4)]
                mg = sb("mg_mg", [128, 8, TT], BF16, ps)
                Bmg = Buf()
                gs = [sb("mg_gs%d" % i, [128, TT], F32, ps) for i in range(4)]
                Bgs = [Buf() for _ in range(4)]
                t1 = [sb("mg_t1%d" % i, [128, TT], F32, ps) for i in range(2)]
                t2 = [sb("mg_t2%d" % i, [128, TT], F32, ps) for i in range(2)]
                Bt1, Bt2 = [Buf(), Buf()], [Buf(), Buf()]
                sq = [sb("mg_sq%d" % i, [128, 512], BF16, ps) for i in range(2)]
                Bsq = [Buf(), Buf()]
                rs = sb("mg_rs", [128, 512], F32, ps)
                Brs = Buf()
                xsrc = xT_v if l == 0 else xr_v
                ao_v = ao_d.rearrange("j p t -> p j t")

                def nb6():
                    bk = rr["bank"] % 6
                    rr["bank"] += 1
                    return bk

                def loads(tt_):
                    p = tt_ % 2
                    a0 = tt_ * TT
                    t5, off = divmod(a0, 512)
                    S.dma("sp", ao[p][:], ao_v[:, :, a0:a0 + TT], [Bao], [Bao_t[p]])
                    S.dma("sp", yn[p][:], yn_d[t5][:, :, off:off + TT], [Byn[t5]], [Byn_t[p]])
                    S.dma("sp", mc[p][:], mac_d[t5][:, :, off:off + TT], [Bmac[t5]], [Bmc_t[p]])
                    S.dma("sp", xt[p][:], xsrc[:, :, a0:a0 + TT], [Bxr[tt_]], [Bxt[p]])

                ntt = SEQ // TT
                loads(0)
                for tt_ in range(ntt):
                    p = tt_ % 2
                    a0 = tt_ * TT
                    if tt_ + 1 < ntt:
                        loads(tt_ + 1)
                    uRt = uR(a0, a0 + TT)
                    for oc in range(8):
                        osl = slice(oc * 128, (oc + 1) * 128)
                        i2 = oc % 2
                        bka = nb6()
                        for kc in range(3):
                            mm(banks[bka][:, 0:TT], wa[:, kc, osl], ao[p][:, kc, :], kc == 0, kc == 2, [Bwa, Bao_t[p]], [Bbank[bka]])
                        bkg = nb6()
                        for kc in range(8):
                            mm(banks[bkg][:, 0:TT], Wg0[:, kc, osl], uT[:, kc, a0:a0 + TT], kc == 0, kc == 7, [BWg0] + uRt, [Bbank[bkg]])
                        act(gs[i2][:, :], banks[bkg][:, 0:TT], AF.Sigmoid, [Bbank[bkg], Bpp], [Bgs[i2]], bias=ppc(l, PP_BG + oc))
                        tt("dve", t1[i2][:, :], banks[bka][:, 0:TT], gs[i2][:, :], ALU.mult, [Bbank[bka], Bgs[i2]], [Bt1[i2]])
                        bkc = nb6()
                        for kc in range(8):
                            mm(banks[bkc][:, 0:TT], wc[:, kc, osl], yn[p][:, kc, :], kc == 0, kc == 7, [Bwc, Byn_t[p]], [Bbank[bkc]])
                        bkg2 = nb6()
                        for kc in range(8):
                            mm(banks[bkg2][:, 0:TT], Wg2[:, kc, osl], uT[:, kc, a0:a0 + TT], kc == 0, kc == 7, [BWg2] + uRt, [Bbank[bkg2]])
                        act(gs[2 + i2][:, :], banks[bkg2][:, 0:TT], AF.Sigmoid, [Bbank[bkg2], Bpp], [Bgs[2 + i2]],
                            bias=ppc(l, PP_BG + 16 + oc))
                        tt("dve", t2[i2][:, :], banks[bkc][:, 0:TT], gs[2 + i2][:, :], ALU.mult, [Bbank[bkc], Bgs[2 + i2]], [Bt2[i2]])
                        tt("pool", t1[i2][:, :], t1[i2][:, :], mc[p][:, oc, :], ALU.add, [Bt1[i2], Bmc_t[p]], [Bt1[i2]])
                        tt("pool", mg[:, oc, :], t1[i2][:, :], t2[i2][:, :], ALU.add, [Bt1[i2], Bt2[i2]], [Bmg])
                    for oc in range(8):
                        osl = slice(oc * 128, (oc + 1) * 128)
                        bk = nb6()
                        for kc in range(8):
                            mm(banks[bk][:, 0:TT], wo[:, kc, osl], mg[:, kc, :], kc == 0, kc == 7, [Bwo, Bmg], [Bbank[bk]])
                        tt("dve", xt[p][:, oc, :], banks[bk][:, 0:TT], xt[p][:, oc, :], ALU.add, [Bbank[bk], Bxt[p]], [Bxt[p]])
                    S.dma("sp", xr_v[:, :, a0:a0 + TT], xt[p][:], [Bxt[p]], [Bxr[tt_]])
                    norm_tile(None, xt[p], Bxt[p], PP_LN2, l, lambda c: uT[:, c, a0:a0 + TT], uRt, TT,
                              banks[6], Bbank[6], sq, Bsq, rs, Brs)
            S.barrier()

        def phase_ffn_up(l):
            with ExitStack() as ps:
                NG = 6
                wu = [sb("fu_w%d" % i, [128, 8, 2, 512], BF16, ps) for i in range(2)]
                Bwu = [Buf(), Buf()]
                raw = [[sb("fu_raw%d%d" % (s_, i), [128, 514], F32, ps) for i in range(2)] for s_ in range(2)]
                Braw = [[Buf(), Buf()], [Buf(), Buf()]]
                Bhal = [[Buf(), Buf()], [Buf(), Buf()]]
                ct = [[sb("fu_ct%d%d" % (s_, i), [128, 512], F32, ps) for i in range(2)] for s_ in range(2)]
                Bct = [[Buf(), Buf()], [Buf(), Buf()]]
                sa = [sb("fu_sa%d" % i, [128, 512], F32, ps) for i in range(2)]
                Bsa = [Buf(), Buf()]
                ab = [sb("fu_ab%d" % i, [128, 512], BF16, ps) for i in range(4)]
                Bab = [Buf() for _ in range(4)]
                wu_l = w_up[l].rearrange("(c p) f -> p c f", p=128)

                def load_g(gi):
                    sl = gi % 2
                    n = min(4, 22 - gi * 4) * 128
                    wload(wu[sl][:, :, 0, 0:n], wu_l[:, :, gi * 512:gi * 512 + n], Bwu[sl])
                    wload(wu[sl][:, :, 1, 0:n], wu_l[:, :, DFF + gi * 512:DFF + gi * 512 + n], Bwu[sl])

                pend = {"v": None, "k": 0}

                def finish(t, j, par):
                    act(sa[par][:, :], ct[0][par][:, :], AF.Silu, [Bct[0][par]], [Bsa[par]])
                    ai = pend["k"] % 4
                    pend["k"] += 1
                    tt("pool", ab[ai][:, :], sa[par][:, :], ct[1][par][:, :], ALU.mult, [Bsa[par], Bct[1][par]], [Bab[ai]])
                    S.dma("sp", act_d[t, :, j, :], ab[ai][:, :], [Bab[ai]], [Bact[t]])

                load_g(0)
                for gi in range(NG):
                    sl = gi % 2
                    if gi + 1 < NG:
                        load_g(gi + 1)
                    for jj in range(min(4, 22 - gi * 4)):
                        j = gi * 4 + jj
                        for s_ in range(2):
                            memset("dve", raw[s_][0][:, 0:2], 0.0, [Bhal[s_][0]])
                        for t in range(NT):
                            t0 = t * 512
                            par = t % 2
                            uRt = uR(t0, t0 + 512)
                            for s_ in range(2):
                                bk = rr["bank"] % 4
                                rr["bank"] += 1
                                ch = j if s_ == 0 else 22 + j
                                for kc in range(8):
                                    mm(banks[bk][:, :], wu[sl][:, kc, s_, jj * 128:(jj + 1) * 128], uT[:, kc, t0:t0 + 512],
                                       kc == 0, kc == 7, [Bwu[sl]] + uRt, [Bbank[bk]])
                                tcopy("act", raw[s_][par][:, 2:514], banks[bk][:, :], [Bbank[bk]], [Braw[s_][par]])
                                tcopy("act", raw[s_][1 - par][:, 0:2], banks[bk][:, 510:512], [Bbank[bk]], [Bhal[s_][1 - par]])
                                wcol = PP_FCW + ch * 3
                                if FFN_ACT_TAP:
                                    act(ct[s_][par][:, :], banks[bk][:, :], AF.Identity, [Bbank[bk], Bpp], [Bct[s_][par]],
                                        bias=ppc(l, PP_FCB + ch), scale=ppc(l, wcol + 2))
                                else:
                                    ts("dve", ct[s_][par][:, :], raw[s_][par][:, 2:514], ppc(l, wcol + 2), ppc(l, PP_FCB + ch),
                                       ALU.mult, ALU.add, [Braw[s_][par], Bpp], [Bct[s_][par]])
                                for kk in (1, 0):
                                    stt(ct[s_][par][:, :], raw[s_][par][:, kk:kk + 512], ppc(l, wcol + kk), ct[s_][par][:, :],
                                        ALU.mult, ALU.add, [Braw[s_][par], Bhal[s_][par], Bct[s_][par], Bpp], [Bct[s_][par]])
                            if pend["v"] is not None:
                                finish(*pend["v"])
                            pend["v"] = (t, j, par)
                finish(*pend["v"])
            S.barrier()

        def phase_ffn_down(l, last):
            TT = 256
            with ExitStack() as ps:
                wd = sb("fd_w", [128, 22, 1024], BF16, ps)
                Bwd = Buf()
                wload(wd[:, 0:11, :], w_down[l, 0:1408].rearrange("(c p) f -> p c f", p=128), Bwd)
                wload(wd[:, 11:22, :], w_down[l, 1408:2816].rearrange("(c p) f -> p c f", p=128), Bwd)
                at = [sb("fd_at%d" % i, [128, 22, 512], BF16, ps) for i in range(2)]
                Bat = [Buf(), Buf()]
                xt = [sb("fd_xt%d" % i, [128, 8, TT], F32, ps) for i in range(2)]
                Bxt = [Buf(), Buf()]
                ot = [sb("fd_ot%d" % i, [128, 8, TT], F32, ps) for i in range(2)] if last else None
                Bot = [Buf(), Buf()]
                sq = [sb("fd_sq%d" % i, [128, 512], BF16, ps) for i in range(2)]
                Bsq = [Buf(), Buf()]
                rs = sb("fd_rs", [128, 512], F32, ps)
                Brs = Buf()
                ntt = SEQ // TT

                def load_a(t):
                    S.dma("sp", at[t % 2][:], act_d[t], [Bact[t]], [Bat[t % 2]])

                def load_x(tt_):
                    a0 = tt_ * TT
                    S.dma("sp", xt[tt_ % 2][:], xr_v[:, :, a0:a0 + TT], [Bxr[tt_]], [Bxt[tt_ % 2]])

                load_a(0)
                load_x(0)
                for tt_ in range(ntt):
                    p = tt_ % 2
                    a0 = tt_ * TT
                    t5, off = divmod(a0, 512)
                    if off == 0 and t5 + 1 < NT:
                        load_a(t5 + 1)
                    if tt_ + 1 < ntt:
                        load_x(tt_ + 1)
                    for oc in range(8):
                        bk = rr["bank"] % 6
                        rr["bank"] += 1
                        for kc in range(22):
                            mm(banks[bk][:, 0:TT], wd[:, kc, oc * 128:(oc + 1) * 128], at[t5 % 2][:, kc, off:off + TT],
                               kc == 0, kc == 21, [Bwd, Bat[t5 % 2]], [Bbank[bk]])
                        tt("dve", xt[p][:, oc, :], banks[bk][:, 0:TT], xt[p][:, oc, :], ALU.add, [Bbank[bk], Bxt[p]], [Bxt[p]])
                    uRt = uR(a0, a0 + TT)
                    if not last:
                        S.dma("sp", xr_v[:, :, a0:a0 + TT], xt[p][:], [Bxt[p]], [Bxr[tt_]])
                        norm_tile(None, xt[p], Bxt[p], PP_LN1, l + 1, lambda c: uT[:, c, a0:a0 + TT], uRt, TT,
                                  banks[6], Bbank[6], sq, Bsq, rs, Brs)
                    else:
                        if debug:
                            S.dma("sp", xr_v[:, :, a0:a0 + TT], xt[p][:], [Bxt[p]], [Bxr[tt_]])
                        norm_tile(None, xt[p], Bxt[p], PP_FG, l, lambda c: ot[p][:, c, :], [Bot[p]], TT,
                                  banks[6], Bbank[6], sq, Bsq, rs, Brs)
                        S.dma("sp", outT_v[:, :, a0:a0 + TT], ot[p][:], [Bot[p]], [Bout])
            S.barrier()

        phase_norm0()
        for l in range(depth):
            for name, fn in (("attn", lambda: phase_attn(l)), ("poolz", lambda: phase_poolz(l)),
                             ("ssd", lambda: phase_ssd(l)), ("merge", lambda: phase_merge(l)),
                             ("ffn_up", lambda: phase_ffn_up(l)),
                             ("ffn_down", lambda: phase_ffn_down(l, l == depth - 1))):
                if stop["flag"]:
                    break
                if name not in SKIP:
                    fn()
                done(l, name)
            if stop["flag"]:
                break
        if stop["flag"]:
            with ExitStack() as ps:
                z = sb("dbg_z", [128, 8, 512], F32, ps)
                Bz_ = Buf()
                memset("dve", z[:], 0.0, [Bz_])
                for t in range(NT):
                    S.dma("sp", outT_v[:, :, t * 512:(t + 1) * 512], z[:], [Bz_], [Bout])

        S.emit(lambda name: es.enter_context(nc.semaphore(name)))
    nc._sched_stats = {e: len(S.streams[e]) for e in ENGS}
    nc._sched_stats["waits"] = S.nwait
    return nc


def _t5_bucket(dist):
    dist = np.asarray(dist, dtype=np.int64)
    max_exact = 16
    nf = np.maximum(dist, 1).astype(np.float32)
    large = max_exact + (np.log(nf / np.float32(max_exact)) / np.float32(math.log(2048 / max_exact))
                         * np.float32(32 - max_exact)).astype(np.int32)
    large = np.minimum(large, 31)
    return np.where(dist < max_exact, dist, large)


def _bias_tables(rel_bias):
    out = np.full((128, 18, 2, 128), NEG, dtype=np.float32)
    j = np.arange(128)[:, None]
    i = np.arange(128)[None, :]
    for g, (win, d) in enumerate(GROUPS):
        rel_c = i - j
        rel_p = i + 128 - j
        for hh in range(6):
            h = g * 6 + hh
            bc = rel_bias[_t5_bucket(np.clip(rel_c, 0, None) * d), h]
            out[:, h, 0, :] = np.where(rel_c >= 0, bc, NEG)
            bp = rel_bias[_t5_bucket(np.clip(rel_p, 0, None) * d), h]
            out[:, h, 1, :] = np.where(rel_p <= 128, bp, NEG)
    return out.reshape(128, 18, 256)


def _constants():
    c = np.zeros((128, NCST), dtype=np.float32)
    c[:, C_ID:C_ID + 128] = np.eye(128, dtype=np.float32)
    c[:, C_ONE:C_ONE + 128] = 1.0
    a = np.arange(128)
    c[:, C_TRI:C_TRI + 128] = (a[:, None] <= a[None, :]).astype(np.float32)
    tp = a[:, None]
    t = a[None, :]
    for gi, w in enumerate(POOLW):
        cur = ((t - tp >= 0) & (t - tp < w)).astype(np.float32) / w - (t == tp).astype(np.float32)
        prev = ((t + 128 - tp) < w).astype(np.float32) / w
        cnt = np.minimum(t + 1, w).astype(np.float32)
        cur0 = ((t - tp >= 0) & (t - tp < w)).astype(np.float32) / cnt - (t == tp).astype(np.float32)
        for kind, m in enumerate((cur, prev, cur0)):
            o = C_BAND + (kind * 4 + gi) * 128
            c[:, o:o + 128] = m
    return c


def _pack_params(inp, depth):
    pp = np.zeros((128, depth, NPP), dtype=np.float32)

    def fm(v, n):
        return np.asarray(v, dtype=np.float32).reshape(n, 128).T

    for l in range(depth):
        pp[:, l, PP_LN1:PP_LN1 + 8] = fm(inp["ln1_g"][l], 8)
        pp[:, l, PP_LN2:PP_LN2 + 8] = fm(inp["ln2_g"][l], 8)
        pp[:, l, PP_BG:PP_BG + 24] = fm(inp["b_gate"][l], 24)
        pp[:, l, PP_PSC:PP_PSC + 8] = fm(inp["pool_scale"][l], 8)
        cw = np.asarray(inp["ssd_conv_w"][l], dtype=np.float32)
        pp[:, l, PP_SCW:PP_SCW + 48] = cw.reshape(4, 12, 128).transpose(2, 1, 0).reshape(128, 48)
        pp[:, l, PP_SCB:PP_SCB + 12] = fm(inp["ssd_conv_b"][l], 12)
        pp[:, l, PP_NW:PP_NW + 8] = fm(inp["ssd_norm_w"][l], 8)
        pp[:, l, PP_DSK:PP_DSK + 8] = fm(np.repeat(np.asarray(inp["ssd_d"][l], dtype=np.float32), 64), 8)
        fw = np.asarray(inp["ffn_conv_w"][l], dtype=np.float32)
        pp[:, l, PP_FCW:PP_FCW + 132] = fw.reshape(3, 44, 128).transpose(2, 1, 0).reshape(128, 132)
        pp[:, l, PP_FCB:PP_FCB + 44] = fm(inp["ffn_conv_b"][l], 44)
        pp[:, l, PP_DTB:PP_DTB + 16] = np.asarray(inp["ssd_dt_bias"][l], dtype=np.float32)[None, :]
        pp[:, l, PP_ALOG:PP_ALOG + 16] = np.asarray(inp["ssd_a_log"][l], dtype=np.float32)[None, :]
        pp[:, l, PP_FG:PP_FG + 8] = fm(inp["final_g"], 8)
    return pp


_CACHE = {}


def make_in_maps(inp, depth, cores):
    f = lambda a: np.ascontiguousarray(np.asarray(a, dtype=np.float32))
    shared = {
        "w_in": f(inp["w_in"][:depth]), "w_a": f(inp["w_a"][:depth]), "pool_w": f(inp["pool_w"][:depth]),
        "w_b": f(inp["w_b"][:depth]), "w_c": f(inp["w_c"][:depth]), "w_o": f(inp["w_o"][:depth]),
        "w_up": f(inp["ffn_w_up"][:depth]), "w_down": f(inp["ffn_w_down"][:depth]),
        "pp": _pack_params(inp, depth), "cst": _constants(), "biasT": _bias_tables(f(inp["rel_bias"])),
    }
    x = np.asarray(inp["x"], dtype=np.float32)
    maps = []
    for b in cores:
        m = dict(shared)
        m["xT"] = np.ascontiguousarray(x[b].T)
        maps.append(m)
    return maps


def kernel(**inputs):
    if "nc" not in _CACHE:
        _CACHE["nc"] = build_program(DEPTH)
    nc = _CACHE["nc"]
    in_maps = make_in_maps(inputs, DEPTH, list(range(NCORES)))
    res = run_bass_kernel_spmd(nc, in_maps, core_ids=list(range(NCORES)))
    out = np.stack([np.ascontiguousarray(r["outT"].T) for r in res.results], axis=0)
    return out.astype(np.float32)
```

```python
import math
import numpy as np
from contextlib import ExitStack
import concourse.bass as bass
import concourse.mybir as mybir
from concourse.bass_utils import run_bass_kernel_spmd

F32 = mybir.dt.float32
BF16 = mybir.dt.bfloat16
AF = mybir.ActivationFunctionType
ALU = mybir.AluOpType

D = 1024
SEQ = 4096
DEPTH = 4
NCORES = 8
IN_W = 10128
Q0, K0, V0, P0, Z0, X0, DT0, G0 = 0, 1152, 2304, 3456, 4480, 5504, 7040, 7056
DFF = 2816
EPS = 1e-6
GROUPS = ((128, 1), (512, 4), (2048, 16))
POOLW = (2, 4, 8, 16)
NEG = -30000.0
import os
SSD_LEVEL = int(os.environ.get('SSD_LEVEL', '9'))
SKIP = os.environ.get('SKIP', '').split(',')
FFN_ACT_TAP = int(os.environ.get('FFN_ACT_TAP', '1'))
OPLIMIT = int(os.environ.get('OPLIMIT', '1000000000'))

PP_LN1, PP_LN2, PP_BG, PP_PSC, PP_SCW, PP_SCB, PP_NW, PP_DSK, PP_FCW, PP_FCB, PP_DTB, PP_ALOG, PP_FG = (
    0, 8, 16, 40, 48, 96, 108, 116, 124, 256, 300, 316, 332)
NPP = 340
C_ID, C_ONE, C_TRI, C_BAND = 0, 128, 256, 384
NCST = 384 + 12 * 128

ENGS = ("pe", "act", "dve", "pool", "sp")


class Buf:
    __slots__ = ("name", "w", "rs")

    def __init__(self, name=""):
        self.name = name
        self.w = None
        self.rs = []


class Op:
    __slots__ = ("eng", "idx", "fn", "deps", "dma", "sem", "val", "inc", "waits", "presem")

    def __init__(self, eng, idx, fn, dma):
        self.eng = eng
        self.idx = idx
        self.fn = fn
        self.deps = {}
        self.dma = dma
        self.sem = None
        self.val = 0
        self.inc = False
        self.waits = []
        self.presem = None


class Sched:
    def __init__(self, nc, n_dma_sems=40, raw_gap=3):
        self.nc = nc
        self.streams = {e: [] for e in ENGS}
        self.n_dma_sems = n_dma_sems
        self.raw_gap = raw_gap
        self.bar = None
        self.bar_seen = set()
        self.region = False
        self.rcount = 0

    def _add_dep(self, op, d, raw):
        if d is None or d is op:
            return
        if d.eng == op.eng and not d.dma:
            if op.eng == "pe" or not raw or op.dma:
                return
            if op.idx - d.idx >= self.raw_gap:
                return
        if d.dma:
            op.deps[("dma", d.eng, d.idx)] = d
        else:
            cur = op.deps.get(d.eng)
            if cur is None or cur.idx < d.idx:
                op.deps[d.eng] = d

    def barrier(self):
        self.bar = [st[-1] for st in self.streams.values() if st]
        self.bar_seen = set()

    def op(self, eng, fn, reads=(), writes=(), dma=False):
        if self.region:
            self.rcount += 1
            if self.rcount > OPLIMIT:
                return None
        st = self.streams[eng]
        o = Op(eng, len(st), fn, dma)
        if self.bar is not None and eng not in self.bar_seen:
            self.bar_seen.add(eng)
            for d in self.bar:
                if d.eng != eng or d.dma:
                    self._add_dep(o, d, False)
        for b in reads:
            self._add_dep(o, b.w, True)
        for b in writes:
            self._add_dep(o, b.w, False)
            for r in b.rs:
                self._add_dep(o, r, False)
        for b in writes:
            b.w = o
            b.rs = []
        for b in reads:
            if b.w is not o:
                b.rs.append(o)
        st.append(o)
        return o

    def dma(self, eng, out, in_, reads=(), writes=(), **kw):
        return self.op(eng, lambda e: e.dma_start(out=out, in_=in_, **kw), reads, writes, dma=True)

    def emit(self, sem_ctx):
        nc = self.nc
        esem = {e: sem_ctx("c_" + e) for e in ENGS if e != "sp"}
        dsem = {}
        for e in ENGS:
            if any(o.dma for o in self.streams[e]):
                dsem[e] = [sem_ctx("d_%s_%d" % (e, i)) for i in range(self.n_dma_sems)]
        for e in ENGS:
            for o in self.streams[e]:
                for d in o.deps.values():
                    if not d.dma:
                        d.inc = True
        for e in ENGS:
            cnt = 0
            dcnt = [0] * self.n_dma_sems
            k = 0
            for o in self.streams[e]:
                if o.dma:
                    s = k % self.n_dma_sems
                    k += 1
                    o.presem = (dsem[e][s], dcnt[s])
                    dcnt[s] += 16
                    o.sem = dsem[e][s]
                    o.val = dcnt[s]
                elif o.inc:
                    cnt += 1
                    o.sem = esem[e]
                    o.val = cnt
        nwait = 0
        for e in ENGS:
            known = {}
            for o in self.streams[e]:
                ws = []
                if o.dma and o.presem[1] > 0:
                    s, v = o.presem
                    if known.get(id(s), 0) < v:
                        known[id(s)] = v
                        ws.append((s, v))
                for d in o.deps.values():
                    if known.get(id(d.sem), 0) < d.val:
                        known[id(d.sem)] = d.val
                        ws.append((d.sem, d.val))
                o.waits = ws
                nwait += len(ws)
        self.nwait = nwait

        def run(eng_name, eh):
            final = {}
            for o in self.streams[eng_name]:
                for s, v in o.waits:
                    eh.wait_ge(s, v)
                ins = o.fn(eh)
                if o.dma:
                    ins.then_inc(o.sem, 16)
                    final[id(o.sem)] = (o.sem, o.val)
                elif o.inc:
                    ins.then_inc(o.sem, 1)
            for s, v in final.values():
                eh.wait_ge(s, v)

        with nc.Block() as block:
            if self.streams["pe"]:
                @block.tensor
                def _(eh):
                    run("pe", eh)
            if self.streams["act"]:
                @block.scalar
                def _(eh):
                    run("act", eh)
            if self.streams["dve"]:
                @block.vector
                def _(eh):
                    run("dve", eh)
            if self.streams["pool"]:
                @block.gpsimd
                def _(eh):
                    run("pool", eh)
            if self.streams["sp"]:
                @block.sync
                def _(eh):
                    run("sp", eh)


def build_program(depth=DEPTH, debug=False, upto=None):
    nc = bass.Bass("TRN2", target_bir_lowering=False)
    S = Sched(nc)
    NT = SEQ // 512
    dbg_kind = "ExternalOutput" if debug else "Internal"

    def dram_in(name, shape, dt=F32):
        return nc.dram_tensor(name, list(shape), dt, kind="ExternalInput").ap()

    xT = dram_in("xT", [D, SEQ])
    w_in = dram_in("w_in", [depth, D, IN_W])
    w_a = dram_in("w_a", [depth, 384, D])
    pool_w = dram_in("pool_w", [depth, 4, 256, 256])
    w_b = dram_in("w_b", [depth, D, D])
    w_c = dram_in("w_c", [depth, D, D])
    w_o = dram_in("w_o", [depth, D, D])
    w_up = dram_in("w_up", [depth, D, 2 * DFF])
    w_down = dram_in("w_down", [depth, DFF, D])
    pp_d = dram_in("pp", [128, depth, NPP])
    cst_d = dram_in("cst", [128, NCST])
    bias_d = dram_in("biasT", [128, 18, 256])
    outT = nc.dram_tensor("outT", [D, SEQ], F32, kind="ExternalOutput").ap()

    xr = nc.dram_tensor("xr", [D, SEQ], F32, kind=dbg_kind).ap()
    ao_d = nc.dram_tensor("ao_d", [3, 128, SEQ], BF16, kind=dbg_kind).ap()
    zs_d = nc.dram_tensor("zs_d", [NT, 128, 8, 512], BF16, kind=dbg_kind).ap()
    mac_d = nc.dram_tensor("mac_d", [NT, 128, 8, 512], BF16, kind=dbg_kind).ap()
    yn_d = nc.dram_tensor("yn_d", [NT, 128, 8, 512], BF16, kind=dbg_kind).ap()
    act_d = nc.dram_tensor("act_d", [NT, 128, 22, 512], BF16, kind=dbg_kind).ap()

    xT_v = xT.rearrange("(c p) t -> p c t", p=128)
    xr_v = xr.rearrange("(c p) t -> p c t", p=128)
    outT_v = outT.rearrange("(c p) t -> p c t", p=128)

    Bxr = [Buf() for _ in range(16)]
    Bao = Buf()
    Bzs = [Buf() for _ in range(NT)]
    Bmac = [Buf() for _ in range(NT)]
    Byn = [Buf() for _ in range(NT)]
    Bact = [Buf() for _ in range(NT)]
    Bout = Buf()

    es = ExitStack()
    with es:
        uid = {"n": 0}

        def sb(name, shape, dt, stack=es):
            uid["n"] += 1
            return stack.enter_context(nc.sbuf_tensor("s%d_%s" % (uid["n"], name), list(shape), dt))

        uT = sb("uT", [128, 8, SEQ], BF16)
        BuT = [Buf() for _ in range(16)]
        pp = sb("pp", [128, depth, NPP], F32)
        Bpp = Buf()
        cstf = sb("cstf", [128, 256], F32)
        cstb = sb("cstb", [128, NCST], BF16)
        Bcst = Buf()
        banks = [es.enter_context(nc.psum_tensor("bank%d" % i, [128, 512], F32)) for i in range(8)]
        Bbank = [Buf() for _ in range(8)]

        def uR(t0, t1):
            return BuT[t0 // 256:(t1 + 255) // 256]

        def mm(out, lhsT, rhs, start, stop, R, W):
            S.op("pe", lambda e: e.matmul(out, lhsT=lhsT, rhs=rhs, start=start, stop=stop), R, W)

        def transp(out, in_, R, W):
            S.op("pe", lambda e: e.transpose(out, in_, cstb[:, C_ID:C_ID + 128]), R, W)

        def act(out, in_, func, R, W, bias=None, scale=None):
            kw = {}
            if bias is not None:
                kw["bias"] = bias
            if scale is not None:
                kw["scale"] = scale
            S.op("act", lambda e: e.activation(out=out, in_=in_, func=func, **kw), R, W)

        def tcopy(eng, out, in_, R, W):
            if eng == "act":
                act(out, in_, AF.Copy, R, W)
            else:
                S.op(eng, lambda e: e.tensor_copy(out=out, in_=in_), R, W)

        def tt(eng, out, in0, in1, op, R, W):
            S.op(eng, lambda e: e.tensor_tensor(out=out, in0=in0, in1=in1, op=op), R, W)

        def ts(eng, out, in0, s1, s2, op0, op1, R, W):
            if op1 is None:
                S.op(eng, lambda e: e.tensor_scalar(out=out, in0=in0, scalar1=s1, scalar2=None, op0=op0), R, W)
            else:
                S.op(eng, lambda e: e.tensor_scalar(out=out, in0=in0, scalar1=s1, scalar2=s2, op0=op0, op1=op1), R, W)

        def stt(out, in0, scalar, in1, op0, op1, R, W):
            S.op("dve", lambda e: e.scalar_tensor_tensor(out=out, in0=in0, scalar=scalar, in1=in1, op0=op0, op1=op1), R, W)

        def memset(eng, ap, val, W):
            S.op(eng, lambda e: e.memset(ap, val), (), W)

        def wload(dst, src, B):
            S.dma("pool", dst, src, (), [B])

        rr = {"ev": 0, "bank": 0}

        def ev_eng():
            rr["ev"] += 1
            return "act" if rr["ev"] % 2 else "dve"

        S.dma("sp", pp[:], pp_d, (), [Bpp])
        S.dma("sp", cstf[:], cst_d[:, C_ONE:C_ONE + 256], (), [Bcst])
        S.dma("pool", cstb[:], cst_d, (), [Bcst])
        ident = cstb[:, C_ID:C_ID + 128]
        ones_b = cstb[:, C_ONE:C_ONE + 128]
        ones_f = cstf[:, 0:128]
        tri_f = cstf[:, 128:256]

        def band(kind, gi):
            o = C_BAND + (kind * 4 + gi) * 128
            return cstb[:, o:o + 128]

        def ppc(l, col, n=1):
            return pp[:, l, col:col + n]

        def norm_tile(st, xt, Bx, gcol0, l, dst_fn, Bdst, TT, nb, Bnb, sq, Bsq, rs, Brs):
            for c in range(8):
                act(sq[c % 2][:, 0:TT], xt[:, c, :], AF.Square, [Bx], [Bsq[c % 2]])
                mm(nb[:, 0:TT], ones_b, sq[c % 2][:, 0:TT], c == 0, c == 7, [Bsq[c % 2], Bcst], [Bnb])
            act(rs[:, 0:TT], nb[:, 0:TT], AF.Ln, [Bnb], [Brs], bias=EPS, scale=1.0 / D)
            act(rs[:, 0:TT], rs[:, 0:TT], AF.Exp, [Brs], [Brs], scale=-0.5)
            for c in range(8):
                stt(dst_fn(c), xt[:, c, :], ppc(l, gcol0 + c), rs[:, 0:TT], ALU.mult, ALU.mult,
                    [Bx, Brs, Bpp], Bdst)

        stop = {"flag": False}

        def done(l, name):
            if upto is not None and (l, name) == tuple(upto):
                stop["flag"] = True

        def phase_norm0():
            with ExitStack() as ps:
                xt = [sb("n0_xt%d" % i, [128, 8, 512], F32, ps) for i in range(2)]
                Bxt = [Buf() for _ in range(2)]
                sq = [sb("n0_sq%d" % i, [128, 512], BF16, ps) for i in range(2)]
                Bsq = [Buf(), Buf()]
                rs = sb("n0_rs", [128, 512], F32, ps)
                Brs = Buf()
                for t in range(NT):
                    t0 = t * 512
                    S.dma("sp", xt[t % 2][:], xT_v[:, :, t0:t0 + 512], (), [Bxt[t % 2]])
                    norm_tile(None, xt[t % 2], Bxt[t % 2], PP_LN1, 0,
                              lambda c: uT[:, c, t0:t0 + 512], uR(t0, t0 + 512), 512,
                              banks[0], Bbank[0], sq, Bsq, rs, Brs)
            S.barrier()

        def phase_attn(l):
            with ExitStack() as ps:
                bias_sb = sb("at_bias", [128, 18, 256], F32, ps)
                Bbias = Buf()
                S.dma("sp", bias_sb[:], bias_d, (), [Bbias])
                acc = sb("at_acc", [128, 2, SEQ], F32, ps)
                Bacc = Buf()
                qT = [sb("at_q%d" % i, [128, SEQ], BF16, ps) for i in range(2)]
                kT = [sb("at_k%d" % i, [128, SEQ], BF16, ps) for i in range(2)]
                vS = [sb("at_v%d" % i, [128, 32, 128], BF16, ps) for i in range(2)]
                wq = [sb("at_w%d" % i, [128, 8, 3, 128], BF16, ps) for i in range(2)]
                Bq = [Buf(), Buf()]
                Bk = [Buf(), Buf()]
                Bv = [Buf(), Buf()]
                Bw = [Buf(), Buf()]
                tmp = [sb("at_tmp%d" % i, [128, 256], F32, ps) for i in range(4)]
                pT = [sb("at_p%d" % i, [128, 256], BF16, ps) for i in range(4)]
                Btmp = [Buf() for _ in range(4)]
                BpT = [Buf() for _ in range(4)]
                osb = sb("at_o", [128, SEQ], BF16, ps)
                Bosb = Buf()
                w_l = w_in[l].rearrange("(c p) f -> p c f", p=128)
                it = 0
                combos = [(j, g) for j in range(3) for g in range(3)]

                def load_w(idx):
                    j, g = combos[idx]
                    sl = idx % 2
                    fo = (g * 6 + 2 * j) * 64
                    for wi, base in enumerate((Q0, K0, V0)):
                        wload(wq[sl][:, :, wi, :], w_l[:, :, base + fo:base + fo + 128], Bw[sl])

                load_w(0)
                for idx, (j, g) in enumerate(combos):
                    sl = idx % 2
                    if idx + 1 < len(combos):
                        load_w(idx + 1)
                    win, d = GROUPS[g]
                    L = SEQ // d
                    nbr = L // 128
                    for t in range(NT):
                        t0 = t * 512
                        for wi in range(2):
                            bk = rr["bank"] % 2
                            rr["bank"] += 1
                            for kc in range(8):
                                mm(banks[bk][:, :], wq[sl][:, kc, wi, :], uT[:, kc, t0:t0 + 512],
                                   kc == 0, kc == 7, [Bw[sl]] + uR(t0, t0 + 512), [Bbank[bk]])
                            dst_t = qT[sl] if wi == 0 else kT[sl]
                            Bd = Bq[sl] if wi == 0 else Bk[sl]
                            if d == 1:
                                src = banks[bk][:, :]
                                dst = dst_t[:, t0:t0 + 512]
                            else:
                                src = banks[bk][:, :].rearrange("p (m r) -> p r m", r=d)
                                dst = dst_t[:, :].rearrange("p (r l) -> p r l", r=d)[:, :, t0 // d:(t0 + 512) // d]
                            if wi == 0:
                                act(dst, src, AF.Copy, [Bbank[bk]], [Bd], scale=0.125)
                            else:
                                tcopy("dve", dst, src, [Bbank[bk]], [Bd])
                    uv = None
                    if d > 1:
                        uv = [uT[:, kc, :].rearrange("p (n i r) -> p r n i", r=d, i=128) for kc in range(8)]
                    for b4 in range(8):
                        bk = rr["bank"] % 2
                        rr["bank"] += 1
                        for bb in range(4):
                            b = b4 * 4 + bb
                            r, n = divmod(b, nbr)
                            for kc in range(8):
                                lhs = uT[:, kc, b * 128:(b + 1) * 128] if d == 1 else uv[kc][:, r, n, :]
                                mm(banks[bk][:, bb * 128:(bb + 1) * 128], lhs, wq[sl][:, kc, 2, :],
                                   kc == 0, kc == 7, [Bw[sl]] + BuT, [Bbank[bk]])
                        tcopy(ev_eng(), vS[sl][:, b4 * 4:(b4 + 1) * 4, :],
                              banks[bk][:, :].rearrange("p (b f) -> p b f", b=4), [Bbank[bk]], [Bv[sl]])
                    if d > 1:
                        accv = acc[:, :, :].rearrange("p a (n i r) -> p a r n i", r=d, i=128)
                    def emit_S(b):
                        r, n = divmod(b, nbr)
                        hp = n > 0
                        for hh in range(2):
                            sbk = 2 + (b % 2) * 2 + hh
                            ps_ = slice(hh * 64, (hh + 1) * 64)
                            mm(banks[sbk][:, 0:128], kT[sl][ps_, b * 128:(b + 1) * 128],
                               qT[sl][ps_, b * 128:(b + 1) * 128], True, True, [Bq[sl], Bk[sl]], [Bbank[sbk]])
                            if hp:
                                mm(banks[sbk][:, 128:256], kT[sl][ps_, (b - 1) * 128:b * 128],
                                   qT[sl][ps_, b * 128:(b + 1) * 128], True, True, [Bq[sl], Bk[sl]], [Bbank[sbk]])

                    def emit_add(b):
                        r, n = divmod(b, nbr)
                        ncol = 256 if n > 0 else 128
                        for hh in range(2):
                            sbk = 2 + (b % 2) * 2 + hh
                            bi = (b % 2) * 2 + hh
                            hg = g * 6 + 2 * j + hh
                            tt("dve", tmp[bi][:, 0:ncol], banks[sbk][:, 0:ncol], bias_sb[:, hg, 0:ncol], ALU.add,
                               [Bbank[sbk], Bbias], [Btmp[bi]])

                    def emit_exp(b):
                        r, n = divmod(b, nbr)
                        ncol = 256 if n > 0 else 128
                        for hh in range(2):
                            bi = (b % 2) * 2 + hh
                            act(pT[bi][:, 0:ncol], tmp[bi][:, 0:ncol], AF.Exp, [Btmp[bi]], [BpT[bi]])

                    def emit_PV(b):
                        r, n = divmod(b, nbr)
                        hp = n > 0
                        nzb = 6 + (b % 2)
                        for hh in range(2):
                            bi = (b % 2) * 2 + hh
                            ps_ = slice(hh * 64, (hh + 1) * 64)
                            mm(banks[nzb][ps_, 0:128], vS[sl][:, b, ps_], pT[bi][:, 0:128], True, not hp,
                               [Bv[sl], BpT[bi]], [Bbank[nzb]])
                            if hp:
                                mm(banks[nzb][ps_, 0:128], vS[sl][:, b - 1, ps_], pT[bi][:, 128:256], False, True,
                                   [Bv[sl], BpT[bi]], [Bbank[nzb]])
                            mm(banks[nzb][ps_, 128:256], ones_b[:, 0:64], pT[bi][:, 0:128], True, not hp,
                               [Bcst, BpT[bi]], [Bbank[nzb]])
                            if hp:
                                mm(banks[nzb][ps_, 128:256], ones_b[:, 0:64], pT[bi][:, 128:256], False, True,
                                   [Bcst, BpT[bi]], [Bbank[nzb]])

                    def emit_evac(b):
                        r, n = divmod(b, nbr)
                        nzb = 6 + (b % 2)
                        src = banks[nzb][:, 0:256].rearrange("p (a q) -> p a q", a=2)
                        if d == 1:
                            dst = acc[:, :, b * 128:(b + 1) * 128]
                        else:
                            dst = accv[:, :, r, n, :]
                        if g == 0:
                            tcopy("act", dst, src, [Bbank[nzb]], [Bacc])
                        else:
                            tt("dve", dst, src, dst, ALU.add, [Bbank[nzb], Bacc], [Bacc])

                    for k in range(32 + 4):
                        if k < 32:
                            emit_S(k)
                        if 0 <= k - 1 < 32:
                            emit_add(k - 1)
                        if 0 <= k - 2 < 32:
                            emit_exp(k - 2)
                        if 0 <= k - 3 < 32:
                            emit_PV(k - 3)
                        if 0 <= k - 4 < 32:
                            emit_evac(k - 4)
                    if g == 2:
                        for hf in range(4):
                            cs = slice(hf * 1024, (hf + 1) * 1024)
                            S.op("dve", lambda e, cs=cs: e.reciprocal(out=acc[:, 1, cs], in_=acc[:, 1, cs]), [Bacc], [Bacc])
                            tt("dve", osb[:, cs], acc[:, 0, cs], acc[:, 1, cs], ALU.mult, [Bacc], [Bosb])
                        S.dma("sp", ao_d[j], osb[:], [Bosb], [Bao])
            S.barrier()

        def phase_poolz(l):
            with ExitStack() as ps:
                Wp = sb("pz_wp", [128, 8, 1024], BF16, ps)
                Wz = sb("pz_wz", [128, 8, 1024], BF16, ps)
                pw = sb("pz_pw", [128, 4, 2, 256], BF16, ps)
                wb = sb("pz_wb", [128, 8, 1024], BF16, ps)
                Wg = sb("pz_wg", [128, 8, 1024], BF16, ps)
                BWp, BWz, Bpw, Bwb, BWg = [Buf() for _ in range(5)]
                w_l = w_in[l].rearrange("(c p) f -> p c f", p=128)
                wload(Wp[:], w_l[:, :, P0:P0 + 1024], BWp)
                for g in range(4):
                    wload(pw[:, g, :, :], pool_w[l, g].rearrange("(cc p) d -> p cc d", p=128), Bpw)
                wload(wb[:], w_b[l].rearrange("(c p) f -> p c f", p=128), Bwb)
                wload(Wg[:], w_l[:, :, G0 + 1024:G0 + 2048], BWg)
                wload(Wz[:], w_l[:, :, Z0:Z0 + 1024], BWz)
                pin = [sb("pz_pin%d" % i, [128, 1024], BF16, ps) for i in range(2)]
                Bpin = [Buf(), Buf()]
                dT = sb("pz_dT", [128, 8, 512], BF16, ps)
                BdT = Buf()
                yp = sb("pz_yp", [128, 8, 512], BF16, ps)
                Byp = Buf()
                g1 = [sb("pz_g%d" % i, [128, 512], F32, ps) for i in range(2)]
                Bg1 = [Buf(), Buf()]
                mac = [sb("pz_mac%d" % i, [128, 8, 512], BF16, ps) for i in range(2)]
                Bm = [Buf(), Buf()]
                zst = [sb("pz_zs%d" % i, [128, 8, 512], BF16, ps) for i in range(2)]
                Bz = [Buf(), Buf()]

                def nb4():
                    bk = rr["bank"] % 4
                    rr["bank"] += 1
                    return bk

                for t in range(NT):
                    t0 = t * 512
                    uRt = uR(t0, t0 + 512)
                    for blk in range(4):
                        b = t * 4 + blk
                        tok = b * 128
                        for half in range(2):
                            bk = nb4()
                            for kc in range(8):
                                mm(banks[bk][:, :], uT[:, kc, tok:tok + 128], Wp[:, kc, half * 512:(half + 1) * 512],
                                   kc == 0, kc == 7, [BWp] + uRt, [Bbank[bk]])
                            tcopy(ev_eng(), pin[b % 2][:, half * 512:(half + 1) * 512], banks[bk][:, :],
                                  [Bbank[bk]], [Bpin[b % 2]])
                        for half in range(2):
                            dbk = 4 + half
                            for c4 in range(4):
                                c = half * 4 + c4
                                gi = c // 2
                                mm(banks[dbk][:, c4 * 128:(c4 + 1) * 128], pin[b % 2][:, c * 128:(c + 1) * 128],
                                   band(2 if b == 0 else 0, gi), True, b == 0, [Bpin[b % 2], Bcst], [Bbank[dbk]])
                                if b > 0:
                                    mm(banks[dbk][:, c4 * 128:(c4 + 1) * 128], pin[(b - 1) % 2][:, c * 128:(c + 1) * 128],
                                       band(1, gi), False, True, [Bpin[(b - 1) % 2], Bcst], [Bbank[dbk]])
                            tcopy(ev_eng(), dT[:, half * 4:(half + 1) * 4, blk * 128:(blk + 1) * 128],
                                  banks[dbk][:, :].rearrange("p (c q) -> p c q", c=4), [Bbank[dbk]], [BdT])
                    for oc in range(8):
                        g, dc = divmod(oc, 2)
                        bk = nb4()
                        for cc in range(2):
                            mm(banks[bk][:, :], pw[:, g, cc, dc * 128:(dc + 1) * 128], dT[:, g * 2 + cc, :],
                               cc == 0, cc == 1, [Bpw, BdT], [Bbank[bk]])
                        act(yp[:, oc, :], banks[bk][:, :], AF.Identity, [Bbank[bk], Bpp], [Byp], scale=ppc(l, PP_PSC + oc))
                    for oc in range(8):
                        bka = nb4()
                        for kc in range(8):
                            mm(banks[bka][:, :], wb[:, kc, oc * 128:(oc + 1) * 128], yp[:, kc, :], kc == 0, kc == 7,
                               [Bwb, Byp], [Bbank[bka]])
                        bkg = nb4()
                        for kc in range(8):
                            mm(banks[bkg][:, :], Wg[:, kc, oc * 128:(oc + 1) * 128], uT[:, kc, t0:t0 + 512], kc == 0, kc == 7,
                               [BWg] + uRt, [Bbank[bkg]])
                        act(g1[oc % 2][:, :], banks[bkg][:, :], AF.Sigmoid, [Bbank[bkg], Bpp], [Bg1[oc % 2]],
                            bias=ppc(l, PP_BG + 8 + oc))
                        tt("dve", mac[t % 2][:, oc, :], banks[bka][:, :], g1[oc % 2][:, :], ALU.mult,
                           [Bbank[bka], Bg1[oc % 2]], [Bm[t % 2]])
                    S.dma("sp", mac_d[t], mac[t % 2][:], [Bm[t % 2]], [Bmac[t]])
                    for oc in range(8):
                        bk = nb4()
                        for kc in range(8):
                            mm(banks[bk][:, :], Wz[:, kc, oc * 128:(oc + 1) * 128], uT[:, kc, t0:t0 + 512], kc == 0, kc == 7,
                               [BWz] + uRt, [Bbank[bk]])
                        act(zst[t % 2][:, oc, :], banks[bk][:, :], AF.Silu, [Bbank[bk]], [Bz[t % 2]])
                    S.dma("sp", zs_d[t], zst[t % 2][:], [Bz[t % 2]], [Bzs[t]])
            S.barrier()

        def phase_ssd(l):
            with ExitStack() as ps:
                Wx = sb("sd_wx", [128, 8, 1536], BF16, ps)
                Wd = sb("sd_wd", [128, 8, 16], BF16, ps)
                BWx, BWd = Buf(), Buf()
                w_l = w_in[l].rearrange("(c p) f -> p c f", p=128)
                wload(Wx[:], w_l[:, :, X0:X0 + 1536], BWx)
                wload(Wd[:], w_l[:, :, DT0:DT0 + 16], BWd)
                zt = sb("sd_z", [128, 8, 512], BF16, ps)
                Bzt = Buf()
                xraw = sb("sd_xr", [128, 12, 515], BF16, ps)
                hal = sb("sd_hal", [128, 12, 3], BF16, ps)
                Bxraw = Buf()
                Bhalo = Buf()
                Bhal = Buf()
                ctmp = [sb("sd_ct%d" % i, [128, 512], F32, ps) for i in range(2)]
                Bct = [Buf(), Buf()]
                xcT2 = [sb("sd_xc%d" % i, [128, 12, 512], BF16, ps) for i in range(2)]
                BxcT2 = [Buf(), Buf()]
                yT = sb("sd_y", [128, 8, 512], BF16, ps)
                ByT = Buf()
                sq = [sb("sd_sq%d" % i, [128, 512], BF16, ps) for i in range(2)]
                Bsq = [Buf(), Buf()]
                rstd = sb("sd_rs", [128, 2, 512], F32, ps)
                Brs = Buf()
                A_bc = sb("sd_A", [128, 16], F32, ps)
                BA = Buf()

                def two(name, shape, dt):
                    return [sb("%s%d" % (name, i), shape, dt, ps) for i in range(2)]

                dtp = two("sd_dtp", [128, 4, 16], F32)
                dt_ = two("sd_dt", [128, 4, 16], F32)
                da = two("sd_da", [128, 4, 16], F32)
                acs = two("sd_acs", [128, 4, 16], F32)
                nacs = two("sd_nacs", [128, 4, 16], F32)
                dec = two("sd_dec", [128, 4, 16], F32)
                w2 = two("sd_w2", [128, 4, 16], F32)
                cd = two("sd_cd", [128, 4, 16], F32)
                Bsm = [Buf(), Buf()]
                dtri = two("sd_dtri", [128, 2, 4, 128], BF16)
                Bdabc = [Buf(), Buf()]
                da_h = two("sd_dah", [128, 4, 16], BF16)
                da_l = two("sd_dal", [128, 4, 16], F32)
                xs_tok = sb("sd_xst", [128, 1024], BF16, ps)
                Bxst = Buf()
                xc_tok = two("sd_xct", [128, 1024], BF16)
                xdec = two("sd_xdec", [128, 1024], BF16)
                Bxct, Bxdec = [Buf(), Buf()], [Buf(), Buf()]
                Btok = two("sd_btok", [128, 256], BF16)
                BBtok = [Buf(), Buf()]
                cbm = [two("sd_cbm%d_" % i, [128, 128], F32) for i in range(2)]
                Bcbm = [[Buf(), Buf()], [Buf(), Buf()]]
                E1 = [sb("sd_e1%d" % i, [128, 512], F32, ps) for i in range(3)]
                BE1 = [Buf() for _ in range(3)]
                OD = two("sd_od", [128, 512], F32)
                Gm = two("sd_g", [128, 512], BF16)
                Cod = two("sd_cod", [128, 512], BF16)
                BOD, BG, BCod = [[Buf(), Buf()] for _ in range(3)]
                prev_f = sb("sd_prevf", [128, 1024], F32, ps)
                prev_b = sb("sd_prevb", [128, 1024], BF16, ps)
                Bpf, Bpb = Buf(), Buf()

                act(A_bc[:], ppc(l, PP_ALOG, 16), AF.Exp, [Bpp], [BA])
                ts("dve", A_bc[:], A_bc[:], -1.0, None, ALU.mult, None, [BA], [BA])
                memset("dve", prev_f[:], 0.0, [Bpf])
                memset("dve", prev_b[:], 0.0, [Bpb])
                memset("dve", hal[:], 0.0, [Bhal])

                tb16 = banks[7][:, :].bitcast(BF16)
                misc = banks[2]
                mb16 = banks[2][:, :].bitcast(BF16)

                def conv(t):
                    t0 = t * 512
                    uRt = uR(t0, t0 + 512)
                    xcT = xcT2[t % 2]
                    BxcT = BxcT2[t % 2]
                    tcopy("pool", xraw[:, :, 0:3], hal[:], [Bhal], [Bhalo])
                    for c in range(12):
                        bk = rr["bank"] % 2
                        rr["bank"] += 1
                        for kc in range(8):
                            mm(banks[bk][:, :], Wx[:, kc, c * 128:(c + 1) * 128], uT[:, kc, t0:t0 + 512], kc == 0, kc == 7,
                               [BWx] + uRt, [Bbank[bk]])
                        tcopy("act", xraw[:, c, 3:515], banks[bk][:, :], [Bbank[bk]], [Bxraw])
                        tcopy("act", hal[:, c, :], banks[bk][:, 509:512], [Bbank[bk]], [Bhal])
                        ci = c % 2
                        wcol = PP_SCW + c * 4
                        ts("dve", ctmp[ci][:, :], xraw[:, c, 3:515], ppc(l, wcol + 3), ppc(l, PP_SCB + c),
                           ALU.mult, ALU.add, [Bxraw, Bpp], [Bct[ci]])
                        for k in (2, 1, 0):
                            stt(ctmp[ci][:, :], xraw[:, c, k:k + 512], ppc(l, wcol + k), ctmp[ci][:, :],
                                ALU.mult, ALU.add, [Bxraw, Bhalo, Bct[ci], Bpp], [Bct[ci]])
                        act(xcT[:, c, :], ctmp[ci][:, :], AF.Silu, [Bct[ci]], [BxcT])

                conv(0)
                for t in range(NT):
                    t0 = t * 512
                    uRt = uR(t0, t0 + 512)
                    xcT = xcT2[t % 2]
                    BxcT = BxcT2[t % 2]
                    S.dma("sp", zt[:], zs_d[t], [Bzs[t]], [Bzt])

                    tp = t % 2
                    A0 = banks[0]

                    def stageA():
                        for ci in range(4):
                            tk0 = t0 + ci * 128
                            for kc in range(8):
                                mm(A0[:, ci * 16:(ci + 1) * 16], uT[:, kc, tk0:tk0 + 128], Wd[:, kc, :], kc == 0, kc == 7,
                                   [BWd] + uRt, [Bbank[0]])
                        v3 = lambda ap: ap.rearrange("p (c h) -> p c h", c=4)
                        tt("dve", dtp[tp][:], v3(A0[:, 0:64]), ppc(l, PP_DTB, 16).unsqueeze(1).to_broadcast([128, 4, 16]),
                           ALU.add, [Bbank[0], Bpp], [Bsm[tp]])
                        act(dtp[tp][:], dtp[tp][:], AF.Exp, [Bsm[tp]], [Bsm[tp]])
                        act(dt_[tp][:], dtp[tp][:], AF.Ln, [Bsm[tp]], [Bsm[tp]], bias=1.0)
                        tt("dve", da[tp][:], dt_[tp][:], A_bc[:].unsqueeze(1).to_broadcast([128, 4, 16]), ALU.mult,
                           [Bsm[tp], BA], [Bsm[tp]])
                        tcopy("dve", da_h[tp][:], da[tp][:], [Bsm[tp]], [Bsm[tp]])
                        tt("dve", da_l[tp][:], da[tp][:], da_h[tp][:], ALU.subtract, [Bsm[tp]], [Bsm[tp]])
                        for ci in range(4):
                            mm(A0[:, 64 + ci * 16:64 + (ci + 1) * 16], tri_f, da[tp][:, ci, :], True, True, [Bcst, Bsm[tp]], [Bbank[0]])
                        for ci in range(4):
                            mm(A0[:, 128 + ci * 16:128 + (ci + 1) * 16], ones_f, da[tp][:, ci, :], True, True, [Bcst, Bsm[tp]], [Bbank[0]])
                        tcopy("dve", acs[tp][:], v3(A0[:, 64:128]), [Bbank[0]], [Bsm[tp]])
                        ts("dve", nacs[tp][:], v3(A0[:, 64:128]), -1.0, None, ALU.mult, None, [Bbank[0]], [Bsm[tp]])
                        tt("dve", dec[tp][:], v3(A0[:, 128:192]), acs[tp][:], ALU.subtract, [Bbank[0], Bsm[tp]], [Bsm[tp]])
                        act(dec[tp][:], dec[tp][:], AF.Exp, [Bsm[tp]], [Bsm[tp]])
                        act(cd[tp][:], v3(A0[:, 128:192]), AF.Exp, [Bbank[0]], [Bsm[tp]])
                        tt("dve", w2[tp][:], dt_[tp][:], dec[tp][:], ALU.mult, [Bsm[tp]], [Bsm[tp]])

                    def stageB(ci):
                        sl = ci % 2
                        cs = slice(ci * 128, ci * 128 + 128)
                        for c in range(8):
                            transp(tb16[:, c * 128:(c + 1) * 128], xcT[:, c, cs], [BxcT, Bcst], [Bbank[7]])
                        for g2 in range(2):
                            transp(mb16[:, 768 + g2 * 128:768 + (g2 + 1) * 128], xcT[:, 8 + g2, cs], [BxcT, Bcst], [Bbank[2]])
                        tcopy("act", Btok[sl][:], mb16[:, 768:1024], [Bbank[2]], [BBtok[sl]])
                        tcopy("act", xs_tok[:], tb16[:, :], [Bbank[7]], [Bxst])
                        tt("dve", xc_tok[sl][:].rearrange("p (h q) -> p h q", h=16), xs_tok[:].rearrange("p (h q) -> p h q", h=16),
                           dt_[tp][:, ci, :].unsqueeze(2).to_broadcast([128, 16, 64]), ALU.mult, [Bxst, Bsm[tp]], [Bxct[sl]])
                        tt("pool", xdec[sl][:].rearrange("p (h q) -> p h q", h=16), xs_tok[:].rearrange("p (h q) -> p h q", h=16),
                           w2[tp][:, ci, :].unsqueeze(2).to_broadcast([128, 16, 64]), ALU.mult, [Bxst, Bsm[tp]], [Bxdec[sl]])
                        for g2 in range(2):
                            mm(misc[:, g2 * 128:(g2 + 1) * 128], xcT[:, 8 + g2, cs], xcT[:, 10 + g2, cs], True, True,
                               [BxcT], [Bbank[2]])
                            tt("dve", cbm[sl][g2][:, :], misc[:, g2 * 128:(g2 + 1) * 128], tri_f, ALU.mult,
                               [Bbank[2], Bcst], [Bcbm[sl][g2]])

                    def states(ci, g2):
                        sl = ci % 2
                        mm(banks[1][:, :], Btok[sl][:, g2 * 128:(g2 + 1) * 128], xdec[sl][:, g2 * 512:(g2 + 1) * 512],
                           True, True, [BBtok[sl], Bxdec[sl]], [Bbank[1]])

                    def stage3a(ci):
                        for g2 in range(2):
                            states(ci, g2)
                            pg = prev_f[:, g2 * 512:(g2 + 1) * 512]
                            tt("dve", pg.rearrange("p (h q) -> p h q", h=8), pg.rearrange("p (h q) -> p h q", h=8),
                               cd[tp][:, ci, g2 * 8:(g2 + 1) * 8].unsqueeze(2).to_broadcast([128, 8, 64]), ALU.mult,
                               [Bpf, Bsm[tp]], [Bpf])
                            tt("dve", pg, banks[1][:, :], pg, ALU.add, [Bbank[1], Bpf], [Bpf])

                    def stage3b(ci):
                        tcopy("pool", prev_b[:], prev_f[:], [Bpf], [Bpb])

                    def s0(k):
                        ci, q4 = divmod(k, 4)
                        rb = k % 2
                        rbk = 3 + k % 3
                        hsl = slice(q4 * 4, (q4 + 1) * 4)
                        tri_bc = tri_f.unsqueeze(1).to_broadcast([128, 4, 128])
                        tt("dve", dtri[rb][:, 0, :, :], tri_bc, da[tp][:, ci, hsl].unsqueeze(2).to_broadcast([128, 4, 128]),
                           ALU.mult, [Bsm[tp], Bcst], [Bdabc[rb]])
                        tt("dve", dtri[rb][:, 1, :, :], tri_bc, da_l[tp][:, ci, hsl].unsqueeze(2).to_broadcast([128, 4, 128]),
                           ALU.mult, [Bsm[tp], Bcst], [Bdabc[rb]])
                        for part in range(2):
                            mm(banks[rbk][:, :], ones_b, dtri[rb][:, part, :, :].rearrange("p h q -> p (h q)"),
                               part == 0, part == 1, [Bdabc[rb], Bcst], [Bbank[rbk]])

                    def s1(k):
                        ci, q4 = divmod(k, 4)
                        rbk = 3 + k % 3
                        e = k % 3
                        for h4 in range(4):
                            h = q4 * 4 + h4
                            act(E1[e][:, h4 * 128:(h4 + 1) * 128], banks[rbk][:, h4 * 128:(h4 + 1) * 128], AF.Exp,
                                [Bbank[rbk], Bsm[tp]], [BE1[e]], bias=nacs[tp][:, ci, h:h + 1])

                    def s2(k):
                        rbk = 3 + k % 3
                        act(OD[k % 2][:, :], banks[rbk][:, :], AF.Exp, [Bbank[rbk]], [BOD[k % 2]])

                    def s3(k):
                        ci, q4 = divmod(k, 4)
                        sl = ci % 2
                        e = k % 3
                        rb = k % 2
                        g2 = q4 // 2
                        cs = slice(ci * 128, ci * 128 + 128)
                        stt(Gm[rb][:].rearrange("p (h q) -> p h q", h=4), E1[e][:].rearrange("p (h q) -> p h q", h=4), 1.0,
                            cbm[sl][g2][:, :].unsqueeze(1).to_broadcast([128, 4, 128]), ALU.min, ALU.mult,
                            [BE1[e], Bcbm[sl][g2]], [BG[rb]])
                        tt("dve", Cod[rb][:].rearrange("p (h q) -> p h q", h=4), OD[rb][:].rearrange("p (h q) -> p h q", h=4),
                           xcT[:, 10 + g2, cs].unsqueeze(1).to_broadcast([128, 4, 128]), ALU.mult,
                           [BOD[rb], BxcT], [BCod[rb]])

                    def s4(k):
                        ci, q4 = divmod(k, 4)
                        sl = ci % 2
                        rb = k % 2
                        for h4 in range(4):
                            h = q4 * 4 + h4
                            hs = slice(h * 64, (h + 1) * 64)
                            yc0 = rb * 256 + (h4 // 2) * 128
                            yo = banks[6][(h % 2) * 64:(h % 2 + 1) * 64, yc0:yc0 + 128]
                            mm(yo, xc_tok[sl][:, hs], Gm[rb][:, h4 * 128:(h4 + 1) * 128], True, False, [Bxct[sl], BG[rb]], [Bbank[6]])
                            mm(yo, prev_b[:, hs], Cod[rb][:, h4 * 128:(h4 + 1) * 128], False, True, [Bpb, BCod[rb]], [Bbank[6]])

                    def s5(k):
                        ci, q4 = divmod(k, 4)
                        rb = k % 2
                        cs = slice(ci * 128, ci * 128 + 128)
                        for pi in range(2):
                            pair = q4 * 2 + pi
                            yc0 = rb * 256 + pi * 128
                            stt(yT[:, pair, cs], xcT[:, pair, cs], ppc(l, PP_DSK + pair), banks[6][:, yc0:yc0 + 128],
                                ALU.mult, ALU.add, [BxcT, Bbank[6], Bpp], [ByT])

                    stageA()
                    stageB(0)
                    for k in range(16 + 5):
                        if k < 16:
                            s0(k)
                        if 0 <= k - 1 < 16:
                            s1(k - 1)
                        if 0 <= k - 2 < 16:
                            s2(k - 2)
                        if 0 <= k - 3 < 16:
                            s3(k - 3)
                        if 0 <= k - 4 < 16:
                            s4(k - 4)
                            if (k - 4) % 4 == 3:
                                stage3b((k - 4) // 4)
                        if 0 <= k - 5 < 16:
                            s5(k - 5)
                        if k % 4 == 3 and k // 4 < 4:
                            c = k // 4
                            stage3a(c)
                            if c + 1 < 4:
                                stageB(c + 1)
                        if k == 13 and t + 1 < NT:
                            conv(t + 1)
                    tt("pool", yT[:, :, :], yT[:, :, :], zt[:, :, :], ALU.mult, [ByT, Bzt], [ByT])
                    for g2 in range(2):
                        nbk = g2
                        for i in range(4):
                            c = g2 * 4 + i
                            act(sq[c % 2][:, :], yT[:, c, :], AF.Square, [ByT], [Bsq[c % 2]])
                            mm(banks[nbk][:, :], ones_b, sq[c % 2][:, :], i == 0, i == 3, [Bsq[c % 2], Bcst], [Bbank[nbk]])
                        act(rstd[:, g2, :], banks[nbk][:, :], AF.Ln, [Bbank[nbk]], [Brs], bias=EPS, scale=1.0 / 512)
                        act(rstd[:, g2, :], rstd[:, g2, :], AF.Exp, [Brs], [Brs], scale=-0.5)
                    for c in range(8):
                        stt(yT[:, c, :], yT[:, c, :], ppc(l, PP_NW + c), rstd[:, c // 4, :], ALU.mult, ALU.mult,
                            [ByT, Brs, Bpp], [ByT])
                    S.dma("sp", yn_d[t], yT[:], [ByT], [Byn[t]])
            S.barrier()

        def phase_merge(l):
            TT = 256
            with ExitStack() as ps:
                wa = sb("mg_wa", [128, 3, 1024], BF16, ps)
                Wg0 = sb("mg_wg0", [128, 8, 1024], BF16, ps)
                Wg2 = sb("mg_wg2", [128, 8, 1024], BF16, ps)
                wc = sb("mg_wc", [128, 8, 1024], BF16, ps)
                wo = sb("mg_wo", [128, 8, 1024], BF16, ps)
                Bwa, BWg0, BWg2, Bwc, Bwo = [Buf() for _ in range(5)]
                w_l = w_in[l].rearrange("(c p) f -> p c f", p=128)
                wload(wa[:], w_a[l].rearrange("(c p) f -> p c f", p=128), Bwa)
                wload(Wg0[:], w_l[:, :, G0:G0 + 1024], BWg0)
                wload(wc[:], w_c[l].rearrange("(c p) f -> p c f", p=128), Bwc)
                wload(Wg2[:], w_l[:, :, G0 + 2048:G0 + 3072], BWg2)
                wload(wo[:], w_o[l].rearrange("(c p) f -> p c f", p=128), Bwo)
                ao = [sb("mg_ao%d" % i, [128, 3, TT], BF16, ps) for i in range(2)]
                yn = [sb("mg_yn%d" % i, [128, 8, TT], BF16, ps) for i in range(2)]
                mc = [sb("mg_mc%d" % i, [128, 8, TT], BF16, ps) for i in range(2)]
                xt = [sb("mg_xt%d" % i, [128, 8, TT], F32, ps) for i in range(2)]
                Bao_t, Byn_t, Bmc_t, Bxt = [[Buf(), Buf()] for _ in range(4)]
                mg = sb("mg_mg", [128, 8, TT], BF16, ps)
                Bmg = Buf()
                gs = [sb("mg_gs%d" % i, [128, TT], F32, ps) for i in range(4)]
                Bgs = [Buf() for _ in range(4)]
                t1 = [sb("mg_t1%d" % i, [128, TT], F32, ps) for i in range(2)]
                t2 = [sb("mg_t2%d" % i, [128, TT], F32, ps) for i in range(2)]
                Bt1, Bt2 = [Buf(), Buf()], [Buf(), Buf()]
                sq = [sb("mg_sq%d" % i, [128, 512], BF16, ps) for i in range(2)]
                Bsq = [Buf(), Buf()]
                rs = sb("mg_rs", [128, 512], F32, ps)
                Brs = Buf()
                xsrc = xT_v if l == 0 else xr_v
                ao_v = ao_d.rearrange("j p t -> p j t")

                def nb6():
                    bk = rr["bank"] % 6
                    rr["bank"] += 1
                    return bk

                def loads(tt_):
                    p = tt_ % 2
                    a0 = tt_ * TT
                    t5, off = divmod(a0, 512)
                    S.dma("sp", ao[p][:], ao_v[:, :, a0:a0 + TT], [Bao], [Bao_t[p]])
                    S.dma("sp", yn[p][:], yn_d[t5][:, :, off:off + TT], [Byn[t5]], [Byn_t[p]])
                    S.dma("sp", mc[p][:], mac_d[t5][:, :, off:off + TT], [Bmac[t5]], [Bmc_t[p]])
                    S.dma("sp", xt[p][:], xsrc[:, :, a0:a0 + TT], [Bxr[tt_]], [Bxt[p]])

                ntt = SEQ // TT
                loads(0)
                for tt_ in range(ntt):
                    p = tt_ % 2
                    a0 = tt_ * TT
                    if tt_ + 1 < ntt:
                        loads(tt_ + 1)
                    uRt = uR(a0, a0 + TT)
                    for oc in range(8):
                        osl = slice(oc * 128, (oc + 1) * 128)
                        i2 = oc % 2
                        bka = nb6()
                        for kc in range(3):
                            mm(banks[bka][:, 0:TT], wa[:, kc, osl], ao[p][:, kc, :], kc == 0, kc == 2, [Bwa, Bao_t[p]], [Bbank[bka]])
                        bkg = nb6()
                        for kc in range(8):
                            mm(banks[bkg][:, 0:TT], Wg0[:, kc, osl], uT[:, kc, a0:a0 + TT], kc == 0, kc == 7, [BWg0] + uRt, [Bbank[bkg]])
                        act(gs[i2][:, :], banks[bkg][:, 0:TT], AF.Sigmoid, [Bbank[bkg], Bpp], [Bgs[i2]], bias=ppc(l, PP_BG + oc))
                        tt("dve", t1[i2][:, :], banks[bka][:, 0:TT], gs[i2][:, :], ALU.mult, [Bbank[bka], Bgs[i2]], [Bt1[i2]])
                        bkc = nb6()
                        for kc in range(8):
                            mm(banks[bkc][:, 0:TT], wc[:, kc, osl], yn[p][:, kc, :], kc == 0, kc == 7, [Bwc, Byn_t[p]], [Bbank[bkc]])
                        bkg2 = nb6()
                        for kc in range(8):
                            mm(banks[bkg2][:, 0:TT], Wg2[:, kc, osl], uT[:, kc, a0:a0 + TT], kc == 0, kc == 7, [BWg2] + uRt, [Bbank[bkg2]])
                        act(gs[2 + i2][:, :], banks[bkg2][:, 0:TT], AF.Sigmoid, [Bbank[bkg2], Bpp], [Bgs[2 + i2]],
                            bias=ppc(l, PP_BG + 16 + oc))
                        tt("dve", t2[i2][:, :], banks[bkc][:, 0:TT], gs[2 + i2][:, :], ALU.mult, [Bbank[bkc], Bgs[2 + i2]], [Bt2[i2]])
                        tt("pool", t1[i2][:, :], t1[i2][:, :], mc[p][:, oc, :], ALU.add, [Bt1[i2], Bmc_t[p]], [Bt1[i2]])
                        tt("pool", mg[:, oc, :], t1[i2][:, :], t2[i2][:, :], ALU.add, [Bt1[i2], Bt2[i2]], [Bmg])
                    for oc in range(8):
                        osl = slice(oc * 128, (oc + 1) * 128)
                        bk = nb6()
                        for kc in range(8):
                            mm(banks[bk][:, 0:TT], wo[:, kc, osl], mg[:, kc, :], kc == 0, kc == 7, [Bwo, Bmg], [Bbank[bk]])
                        tt("dve", xt[p][:, oc, :], banks[bk][:, 0:TT], xt[p][:, oc, :], ALU.add, [Bbank[bk], Bxt[p]], [Bxt[p]])
                    S.dma("sp", xr_v[:, :, a0:a0 + TT], xt[p][:], [Bxt[p]], [Bxr[tt_]])
                    norm_tile(None, xt[p], Bxt[p], PP_LN2, l, lambda c: uT[:, c, a0:a0 + TT], uRt, TT,
                              banks[6], Bbank[6], sq, Bsq, rs, Brs)
            S.barrier()

        def phase_ffn_up(l):
            with ExitStack() as ps:
                NG = 6
                wu = [sb("fu_w%d" % i, [128, 8, 2, 512], BF16, ps) for i in range(2)]
                Bwu = [Buf(), Buf()]
                raw = [[sb("fu_raw%d%d" % (s_, i), [128, 514], F32, ps) for i in range(2)] for s_ in range(2)]
                Braw = [[Buf(), Buf()], [Buf(), Buf()]]
                Bhal = [[Buf(), Buf()], [Buf(), Buf()]]
                ct = [[sb("fu_ct%d%d" % (s_, i), [128, 512], F32, ps) for i in range(2)] for s_ in range(2)]
                Bct = [[Buf(), Buf()], [Buf(), Buf()]]
                sa = [sb("fu_sa%d" % i, [128, 512], F32, ps) for i in range(2)]
                Bsa = [Buf(), Buf()]
                ab = [sb("fu_ab%d" % i, [128, 512], BF16, ps) for i in range(4)]
                Bab = [Buf() for _ in range(4)]
                wu_l = w_up[l].rearrange("(c p) f -> p c f", p=128)

                def load_g(gi):
                    sl = gi % 2
                    n = min(4, 22 - gi * 4) * 128
                    wload(wu[sl][:, :, 0, 0:n], wu_l[:, :, gi * 512:gi * 512 + n], Bwu[sl])
                    wload(wu[sl][:, :, 1, 0:n], wu_l[:, :, DFF + gi * 512:DFF + gi * 512 + n], Bwu[sl])

                pend = {"v": None, "k": 0}

                def finish(t, j, par):
                    act(sa[par][:, :], ct[0][par][:, :], AF.Silu, [Bct[0][par]], [Bsa[par]])
                    ai = pend["k"] % 4
                    pend["k"] += 1
                    tt("pool", ab[ai][:, :], sa[par][:, :], ct[1][par][:, :], ALU.mult, [Bsa[par], Bct[1][par]], [Bab[ai]])
                    S.dma("sp", act_d[t, :, j, :], ab[ai][:, :], [Bab[ai]], [Bact[t]])

                load_g(0)
                for gi in range(NG):
                    sl = gi % 2
                    if gi + 1 < NG:
                        load_g(gi + 1)
                    for jj in range(min(4, 22 - gi * 4)):
                        j = gi * 4 + jj
                        for s_ in range(2):
                            memset("dve", raw[s_][0][:, 0:2], 0.0, [Bhal[s_][0]])
                        for t in range(NT):
                            t0 = t * 512
                            par = t % 2
                            uRt = uR(t0, t0 + 512)
                            for s_ in range(2):
                                bk = rr["bank"] % 4
                                rr["bank"] += 1
                                ch = j if s_ == 0 else 22 + j
                                for kc in range(8):
                                    mm(banks[bk][:, :], wu[sl][:, kc, s_, jj * 128:(jj + 1) * 128], uT[:, kc, t0:t0 + 512],
                                       kc == 0, kc == 7, [Bwu[sl]] + uRt, [Bbank[bk]])
                                tcopy("act", raw[s_][par][:, 2:514], banks[bk][:, :], [Bbank[bk]], [Braw[s_][par]])
                                tcopy("act", raw[s_][1 - par][:, 0:2], banks[bk][:, 510:512], [Bbank[bk]], [Bhal[s_][1 - par]])
                                wcol = PP_FCW + ch * 3
                                if FFN_ACT_TAP:
                                    act(ct[s_][par][:, :], banks[bk][:, :], AF.Identity, [Bbank[bk], Bpp], [Bct[s_][par]],
                                        bias=ppc(l, PP_FCB + ch), scale=ppc(l, wcol + 2))
                                else:
                                    ts("dve", ct[s_][par][:, :], raw[s_][par][:, 2:514], ppc(l, wcol + 2), ppc(l, PP_FCB + ch),
                                       ALU.mult, ALU.add, [Braw[s_][par], Bpp], [Bct[s_][par]])
                                for kk in (1, 0):
                                    stt(ct[s_][par][:, :], raw[s_][par][:, kk:kk + 512], ppc(l, wcol + kk), ct[s_][par][:, :],
                                        ALU.mult, ALU.add, [Braw[s_][par], Bhal[s_][par], Bct[s_][par], Bpp], [Bct[s_][par]])
                            if pend["v"] is not None:
                                finish(*pend["v"])
                            pend["v"] = (t, j, par)
                finish(*pend["v"])
            S.barrier()

        def phase_ffn_down(l, last):
            TT = 256
            with ExitStack() as ps:
                wd = sb("fd_w", [128, 22, 1024], BF16, ps)
                Bwd = Buf()
                wload(wd[:, 0:11, :], w_down[l, 0:1408].rearrange("(c p) f -> p c f", p=128), Bwd)
                wload(wd[:, 11:22, :], w_down[l, 1408:2816].rearrange("(c p) f -> p c f", p=128), Bwd)
                at = [sb("fd_at%d" % i, [128, 22, 512], BF16, ps) for i in range(2)]
                Bat = [Buf(), Buf()]
                xt = [sb("fd_xt%d" % i, [128, 8, TT], F32, ps) for i in range(2)]
                Bxt = [Buf(), Buf()]
                ot = [sb("fd_ot%d" % i, [128, 8, TT], F32, ps) for i in range(2)] if last else None
                Bot = [Buf(), Buf()]
                sq = [sb("fd_sq%d" % i, [128, 512], BF16, ps) for i in range(2)]
                Bsq = [Buf(), Buf()]
                rs = sb("fd_rs", [128, 512], F32, ps)
                Brs = Buf()
                ntt = SEQ // TT

                def load_a(t):
                    S.dma("sp", at[t % 2][:], act_d[t], [Bact[t]], [Bat[t % 2]])

                def load_x(tt_):
                    a0 = tt_ * TT
                    S.dma("sp", xt[tt_ % 2][:], xr_v[:, :, a0:a0 + TT], [Bxr[tt_]], [Bxt[tt_ % 2]])

                load_a(0)
                load_x(0)
                for tt_ in range(ntt):
                    p = tt_ % 2
                    a0 = tt_ * TT
                    t5, off = divmod(a0, 512)
                    if off == 0 and t5 + 1 < NT:
                        load_a(t5 + 1)
                    if tt_ + 1 < ntt:
                        load_x(tt_ + 1)
                    for oc in range(8):
                        bk = rr["bank"] % 6
                        rr["bank"] += 1
                        for kc in range(22):
                            mm(banks[bk][:, 0:TT], wd[:, kc, oc * 128:(oc + 1) * 128], at[t5 % 2][:, kc, off:off + TT],
                               kc == 0, kc == 21, [Bwd, Bat[t5 % 2]], [Bbank[bk]])
                        tt("dve", xt[p][:, oc, :], banks[bk][:, 0:TT], xt[p][:, oc, :], ALU.add, [Bbank[bk], Bxt[p]], [Bxt[p]])
                    uRt = uR(a0, a0 + TT)
                    if not last:
                        S.dma("sp", xr_v[:, :, a0:a0 + TT], xt[p][:], [Bxt[p]], [Bxr[tt_]])
                        norm_tile(None, xt[p], Bxt[p], PP_LN1, l + 1, lambda c: uT[:, c, a0:a0 + TT], uRt, TT,
                                  banks[6], Bbank[6], sq, Bsq, rs, Brs)
                    else:
                        if debug:
                            S.dma("sp", xr_v[:, :, a0:a0 + TT], xt[p][:], [Bxt[p]], [Bxr[tt_]])
                        norm_tile(None, xt[p], Bxt[p], PP_FG, l, lambda c: ot[p][:, c, :], [Bot[p]], TT,
                                  banks[6], Bbank[6], sq, Bsq, rs, Brs)
                        S.dma("sp", outT_v[:, :, a0:a0 + TT], ot[p][:], [Bot[p]], [Bout])
            S.barrier()

        phase_norm0()
        for l in range(depth):
            for name, fn in (("attn", lambda: phase_attn(l)), ("poolz", lambda: phase_poolz(l)),
                             ("ssd", lambda: phase_ssd(l)), ("merge", lambda: phase_merge(l)),
                             ("ffn_up", lambda: phase_ffn_up(l)),
                             ("ffn_down", lambda: phase_ffn_down(l, l == depth - 1))):
                if stop["flag"]:
                    break
                if name not in SKIP:
                    fn()
                done(l, name)
            if stop["flag"]:
                break
        if stop["flag"]:
            with ExitStack() as ps:
                z = sb("dbg_z", [128, 8, 512], F32, ps)
                Bz_ = Buf()
                memset("dve", z[:], 0.0, [Bz_])
                for t in range(NT):
                    S.dma("sp", outT_v[:, :, t * 512:(t + 1) * 512], z[:], [Bz_], [Bout])

        S.emit(lambda name: es.enter_context(nc.semaphore(name)))
    nc._sched_stats = {e: len(S.streams[e]) for e in ENGS}
    nc._sched_stats["waits"] = S.nwait
    return nc


def _t5_bucket(dist):
    dist = np.asarray(dist, dtype=np.int64)
    max_exact = 16
    nf = np.maximum(dist, 1).astype(np.float32)
    large = max_exact + (np.log(nf / np.float32(max_exact)) / np.float32(math.log(2048 / max_exact))
                         * np.float32(32 - max_exact)).astype(np.int32)
    large = np.minimum(large, 31)
    return np.where(dist < max_exact, dist, large)


def _bias_tables(rel_bias):
    out = np.full((128, 18, 2, 128), NEG, dtype=np.float32)
    j = np.arange(128)[:, None]
    i = np.arange(128)[None, :]
    for g, (win, d) in enumerate(GROUPS):
        rel_c = i - j
        rel_p = i + 128 - j
        for hh in range(6):
            h = g * 6 + hh
            bc = rel_bias[_t5_bucket(np.clip(rel_c, 0, None) * d), h]
            out[:, h, 0, :] = np.where(rel_c >= 0, bc, NEG)
            bp = rel_bias[_t5_bucket(np.clip(rel_p, 0, None) * d), h]
            out[:, h, 1, :] = np.where(rel_p <= 128, bp, NEG)
    return out.reshape(128, 18, 256)


def _constants():
    c = np.zeros((128, NCST), dtype=np.float32)
    c[:, C_ID:C_ID + 128] = np.eye(128, dtype=np.float32)
    c[:, C_ONE:C_ONE + 128] = 1.0
    a = np.arange(128)
    c[:, C_TRI:C_TRI + 128] = (a[:, None] <= a[None, :]).astype(np.float32)
    tp = a[:, None]
    t = a[None, :]
    for gi, w in enumerate(POOLW):
        cur = ((t - tp >= 0) & (t - tp < w)).astype(np.float32) / w - (t == tp).astype(np.float32)
        prev = ((t + 128 - tp) < w).astype(np.float32) / w
        cnt = np.minimum(t + 1, w).astype(np.float32)
        cur0 = ((t - tp >= 0) & (t - tp < w)).astype(np.float32) / cnt - (t == tp).astype(np.float32)
        for kind, m in enumerate((cur, prev, cur0)):
            o = C_BAND + (kind * 4 + gi) * 128
            c[:, o:o + 128] = m
    return c


def _pack_params(inp, depth):
    pp = np.zeros((128, depth, NPP), dtype=np.float32)

    def fm(v, n):
        return np.asarray(v, dtype=np.float32).reshape(n, 128).T

    for l in range(depth):
        pp[:, l, PP_LN1:PP_LN1 + 8] = fm(inp["ln1_g"][l], 8)
        pp[:, l, PP_LN2:PP_LN2 + 8] = fm(inp["ln2_g"][l], 8)
        pp[:, l, PP_BG:PP_BG + 24] = fm(inp["b_gate"][l], 24)
        pp[:, l, PP_PSC:PP_PSC + 8] = fm(inp["pool_scale"][l], 8)
        cw = np.asarray(inp["ssd_conv_w"][l], dtype=np.float32)
        pp[:, l, PP_SCW:PP_SCW + 48] = cw.reshape(4, 12, 128).transpose(2, 1, 0).reshape(128, 48)
        pp[:, l, PP_SCB:PP_SCB + 12] = fm(inp["ssd_conv_b"][l], 12)
        pp[:, l, PP_NW:PP_NW + 8] = fm(inp["ssd_norm_w"][l], 8)
        pp[:, l, PP_DSK:PP_DSK + 8] = fm(np.repeat(np.asarray(inp["ssd_d"][l], dtype=np.float32), 64), 8)
        fw = np.asarray(inp["ffn_conv_w"][l], dtype=np.float32)
        pp[:, l, PP_FCW:PP_FCW + 132] = fw.reshape(3, 44, 128).transpose(2, 1, 0).reshape(128, 132)
        pp[:, l, PP_FCB:PP_FCB + 44] = fm(inp["ffn_conv_b"][l], 44)
        pp[:, l, PP_DTB:PP_DTB + 16] = np.asarray(inp["ssd_dt_bias"][l], dtype=np.float32)[None, :]
        pp[:, l, PP_ALOG:PP_ALOG + 16] = np.asarray(inp["ssd_a_log"][l], dtype=np.float32)[None, :]
        pp[:, l, PP_FG:PP_FG + 8] = fm(inp["final_g"], 8)
    return pp


_CACHE = {}


def make_in_maps(inp, depth, cores):
    f = lambda a: np.ascontiguousarray(np.asarray(a, dtype=np.float32))
    shared = {
        "w_in": f(inp["w_in"][:depth]), "w_a": f(inp["w_a"][:depth]), "pool_w": f(inp["pool_w"][:depth]),
        "w_b": f(inp["w_b"][:depth]), "w_c": f(inp["w_c"][:depth]), "w_o": f(inp["w_o"][:depth]),
        "w_up": f(inp["ffn_w_up"][:depth]), "w_down": f(inp["ffn_w_down"][:depth]),
        "pp": _pack_params(inp, depth), "cst": _constants(), "biasT": _bias_tables(f(inp["rel_bias"])),
    }
    x = np.asarray(inp["x"], dtype=np.float32)
    maps = []
    for b in cores:
        m = dict(shared)
        m["xT"] = np.ascontiguousarray(x[b].T)
        maps.append(m)
    return maps


def kernel(**inputs):
    if "nc" not in _CACHE:
        _CACHE["nc"] = build_program(DEPTH)
    nc = _CACHE["nc"]
    in_maps = make_in_maps(inputs, DEPTH, list(range(NCORES)))
    res = run_bass_kernel_spmd(nc, in_maps, core_ids=list(range(NCORES)))
    out = np.stack([np.ascontiguousarray(r["outT"].T) for r in res.results], axis=0)
    return out.astype(np.float32)
```

```python
import math
import numpy as np
from contextlib import ExitStack
import concourse.bass as bass
import concourse.mybir as mybir
from concourse.bass_utils import run_bass_kernel_spmd

F32 = mybir.dt.float32
BF16 = mybir.dt.bfloat16
AF = mybir.ActivationFunctionType
ALU = mybir.AluOpType

D = 1024
SEQ = 4096
DEPTH = 4
NCORES = 8
IN_W = 10128
Q0, K0, V0, P0, Z0, X0, DT0, G0 = 0, 1152, 2304, 3456, 4480, 5504, 7040, 7056
DFF = 2816
EPS = 1e-6
GROUPS = ((128, 1), (512, 4), (2048, 16))
POOLW = (2, 4, 8, 16)
NEG = -30000.0
import os
SSD_LEVEL = int(os.environ.get('SSD_LEVEL', '9'))
SKIP = os.environ.get('SKIP', '').split(',')
FFN_ACT_TAP = int(os.environ.get('FFN_ACT_TAP', '1'))
OPLIMIT = int(os.environ.get('OPLIMIT', '1000000000'))

PP_LN1, PP_LN2, PP_BG, PP_PSC, PP_SCW, PP_SCB, PP_NW, PP_DSK, PP_FCW, PP_FCB, PP_DTB, PP_ALOG, PP_FG = (
    0, 8, 16, 40, 48, 96, 108, 116, 124, 256, 300, 316, 332)
NPP = 340
C_ID, C_ONE, C_TRI, C_BAND = 0, 128, 256, 384
NCST = 384 + 12 * 128

ENGS = ("pe", "act", "dve", "pool", "sp")


class Buf:
    __slots__ = ("name", "w", "rs")

    def __init__(self, name=""):
        self.name = name
        self.w = None
        self.rs = []


class Op:
    __slots__ = ("eng", "idx", "fn", "deps", "dma", "sem", "val", "inc", "waits", "presem")

    def __init__(self, eng, idx, fn, dma):
        self.eng = eng
        self.idx = idx
        self.fn = fn
        self.deps = {}
        self.dma = dma
        self.sem = None
        self.val = 0
        self.inc = False
        self.waits = []
        self.presem = None


class Sched:
    def __init__(self, nc, n_dma_sems=40, raw_gap=3):
        self.nc = nc
        self.streams = {e: [] for e in ENGS}
        self.n_dma_sems = n_dma_sems
        self.raw_gap = raw_gap
        self.bar = None
        self.bar_seen = set()
        self.region = False
        self.rcount = 0

    def _add_dep(self, op, d, raw):
        if d is None or d is op:
            return
        if d.eng == op.eng and not d.dma:
            if op.eng == "pe" or not raw or op.dma:
                return
            if op.idx - d.idx >= self.raw_gap:
                return
        if d.dma:
            op.deps[("dma", d.eng, d.idx)] = d
        else:
            cur = op.deps.get(d.eng)
            if cur is None or cur.idx < d.idx:
                op.deps[d.eng] = d

    def barrier(self):
        self.bar = [st[-1] for st in self.streams.values() if st]
        self.bar_seen = set()

    def op(self, eng, fn, reads=(), writes=(), dma=False):
        if self.region:
            self.rcount += 1
            if self.rcount > OPLIMIT:
                return None
        st = self.streams[eng]
        o = Op(eng, len(st), fn, dma)
        if self.bar is not None and eng not in self.bar_seen:
            self.bar_seen.add(eng)
            for d in self.bar:
                if d.eng != eng or d.dma:
                    self._add_dep(o, d, False)
        for b in reads:
            self._add_dep(o, b.w, True)
        for b in writes:
            self._add_dep(o, b.w, False)
            for r in b.rs:
                self._add_dep(o, r, False)
        for b in writes:
            b.w = o
            b.rs = []
        for b in reads:
            if b.w is not o:
                b.rs.append(o)
        st.append(o)
        return o

    def dma(self, eng, out, in_, reads=(), writes=(), **kw):
        return self.op(eng, lambda e: e.dma_start(out=out, in_=in_, **kw), reads, writes, dma=True)

    def emit(self, sem_ctx):
        nc = self.nc
        esem = {e: sem_ctx("c_" + e) for e in ENGS if e != "sp"}
        dsem = {}
        for e in ENGS:
            if any(o.dma for o in self.streams[e]):
                dsem[e] = [sem_ctx("d_%s_%d" % (e, i)) for i in range(self.n_dma_sems)]
        for e in ENGS:
            for o in self.streams[e]:
                for d in o.deps.values():
                    if not d.dma:
                        d.inc = True
        for e in ENGS:
            cnt = 0
            dcnt = [0] * self.n_dma_sems
            k = 0
            for o in self.streams[e]:
                if o.dma:
                    s = k % self.n_dma_sems
                    k += 1
                    o.presem = (dsem[e][s], dcnt[s])
                    dcnt[s] += 16
                    o.sem = dsem[e][s]
                    o.val = dcnt[s]
                elif o.inc:
                    cnt += 1
                    o.sem = esem[e]
                    o.val = cnt
        nwait = 0
        for e in ENGS:
            known = {}
            for o in self.streams[e]:
                ws = []
                if o.dma and o.presem[1] > 0:
                    s, v = o.presem
                    if known.get(id(s), 0) < v:
                        known[id(s)] = v
                        ws.append((s, v))
                for d in o.deps.values():
                    if known.get(id(d.sem), 0) < d.val:
                        known[id(d.sem)] = d.val
                        ws.append((d.sem, d.val))
                o.waits = ws
                nwait += len(ws)
        self.nwait = nwait

        def run(eng_name, eh):
            final = {}
            for o in self.streams[eng_name]:
                for s, v in o.waits:
                    eh.wait_ge(s, v)
                ins = o.fn(eh)
                if o.dma:
                    ins.then_inc(o.sem, 16)
                    final[id(o.sem)] = (o.sem, o.val)
                elif o.inc:
                    ins.then_inc(o.sem, 1)
            for s, v in final.values():
                eh.wait_ge(s, v)

        with nc.Block() as block:
            if self.streams["pe"]:
                @block.tensor
                def _(eh):
                    run("pe", eh)
            if self.streams["act"]:
                @block.scalar
                def _(eh):
                    run("act", eh)
            if self.streams["dve"]:
                @block.vector
                def _(eh):
                    run("dve", eh)
            if self.streams["pool"]:
                @block.gpsimd
                def _(eh):
                    run("pool", eh)
            if self.streams["sp"]:
                @block.sync
                def _(eh):
                    run("sp", eh)


def build_program(depth=DEPTH, debug=False, upto=None):
    nc = bass.Bass("TRN2", target_bir_lowering=False)
    S = Sched(nc)
    NT = SEQ // 512
    dbg_kind = "ExternalOutput" if debug else "Internal"

    def dram_in(name, shape, dt=F32):
        return nc.dram_tensor(name, list(shape), dt, kind="ExternalInput").ap()

    xT = dram_in("xT", [D, SEQ])
    w_in = dram_in("w_in", [depth, D, IN_W])
    w_a = dram_in("w_a", [depth, 384, D])
    pool_w = dram_in("pool_w", [depth, 4, 256, 256])
    w_b = dram_in("w_b", [depth, D, D])
    w_c = dram_in("w_c", [depth, D, D])
    w_o = dram_in("w_o", [depth, D, D])
    w_up = dram_in("w_up", [depth, D, 2 * DFF])
    w_down = dram_in("w_down", [depth, DFF, D])
    pp_d = dram_in("pp", [128, depth, NPP])
    cst_d = dram_in("cst", [128, NCST])
    bias_d = dram_in("biasT", [128, 18, 256])
    outT = nc.dram_tensor("outT", [D, SEQ], F32, kind="ExternalOutput").ap()

    xr = nc.dram_tensor("xr", [D, SEQ], F32, kind=dbg_kind).ap()
    ao_d = nc.dram_tensor("ao_d", [3, 128, SEQ], BF16, kind=dbg_kind).ap()
    zs_d = nc.dram_tensor("zs_d", [NT, 128, 8, 512], BF16, kind=dbg_kind).ap()
    mac_d = nc.dram_tensor("mac_d", [NT, 128, 8, 512], BF16, kind=dbg_kind).ap()
    yn_d = nc.dram_tensor("yn_d", [NT, 128, 8, 512], BF16, kind=dbg_kind).ap()
    act_d = nc.dram_tensor("act_d", [NT, 128, 22, 512], BF16, kind=dbg_kind).ap()

    xT_v = xT.rearrange("(c p) t -> p c t", p=128)
    xr_v = xr.rearrange("(c p) t -> p c t", p=128)
    outT_v = outT.rearrange("(c p) t -> p c t", p=128)

    Bxr = [Buf() for _ in range(16)]
    Bao = Buf()
    Bzs = [Buf() for _ in range(NT)]
    Bmac = [Buf() for _ in range(NT)]
    Byn = [Buf() for _ in range(NT)]
    Bact = [Buf() for _ in range(NT)]
    Bout = Buf()

    es = ExitStack()
    with es:
        uid = {"n": 0}

        def sb(name, shape, dt, stack=es):
            uid["n"] += 1
            return stack.enter_context(nc.sbuf_tensor("s%d_%s" % (uid["n"], name), list(shape), dt))

        uT = sb("uT", [128, 8, SEQ], BF16)
        BuT = [Buf() for _ in range(16)]
        pp = sb("pp", [128, depth, NPP], F32)
        Bpp = Buf()
        cstf = sb("cstf", [128, 256], F32)
        cstb = sb("cstb", [128, NCST], BF16)
        Bcst = Buf()
        banks = [es.enter_context(nc.psum_tensor("bank%d" % i, [128, 512], F32)) for i in range(8)]
        Bbank = [Buf() for _ in range(8)]

        def uR(t0, t1):
            return BuT[t0 // 256:(t1 + 255) // 256]

        def mm(out, lhsT, rhs, start, stop, R, W):
            S.op("pe", lambda e: e.matmul(out, lhsT=lhsT, rhs=rhs, start=start, stop=stop), R, W)

        def transp(out, in_, R, W):
            S.op("pe", lambda e: e.transpose(out, in_, cstb[:, C_ID:C_ID + 128]), R, W)

        def act(out, in_, func, R, W, bias=None, scale=None):
            kw = {}
            if bias is not None:
                kw["bias"] = bias
            if scale is not None:
                kw["scale"] = scale
            S.op("act", lambda e: e.activation(out=out, in_=in_, func=func, **kw), R, W)

        def tcopy(eng, out, in_, R, W):
            if eng == "act":
                act(out, in_, AF.Copy, R, W)
            else:
                S.op(eng, lambda e: e.tensor_copy(out=out, in_=in_), R, W)

        def tt(eng, out, in0, in1, op, R, W):
            S.op(eng, lambda e: e.tensor_tensor(out=out, in0=in0, in1=in1, op=op), R, W)

        def ts(eng, out, in0, s1, s2, op0, op1, R, W):
            if op1 is None:
                S.op(eng, lambda e: e.tensor_scalar(out=out, in0=in0, scalar1=s1, scalar2=None, op0=op0), R, W)
            else:
                S.op(eng, lambda e: e.tensor_scalar(out=out, in0=in0, scalar1=s1, scalar2=s2, op0=op0, op1=op1), R, W)

        def stt(out, in0, scalar, in1, op0, op1, R, W):
            S.op("dve", lambda e: e.scalar_tensor_tensor(out=out, in0=in0, scalar=scalar, in1=in1, op0=op0, op1=op1), R, W)

        def memset(eng, ap, val, W):
            S.op(eng, lambda e: e.memset(ap, val), (), W)

        def wload(dst, src, B):
            S.dma("pool", dst, src, (), [B])

        rr = {"ev": 0, "bank": 0}

        def ev_eng():
            rr["ev"] += 1
            return "act" if rr["ev"] % 2 else "dve"

        S.dma("sp", pp[:], pp_d, (), [Bpp])
        S.dma("sp", cstf[:], cst_d[:, C_ONE:C_ONE + 256], (), [Bcst])
        S.dma("pool", cstb[:], cst_d, (), [Bcst])
        ident = cstb[:, C_ID:C_ID + 128]
        ones_b = cstb[:, C_ONE:C_ONE + 128]
        ones_f = cstf[:, 0:128]
        tri_f = cstf[:, 128:256]

        def band(kind, gi):
            o = C_BAND + (kind * 4 + gi) * 128
            return cstb[:, o:o + 128]

        def ppc(l, col, n=1):
            return pp[:, l, col:col + n]

        def norm_tile(st, xt, Bx, gcol0, l, dst_fn, Bdst, TT, nb, Bnb, sq, Bsq, rs, Brs):
            for c in range(8):
                act(sq[c % 2][:, 0:TT], xt[:, c, :], AF.Square, [Bx], [Bsq[c % 2]])
                mm(nb[:, 0:TT], ones_b, sq[c % 2][:, 0:TT], c == 0, c == 7, [Bsq[c % 2], Bcst], [Bnb])
            act(rs[:, 0:TT], nb[:, 0:TT], AF.Ln, [Bnb], [Brs], bias=EPS, scale=1.0 / D)
            act(rs[:, 0:TT], rs[:, 0:TT], AF.Exp, [Brs], [Brs], scale=-0.5)
            for c in range(8):
                stt(dst_fn(c), xt[:, c, :], ppc(l, gcol0 + c), rs[:, 0:TT], ALU.mult, ALU.mult,
                    [Bx, Brs, Bpp], Bdst)

        stop = {"flag": False}

        def done(l, name):
            if upto is not None and (l, name) == tuple(upto):
                stop["flag"] = True

        def phase_norm0():
            with ExitStack() as ps:
                xt = [sb("n0_xt%d" % i, [128, 8, 512], F32, ps) for i in range(2)]
                Bxt = [Buf() for _ in range(2)]
                sq = [sb("n0_sq%d" % i, [128, 512], BF16, ps) for i in range(2)]
                Bsq = [Buf(), Buf()]
                rs = sb("n0_rs", [128, 512], F32, ps)
                Brs = Buf()
                for t in range(NT):
                    t0 = t * 512
                    S.dma("sp", xt[t % 2][:], xT_v[:, :, t0:t0 + 512], (), [Bxt[t % 2]])
                    norm_tile(None, xt[t % 2], Bxt[t % 2], PP_LN1, 0,
                              lambda c: uT[:, c, t0:t0 + 512], uR(t0, t0 + 512), 512,
                              banks[0], Bbank[0], sq, Bsq, rs, Brs)
            S.barrier()

        def phase_attn(l):
            with ExitStack() as ps:
                bias_sb = sb("at_bias", [128, 18, 256], F32, ps)
                Bbias = Buf()
                S.dma("sp", bias_sb[:], bias_d, (), [Bbias])
                acc = sb("at_acc", [128, 2, SEQ], F32, ps)
                Bacc = Buf()
                qT = [sb("at_q%d" % i, [128, SEQ], BF16, ps) for i in range(2)]
                kT = [sb("at_k%d" % i, [128, SEQ], BF16, ps) for i in range(2)]
                vS = [sb("at_v%d" % i, [128, 32, 128], BF16, ps) for i in range(2)]
                wq = [sb("at_w%d" % i, [128, 8, 3, 128], BF16, ps) for i in range(2)]
                Bq = [Buf(), Buf()]
                Bk = [Buf(), Buf()]
                Bv = [Buf(), Buf()]
                Bw = [Buf(), Buf()]
                tmp = [sb("at_tmp%d" % i, [128, 256], F32, ps) for i in range(4)]
                pT = [sb("at_p%d" % i, [128, 256], BF16, ps) for i in range(4)]
                Btmp = [Buf() for _ in range(4)]
                BpT = [Buf() for _ in range(4)]
                osb = sb("at_o", [128, SEQ], BF16, ps)
                Bosb = Buf()
                w_l = w_in[l].rearrange("(c p) f -> p c f", p=128)
                it = 0
                combos = [(j, g) for j in range(3) for g in range(3)]

                def load_w(idx):
                    j, g = combos[idx]
                    sl = idx % 2
                    fo = (g * 6 + 2 * j) * 64
                    for wi, base in enumerate((Q0, K0, V0)):
                        wload(wq[sl][:, :, wi, :], w_l[:, :, base + fo:base + fo + 128], Bw[sl])

                load_w(0)
                for idx, (j, g) in enumerate(combos):
                    sl = idx % 2
                    if idx + 1 < len(combos):
                        load_w(idx + 1)
                    win, d = GROUPS[g]
                    L = SEQ // d
                    nbr = L // 128
                    for t in range(NT):
                        t0 = t * 512
                        for wi in range(2):
                            bk = rr["bank"] % 2
                            rr["bank"] += 1
                            for kc in range(8):
                                mm(banks[bk][:, :], wq[sl][:, kc, wi, :], uT[:, kc, t0:t0 + 512],
                                   kc == 0, kc == 7, [Bw[sl]] + uR(t0, t0 + 512), [Bbank[bk]])
                            dst_t = qT[sl] if wi == 0 else kT[sl]
                            Bd = Bq[sl] if wi == 0 else Bk[sl]
                            if d == 1:
                                src = banks[bk][:, :]
                                dst = dst_t[:, t0:t0 + 512]
                            else:
                                src = banks[bk][:, :].rearrange("p (m r) -> p r m", r=d)
                                dst = dst_t[:, :].rearrange("p (r l) -> p r l", r=d)[:, :, t0 // d:(t0 + 512) // d]
                            if wi == 0:
                                act(dst, src, AF.Copy, [Bbank[bk]], [Bd], scale=0.125)
                            else:
                                tcopy("dve", dst, src, [Bbank[bk]], [Bd])
                    uv = None
                    if d > 1:
                        uv = [uT[:, kc, :].rearrange("p (n i r) -> p r n i", r=d, i=128) for kc in range(8)]
                    for b4 in range(8):
                        bk = rr["bank"] % 2
                        rr["bank"] += 1
                        for bb in range(4):
                            b = b4 * 4 + bb
                            r, n = divmod(b, nbr)
                            for kc in range(8):
                                lhs = uT[:, kc, b * 128:(b + 1) * 128] if d == 1 else uv[kc][:, r, n, :]
                                mm(banks[bk][:, bb * 128:(bb + 1) * 128], lhs, wq[sl][:, kc, 2, :],
                                   kc == 0, kc == 7, [Bw[sl]] + BuT, [Bbank[bk]])
                        tcopy(ev_eng(), vS[sl][:, b4 * 4:(b4 + 1) * 4, :],
                              banks[bk][:, :].rearrange("p (b f) -> p b f", b=4), [Bbank[bk]], [Bv[sl]])
                    if d > 1:
                        accv = acc[:, :, :].rearrange("p a (n i r) -> p a r n i", r=d, i=128)
                    def emit_S(b):
                        r, n = divmod(b, nbr)
                        hp = n > 0
                        for hh in range(2):
                            sbk = 2 + (b % 2) * 2 + hh
                            ps_ = slice(hh * 64, (hh + 1) * 64)
                            mm(banks[sbk][:, 0:128], kT[sl][ps_, b * 128:(b + 1) * 128],
                               qT[sl][ps_, b * 128:(b + 1) * 128], True, True, [Bq[sl], Bk[sl]], [Bbank[sbk]])
                            if hp:
                                mm(banks[sbk][:, 128:256], kT[sl][ps_, (b - 1) * 128:b * 128],
                                   qT[sl][ps_, b * 128:(b + 1) * 128], True, True, [Bq[sl], Bk[sl]], [Bbank[sbk]])

                    def emit_add(b):
                        r, n = divmod(b, nbr)
                        ncol = 256 if n > 0 else 128
                        for hh in range(2):
                            sbk = 2 + (b % 2) * 2 + hh
                            bi = (b % 2) * 2 + hh
                            hg = g * 6 + 2 * j + hh
                            tt("dve", tmp[bi][:, 0:ncol], banks[sbk][:, 0:ncol], bias_sb[:, hg, 0:ncol], ALU.add,
                               [Bbank[sbk], Bbias], [Btmp[bi]])

                    def emit_exp(b):
                        r, n = divmod(b, nbr)
                        ncol = 256 if n > 0 else 128
                        for hh in range(2):
                            bi = (b % 2) * 2 + hh
                            act(pT[bi][:, 0:ncol], tmp[bi][:, 0:ncol], AF.Exp, [Btmp[bi]], [BpT[bi]])

                    def emit_PV(b):
                        r, n = divmod(b, nbr)
                        hp = n > 0
                        nzb = 6 + (b % 2)
                        for hh in range(2):
                            bi = (b % 2) * 2 + hh
                            ps_ = slice(hh * 64, (hh + 1) * 64)
                            mm(banks[nzb][ps_, 0:128], vS[sl][:, b, ps_], pT[bi][:, 0:128], True, not hp,
                               [Bv[sl], BpT[bi]], [Bbank[nzb]])
                            if hp:
                                mm(banks[nzb][ps_, 0:128], vS[sl][:, b - 1, ps_], pT[bi][:, 128:256], False, True,
                                   [Bv[sl], BpT[bi]], [Bbank[nzb]])
                            mm(banks[nzb][ps_, 128:256], ones_b[:, 0:64], pT[bi][:, 0:128], True, not hp,
                               [Bcst, BpT[bi]], [Bbank[nzb]])
                            if hp:
                                mm(banks[nzb][ps_, 128:256], ones_b[:, 0:64], pT[bi][:, 128:256], False, True,
                                   [Bcst, BpT[bi]], [Bbank[nzb]])

                    def emit_evac(b):
                        r, n = divmod(b, nbr)
                        nzb = 6 + (b % 2)
                        src = banks[nzb][:, 0:256].rearrange("p (a q) -> p a q", a=2)
                        if d == 1:
                            dst = acc[:, :, b * 128:(b + 1) * 128]
                        else:
                            dst = accv[:, :, r, n, :]
                        if g == 0:
                            tcopy("act", dst, src, [Bbank[nzb]], [Bacc])
                        else:
                            tt("dve", dst, src, dst, ALU.add, [Bbank[nzb], Bacc], [Bacc])

                    for k in range(32 + 4):
                        if k < 32:
                            emit_S(k)
                        if 0 <= k - 1 < 32:
                            emit_add(k - 1)
                        if 0 <= k - 2 < 32:
                            emit_exp(k - 2)
                        if 0 <= k - 3 < 32:
                            emit_PV(k - 3)
                        if 0 <= k - 4 < 32:
                            emit_evac(k - 4)
                    if g == 2:
                        for hf in range(4):
                            cs = slice(hf * 1024, (hf + 1) * 1024)
                            S.op("dve", lambda e, cs=cs: e.reciprocal(out=acc[:, 1, cs], in_=acc[:, 1, cs]), [Bacc], [Bacc])
                            tt("dve", osb[:, cs], acc[:, 0, cs], acc[:, 1, cs], ALU.mult, [Bacc], [Bosb])
                        S.dma("sp", ao_d[j], osb[:], [Bosb], [Bao])
            S.barrier()

        def phase_poolz(l):
            with ExitStack() as ps:
                Wp = sb("pz_wp", [128, 8, 1024], BF16, ps)
                Wz = sb("pz_wz", [128, 8, 1024], BF16, ps)
                pw = sb("pz_pw", [128, 4, 2, 256], BF16, ps)
                wb = sb("pz_wb", [128, 8, 1024], BF16, ps)
                Wg = sb("pz_wg", [128, 8, 1024], BF16, ps)
                BWp, BWz, Bpw, Bwb, BWg = [Buf() for _ in range(5)]
                w_l = w_in[l].rearrange("(c p) f -> p c f", p=128)
                wload(Wp[:], w_l[:, :, P0:P0 + 1024], BWp)
                for g in range(4):
                    wload(pw[:, g, :, :], pool_w[l, g].rearrange("(cc p) d -> p cc d", p=128), Bpw)
                wload(wb[:], w_b[l].rearrange("(c p) f -> p c f", p=128), Bwb)
                wload(Wg[:], w_l[:, :, G0 + 1024:G0 + 2048], BWg)
                wload(Wz[:], w_l[:, :, Z0:Z0 + 1024], BWz)
                pin = [sb("pz_pin%d" % i, [128, 1024], BF16, ps) for i in range(2)]
                Bpin = [Buf(), Buf()]
                dT = sb("pz_dT", [128, 8, 512], BF16, ps)
                BdT = Buf()
                yp = sb("pz_yp", [128, 8, 512], BF16, ps)
                Byp = Buf()
                g1 = [sb("pz_g%d" % i, [128, 512], F32, ps) for i in range(2)]
                Bg1 = [Buf(), Buf()]
                mac = [sb("pz_mac%d" % i, [128, 8, 512], BF16, ps) for i in range(2)]
                Bm = [Buf(), Buf()]
                zst = [sb("pz_zs%d" % i, [128, 8, 512], BF16, ps) for i in range(2)]
                Bz = [Buf(), Buf()]

                def nb4():
                    bk = rr["bank"] % 4
                    rr["bank"] += 1
                    return bk

                for t in range(NT):
                    t0 = t * 512
                    uRt = uR(t0, t0 + 512)
                    for blk in range(4):
                        b = t * 4 + blk
                        tok = b * 128
                        for half in range(2):
                            bk = nb4()
                            for kc in range(8):
                                mm(banks[bk][:, :], uT[:, kc, tok:tok + 128], Wp[:, kc, half * 512:(half + 1) * 512],
                                   kc == 0, kc == 7, [BWp] + uRt, [Bbank[bk]])
                            tcopy(ev_eng(), pin[b % 2][:, half * 512:(half + 1) * 512], banks[bk][:, :],
                                  [Bbank[bk]], [Bpin[b % 2]])
                        for half in range(2):
                            dbk = 4 + half
                            for c4 in range(4):
                                c = half * 4 + c4
                                gi = c // 2
                                mm(banks[dbk][:, c4 * 128:(c4 + 1) * 128], pin[b % 2][:, c * 128:(c + 1) * 128],
                                   band(2 if b == 0 else 0, gi), True, b == 0, [Bpin[b % 2], Bcst], [Bbank[dbk]])
                                if b > 0:
                                    mm(banks[dbk][:, c4 * 128:(c4 + 1) * 128], pin[(b - 1) % 2][:, c * 128:(c + 1) * 128],
                                       band(1, gi), False, True, [Bpin[(b - 1) % 2], Bcst], [Bbank[dbk]])
                            tcopy(ev_eng(), dT[:, half * 4:(half + 1) * 4, blk * 128:(blk + 1) * 128],
                                  banks[dbk][:, :].rearrange("p (c q) -> p c q", c=4), [Bbank[dbk]], [BdT])
                    for oc in range(8):
                        g, dc = divmod(oc, 2)
                        bk = nb4()
                        for cc in range(2):
                            mm(banks[bk][:, :], pw[:, g, cc, dc * 128:(dc + 1) * 128], dT[:, g * 2 + cc, :],
                               cc == 0, cc == 1, [Bpw, BdT], [Bbank[bk]])
                        act(yp[:, oc, :], banks[bk][:, :], AF.Identity, [Bbank[bk], Bpp], [Byp], scale=ppc(l, PP_PSC + oc))
                    for oc in range(8):
                        bka = nb4()
                        for kc in range(8):
                            mm(banks[bka][:, :], wb[:, kc, oc * 128:(oc + 1) * 128], yp[:, kc, :], kc == 0, kc == 7,
                               [Bwb, Byp], [Bbank[bka]])
                        bkg = nb4()
                        for kc in range(8):
                            mm(banks[bkg][:, :], Wg[:, kc, oc * 128:(oc + 1) * 128], uT[:, kc, t0:t0 + 512], kc == 0, kc == 7,
                               [BWg] + uRt, [Bbank[bkg]])
                        act(g1[oc % 2][:, :], banks[bkg][:, :], AF.Sigmoid, [Bbank[bkg], Bpp], [Bg1[oc % 2]],
                            bias=ppc(l, PP_BG + 8 + oc))
                        tt("dve", mac[t % 2][:, oc, :], banks[bka][:, :], g1[oc % 2][:, :], ALU.mult,
                           [Bbank[bka], Bg1[oc % 2]], [Bm[t % 2]])
                    S.dma("sp", mac_d[t], mac[t % 2][:], [Bm[t % 2]], [Bmac[t]])
                    for oc in range(8):
                        bk = nb4()
                        for kc in range(8):
                            mm(banks[bk][:, :], Wz[:, kc, oc * 128:(oc + 1) * 128], uT[:, kc, t0:t0 + 512], kc == 0, kc == 7,
                               [BWz] + uRt, [Bbank[bk]])
                        act(zst[t % 2][:, oc, :], banks[bk][:, :], AF.Silu, [Bbank[bk]], [Bz[t % 2]])
                    S.dma("sp", zs_d[t], zst[t % 2][:], [Bz[t % 2]], [Bzs[t]])
            S.barrier()

        def phase_ssd(l):
            with ExitStack() as ps:
                Wx = sb("sd_wx", [128, 8, 1536], BF16, ps)
                Wd = sb("sd_wd", [128, 8, 16], BF16, ps)
                BWx, BWd = Buf(), Buf()
                w_l = w_in[l].rearrange("(c p) f -> p c f", p=128)
                wload(Wx[:], w_l[:, :, X0:X0 + 1536], BWx)
                wload(Wd[:], w_l[:, :, DT0:DT0 + 16], BWd)
                zt = sb("sd_z", [128, 8, 512], BF16, ps)
                Bzt = Buf()
                xraw = sb("sd_xr", [128, 12, 515], BF16, ps)
                hal = sb("sd_hal", [128, 12, 3], BF16, ps)
                Bxraw = Buf()
                Bhalo = Buf()
                Bhal = Buf()
                ctmp = [sb("sd_ct%d" % i, [128, 512], F32, ps) for i in range(2)]
                Bct = [Buf(), Buf()]
                xcT2 = [sb("sd_xc%d" % i, [128, 12, 512], BF16, ps) for i in range(2)]
                BxcT2 = [Buf(), Buf()]
                yT = sb("sd_y", [128, 8, 512], BF16, ps)
                ByT = Buf()
                sq = [sb("sd_sq%d" % i, [128, 512], BF16, ps) for i in range(2)]
                Bsq = [Buf(), Buf()]
                rstd = sb("sd_rs", [128, 2, 512], F32, ps)
                Brs = Buf()
                A_bc = sb("sd_A", [128, 16], F32, ps)
                BA = Buf()

                def two(name, shape, dt):
                    return [sb("%s%d" % (name, i), shape, dt, ps) for i in range(2)]

                dtp = two("sd_dtp", [128, 4, 16], F32)
                dt_ = two("sd_dt", [128, 4, 16], F32)
                da = two("sd_da", [128, 4, 16], F32)
                acs = two("sd_acs", [128, 4, 16], F32)
                nacs = two("sd_nacs", [128, 4, 16], F32)
                dec = two("sd_dec", [128, 4, 16], F32)
                w2 = two("sd_w2", [128, 4, 16], F32)
                cd = two("sd_cd", [128, 4, 16], F32)
                Bsm = [Buf(), Buf()]
                dtri = two("sd_dtri", [128, 2, 4, 128], BF16)
                Bdabc = [Buf(), Buf()]
                da_h = two("sd_dah", [128, 4, 16], BF16)
                da_l = two("sd_dal", [128, 4, 16], F32)
                xs_tok = sb("sd_xst", [128, 1024], BF16, ps)
                Bxst = Buf()
                xc_tok = two("sd_xct", [128, 1024], BF16)
                xdec = two("sd_xdec", [128, 1024], BF16)
                Bxct, Bxdec = [Buf(), Buf()], [Buf(), Buf()]
                Btok = two("sd_btok", [128, 256], BF16)
                BBtok = [Buf(), Buf()]
                cbm = [two("sd_cbm%d_" % i, [128, 128], F32) for i in range(2)]
                Bcbm = [[Buf(), Buf()], [Buf(), Buf()]]
                E1 = [sb("sd_e1%d" % i, [128, 512], F32, ps) for i in range(3)]
                BE1 = [Buf() for _ in range(3)]
                OD = two("sd_od", [128, 512], F32)
                Gm = two("sd_g", [128, 512], BF16)
                Cod = two("sd_cod", [128, 512], BF16)
                BOD, BG, BCod = [[Buf(), Buf()] for _ in range(3)]
                prev_f = sb("sd_prevf", [128, 1024], F32, ps)
                prev_b2 = [sb("sd_prevb%d" % i, [128, 1024], BF16, ps) for i in range(2)]
                Bpf = Buf()
                Bpb2 = [Buf(), Buf()]

                act(A_bc[:], ppc(l, PP_ALOG, 16), AF.Exp, [Bpp], [BA])
                ts("dve", A_bc[:], A_bc[:], -1.0, None, ALU.mult, None, [BA], [BA])
                memset("dve", prev_f[:], 0.0, [Bpf])
                memset("dve", prev_b2[0][:], 0.0, [Bpb2[0]])
                memset("dve", hal[:], 0.0, [Bhal])

                tb16 = banks[7][:, :].bitcast(BF16)
                misc = banks[2]
                mb16 = banks[2][:, :].bitcast(BF16)

                def conv(t):
                    t0 = t * 512
                    uRt = uR(t0, t0 + 512)
                    xcT = xcT2[t % 2]
                    BxcT = BxcT2[t % 2]
                    tcopy("pool", xraw[:, :, 0:3], hal[:], [Bhal], [Bhalo])
                    for c in range(12):
                        bk = rr["bank"] % 2
                        rr["bank"] += 1
                        for kc in range(8):
                            mm(banks[bk][:, :], Wx[:, kc, c * 128:(c + 1) * 128], uT[:, kc, t0:t0 + 512], kc == 0, kc == 7,
                               [BWx] + uRt, [Bbank[bk]])
                        tcopy("act", xraw[:, c, 3:515], banks[bk][:, :], [Bbank[bk]], [Bxraw])
                        tcopy("act", hal[:, c, :], banks[bk][:, 509:512], [Bbank[bk]], [Bhal])
                        ci = c % 2
                        wcol = PP_SCW + c * 4
                        ts("dve", ctmp[ci][:, :], xraw[:, c, 3:515], ppc(l, wcol + 3), ppc(l, PP_SCB + c),
                           ALU.mult, ALU.add, [Bxraw, Bpp], [Bct[ci]])
                        for k in (2, 1, 0):
                            stt(ctmp[ci][:, :], xraw[:, c, k:k + 512], ppc(l, wcol + k), ctmp[ci][:, :],
                                ALU.mult, ALU.add, [Bxraw, Bhalo, Bct[ci], Bpp], [Bct[ci]])
                        act(xcT[:, c, :], ctmp[ci][:, :], AF.Silu, [Bct[ci]], [BxcT])

                conv(0)
                for t in range(NT):
                    t0 = t * 512
                    uRt = uR(t0, t0 + 512)
                    xcT = xcT2[t % 2]
                    BxcT = BxcT2[t % 2]
                    S.dma("sp", zt[:], zs_d[t], [Bzs[t]], [Bzt])

                    tp = t % 2
                    A0 = banks[0]

                    def stageA():
                        for ci in range(4):
                            tk0 = t0 + ci * 128
                            for kc in range(8):
                                mm(A0[:, ci * 16:(ci + 1) * 16], uT[:, kc, tk0:tk0 + 128], Wd[:, kc, :], kc == 0, kc == 7,
                                   [BWd] + uRt, [Bbank[0]])
                        v3 = lambda ap: ap.rearrange("p (c h) -> p c h", c=4)
                        tt("dve", dtp[tp][:], v3(A0[:, 0:64]), ppc(l, PP_DTB, 16).unsqueeze(1).to_broadcast([128, 4, 16]),
                           ALU.add, [Bbank[0], Bpp], [Bsm[tp]])
                        act(dtp[tp][:], dtp[tp][:], AF.Exp, [Bsm[tp]], [Bsm[tp]])
                        act(dt_[tp][:], dtp[tp][:], AF.Ln, [Bsm[tp]], [Bsm[tp]], bias=1.0)
                        tt("dve", da[tp][:], dt_[tp][:], A_bc[:].unsqueeze(1).to_broadcast([128, 4, 16]), ALU.mult,
                           [Bsm[tp], BA], [Bsm[tp]])
                        tcopy("dve", da_h[tp][:], da[tp][:], [Bsm[tp]], [Bsm[tp]])
                        tcopy("dve", da[tp][:], da_h[tp][:], [Bsm[tp]], [Bsm[tp]])
                        for ci in range(4):
                            mm(A0[:, 64 + ci * 16:64 + (ci + 1) * 16], tri_f, da[tp][:, ci, :], True, True, [Bcst, Bsm[tp]], [Bbank[0]])
                        for ci in range(4):
                            mm(A0[:, 128 + ci * 16:128 + (ci + 1) * 16], ones_f, da[tp][:, ci, :], True, True, [Bcst, Bsm[tp]], [Bbank[0]])
                        tcopy("dve", acs[tp][:], v3(A0[:, 64:128]), [Bbank[0]], [Bsm[tp]])
                        ts("dve", nacs[tp][:], v3(A0[:, 64:128]), -1.0, None, ALU.mult, None, [Bbank[0]], [Bsm[tp]])
                        tt("dve", dec[tp][:], v3(A0[:, 128:192]), acs[tp][:], ALU.subtract, [Bbank[0], Bsm[tp]], [Bsm[tp]])
                        act(dec[tp][:], dec[tp][:], AF.Exp, [Bsm[tp]], [Bsm[tp]])
                        act(cd[tp][:], v3(A0[:, 128:192]), AF.Exp, [Bbank[0]], [Bsm[tp]])
                        tt("dve", w2[tp][:], dt_[tp][:], dec[tp][:], ALU.mult, [Bsm[tp]], [Bsm[tp]])

                    def stageB(ci):
                        sl = ci % 2
                        cs = slice(ci * 128, ci * 128 + 128)
                        for c in range(8):
                            transp(tb16[:, c * 128:(c + 1) * 128], xcT[:, c, cs], [BxcT, Bcst], [Bbank[7]])
                        for g2 in range(2):
                            transp(mb16[:, 768 + g2 * 128:768 + (g2 + 1) * 128], xcT[:, 8 + g2, cs], [BxcT, Bcst], [Bbank[2]])
                        tcopy("act", Btok[sl][:], mb16[:, 768:1024], [Bbank[2]], [BBtok[sl]])
                        tcopy("act", xs_tok[:], tb16[:, :], [Bbank[7]], [Bxst])
                        tt("dve", xc_tok[sl][:].rearrange("p (h q) -> p h q", h=16), xs_tok[:].rearrange("p (h q) -> p h q", h=16),
                           dt_[tp][:, ci, :].unsqueeze(2).to_broadcast([128, 16, 64]), ALU.mult, [Bxst, Bsm[tp]], [Bxct[sl]])
                        tt("pool", xdec[sl][:].rearrange("p (h q) -> p h q", h=16), xs_tok[:].rearrange("p (h q) -> p h q", h=16),
                           w2[tp][:, ci, :].unsqueeze(2).to_broadcast([128, 16, 64]), ALU.mult, [Bxst, Bsm[tp]], [Bxdec[sl]])
                        for g2 in range(2):
                            mm(misc[:, g2 * 128:(g2 + 1) * 128], xcT[:, 8 + g2, cs], xcT[:, 10 + g2, cs], True, True,
                               [BxcT], [Bbank[2]])
                            tt("dve", cbm[sl][g2][:, :], misc[:, g2 * 128:(g2 + 1) * 128], tri_f, ALU.mult,
                               [Bbank[2], Bcst], [Bcbm[sl][g2]])

                    def states(ci, g2):
                        sl = ci % 2
                        mm(banks[1][:, :], Btok[sl][:, g2 * 128:(g2 + 1) * 128], xdec[sl][:, g2 * 512:(g2 + 1) * 512],
                           True, True, [BBtok[sl], Bxdec[sl]], [Bbank[1]])

                    def stage3a(ci):
                        for g2 in range(2):
                            states(ci, g2)
                            pg = prev_f[:, g2 * 512:(g2 + 1) * 512]
                            tt("dve", pg.rearrange("p (h q) -> p h q", h=8), pg.rearrange("p (h q) -> p h q", h=8),
                               cd[tp][:, ci, g2 * 8:(g2 + 1) * 8].unsqueeze(2).to_broadcast([128, 8, 64]), ALU.mult,
                               [Bpf, Bsm[tp]], [Bpf])
                            tt("dve", pg, banks[1][:, :], pg, ALU.add, [Bbank[1], Bpf], [Bpf])

                    def stage3b(ci):
                        gc = t * 4 + ci
                        tcopy("act", prev_b2[(gc + 1) % 2][:], prev_f[:], [Bpf], [Bpb2[(gc + 1) % 2]])

                    def s0(k):
                        ci, q4 = divmod(k, 4)
                        rb = k % 2
                        rbk = 3 + k % 3
                        hsl = slice(q4 * 4, (q4 + 1) * 4)
                        tri_bc = tri_f.unsqueeze(1).to_broadcast([128, 4, 128])
                        tt("dve", dtri[rb][:, 0, :, :], tri_bc, da[tp][:, ci, hsl].unsqueeze(2).to_broadcast([128, 4, 128]),
                           ALU.mult, [Bsm[tp], Bcst], [Bdabc[rb]])
                        mm(banks[rbk][:, :], ones_b, dtri[rb][:, 0, :, :].rearrange("p h q -> p (h q)"),
                           True, True, [Bdabc[rb], Bcst], [Bbank[rbk]])

                    def s1(k):
                        ci, q4 = divmod(k, 4)
                        rbk = 3 + k % 3
                        e = k % 3
                        for h4 in range(4):
                            h = q4 * 4 + h4
                            act(E1[e][:, h4 * 128:(h4 + 1) * 128], banks[rbk][:, h4 * 128:(h4 + 1) * 128], AF.Exp,
                                [Bbank[rbk], Bsm[tp]], [BE1[e]], bias=nacs[tp][:, ci, h:h + 1])

                    def s2(k):
                        rbk = 3 + k % 3
                        act(OD[k % 2][:, :], banks[rbk][:, :], AF.Exp, [Bbank[rbk]], [BOD[k % 2]])

                    def s3(k):
                        ci, q4 = divmod(k, 4)
                        sl = ci % 2
                        e = k % 3
                        rb = k % 2
                        g2 = q4 // 2
                        cs = slice(ci * 128, ci * 128 + 128)
                        stt(Gm[rb][:].rearrange("p (h q) -> p h q", h=4), E1[e][:].rearrange("p (h q) -> p h q", h=4), 1.0,
                            cbm[sl][g2][:, :].unsqueeze(1).to_broadcast([128, 4, 128]), ALU.min, ALU.mult,
                            [BE1[e], Bcbm[sl][g2]], [BG[rb]])
                        tt("dve", Cod[rb][:].rearrange("p (h q) -> p h q", h=4), OD[rb][:].rearrange("p (h q) -> p h q", h=4),
                           xcT[:, 10 + g2, cs].unsqueeze(1).to_broadcast([128, 4, 128]), ALU.mult,
                           [BOD[rb], BxcT], [BCod[rb]])

                    def s4(k):
                        ci, q4 = divmod(k, 4)
                        sl = ci % 2
                        rb = k % 2
                        for h4 in range(4):
                            h = q4 * 4 + h4
                            hs = slice(h * 64, (h + 1) * 64)
                            yc0 = rb * 256 + (h4 // 2) * 128
                            yo = banks[6][(h % 2) * 64:(h % 2 + 1) * 64, yc0:yc0 + 128]
                            mm(yo, xc_tok[sl][:, hs], Gm[rb][:, h4 * 128:(h4 + 1) * 128], True, False, [Bxct[sl], BG[rb]], [Bbank[6]])
                            gc = t * 4 + ci
                            mm(yo, prev_b2[gc % 2][:, hs], Cod[rb][:, h4 * 128:(h4 + 1) * 128], False, True,
                               [Bpb2[gc % 2], BCod[rb]], [Bbank[6]])

                    def s5(k):
                        ci, q4 = divmod(k, 4)
                        rb = k % 2
                        cs = slice(ci * 128, ci * 128 + 128)
                        for pi in range(2):
                            pair = q4 * 2 + pi
                            yc0 = rb * 256 + pi * 128
                            stt(yT[:, pair, cs], xcT[:, pair, cs], ppc(l, PP_DSK + pair), banks[6][:, yc0:yc0 + 128],
                                ALU.mult, ALU.add, [BxcT, Bbank[6], Bpp], [ByT])

                    stageA()
                    stageB(0)
                    for k in range(16 + 5):
                        if k < 16:
                            s0(k)
                        if 0 <= k - 1 < 16:
                            s1(k - 1)
                        if 0 <= k - 2 < 16:
                            s2(k - 2)
                        if 0 <= k - 3 < 16:
                            s3(k - 3)
                        if 0 <= k - 4 < 16:
                            s4(k - 4)
                        if 0 <= k - 5 < 16:
                            s5(k - 5)
                        if k % 4 == 3 and k // 4 < 4:
                            c = k // 4
                            stage3a(c)
                            stage3b(c)
                            if c + 1 < 4:
                                stageB(c + 1)
                        if k == 13 and t + 1 < NT:
                            conv(t + 1)
                    tt("pool", yT[:, :, :], yT[:, :, :], zt[:, :, :], ALU.mult, [ByT, Bzt], [ByT])
                    for g2 in range(2):
                        nbk = g2
                        for i in range(4):
                            c = g2 * 4 + i
                            act(sq[c % 2][:, :], yT[:, c, :], AF.Square, [ByT], [Bsq[c % 2]])
                            mm(banks[nbk][:, :], ones_b, sq[c % 2][:, :], i == 0, i == 3, [Bsq[c % 2], Bcst], [Bbank[nbk]])
                        act(rstd[:, g2, :], banks[nbk][:, :], AF.Ln, [Bbank[nbk]], [Brs], bias=EPS, scale=1.0 / 512)
                        act(rstd[:, g2, :], rstd[:, g2, :], AF.Exp, [Brs], [Brs], scale=-0.5)
                    for c in range(8):
                        stt(yT[:, c, :], yT[:, c, :], ppc(l, PP_NW + c), rstd[:, c // 4, :], ALU.mult, ALU.mult,
                            [ByT, Brs, Bpp], [ByT])
                    S.dma("sp", yn_d[t], yT[:], [ByT], [Byn[t]])
            S.barrier()

        def phase_merge(l):
            TT = 256
            with ExitStack() as ps:
                wa = sb("mg_wa", [128, 3, 1024], BF16, ps)
                Wg0 = sb("mg_wg0", [128, 8, 1024], BF16, ps)
                Wg2 = sb("mg_wg2", [128, 8, 1024], BF16, ps)
                wc = sb("mg_wc", [128, 8, 1024], BF16, ps)
                wo = sb("mg_wo", [128, 8, 1024], BF16, ps)
                Bwa, BWg0, BWg2, Bwc, Bwo = [Buf() for _ in range(5)]
                w_l = w_in[l].rearrange("(c p) f -> p c f", p=128)
                wload(wa[:], w_a[l].rearrange("(c p) f -> p c f", p=128), Bwa)
                wload(Wg0[:], w_l[:, :, G0:G0 + 1024], BWg0)
                wload(wc[:], w_c[l].rearrange("(c p) f -> p c f", p=128), Bwc)
                wload(Wg2[:], w_l[:, :, G0 + 2048:G0 + 3072], BWg2)
                wload(wo[:], w_o[l].rearrange("(c p) f -> p c f", p=128), Bwo)
                ao = [sb("mg_ao%d" % i, [128, 3, TT], BF16, ps) for i in range(2)]
                yn = [sb("mg_yn%d" % i, [128, 8, TT], BF16, ps) for i in range(2)]
                mc = [sb("mg_mc%d" % i, [128, 8, TT], BF16, ps) for i in range(2)]
                xt = [sb("mg_xt%d" % i, [128, 8, TT], F32, ps) for i in range(2)]
                Bao_t, Byn_t, Bmc_t, Bxt = [[Buf(), Buf()] for _ in range(4)]
                mg = sb("mg_mg", [128, 8, TT], BF16, ps)
                Bmg = Buf()
                gs = [sb("mg_gs%d" % i, [128, TT], F32, ps) for i in range(4)]
                Bgs = [Buf() for _ in range(4)]
                t1 = [sb("mg_t1%d" % i, [128, TT], F32, ps) for i in range(2)]
                t2 = [sb("mg_t2%d" % i, [128, TT], F32, ps) for i in range(2)]
                Bt1, Bt2 = [Buf(), Buf()], [Buf(), Buf()]
                sq = [sb("mg_sq%d" % i, [128, 512], BF16, ps) for i in range(2)]
                Bsq = [Buf(), Buf()]
                rs = sb("mg_rs", [128, 512], F32, ps)
                Brs = Buf()
                xsrc = xT_v if l == 0 else xr_v
                ao_v = ao_d.rearrange("j p t -> p j t")

                def nb6():
                    bk = rr["bank"] % 6
                    rr["bank"] += 1
                    return bk

                def loads(tt_):
                    p = tt_ % 2
                    a0 = tt_ * TT
                    t5, off = divmod(a0, 512)
                    S.dma("sp", ao[p][:], ao_v[:, :, a0:a0 + TT], [Bao], [Bao_t[p]])
                    S.dma("sp", yn[p][:], yn_d[t5][:, :, off:off + TT], [Byn[t5]], [Byn_t[p]])
                    S.dma("sp", mc[p][:], mac_d[t5][:, :, off:off + TT], [Bmac[t5]], [Bmc_t[p]])
                    S.dma("sp", xt[p][:], xsrc[:, :, a0:a0 + TT], [Bxr[tt_]], [Bxt[p]])

                ntt = SEQ // TT
                loads(0)
                for tt_ in range(ntt):
                    p = tt_ % 2
                    a0 = tt_ * TT
                    if tt_ + 1 < ntt:
                        loads(tt_ + 1)
                    uRt = uR(a0, a0 + TT)
                    for oc in range(8):
                        osl = slice(oc * 128, (oc + 1) * 128)
                        i2 = oc % 2
                        bka = nb6()
                        for kc in range(3):
                            mm(banks[bka][:, 0:TT], wa[:, kc, osl], ao[p][:, kc, :], kc == 0, kc == 2, [Bwa, Bao_t[p]], [Bbank[bka]])
                        bkg = nb6()
                        for kc in range(8):
                            mm(banks[bkg][:, 0:TT], Wg0[:, kc, osl], uT[:, kc, a0:a0 + TT], kc == 0, kc == 7, [BWg0] + uRt, [Bbank[bkg]])
                        act(gs[i2][:, :], banks[bkg][:, 0:TT], AF.Sigmoid, [Bbank[bkg], Bpp], [Bgs[i2]], bias=ppc(l, PP_BG + oc))
                        tt("dve", t1[i2][:, :], banks[bka][:, 0:TT], gs[i2][:, :], ALU.mult, [Bbank[bka], Bgs[i2]], [Bt1[i2]])
                        bkc = nb6()
                        for kc in range(8):
                            mm(banks[bkc][:, 0:TT], wc[:, kc, osl], yn[p][:, kc, :], kc == 0, kc == 7, [Bwc, Byn_t[p]], [Bbank[bkc]])
                        bkg2 = nb6()
                        for kc in range(8):
                            mm(banks[bkg2][:, 0:TT], Wg2[:, kc, osl], uT[:, kc, a0:a0 + TT], kc == 0, kc == 7, [BWg2] + uRt, [Bbank[bkg2]])
                        act(gs[2 + i2][:, :], banks[bkg2][:, 0:TT], AF.Sigmoid, [Bbank[bkg2], Bpp], [Bgs[2 + i2]],
                            bias=ppc(l, PP_BG + 16 + oc))
                        tt("dve", t2[i2][:, :], banks[bkc][:, 0:TT], gs[2 + i2][:, :], ALU.mult, [Bbank[bkc], Bgs[2 + i2]], [Bt2[i2]])
                        tt("pool", t1[i2][:, :], t1[i2][:, :], mc[p][:, oc, :], ALU.add, [Bt1[i2], Bmc_t[p]], [Bt1[i2]])
                        tt("pool", mg[:, oc, :], t1[i2][:, :], t2[i2][:, :], ALU.add, [Bt1[i2], Bt2[i2]], [Bmg])
                    for oc in range(8):
                        osl = slice(oc * 128, (oc + 1) * 128)
                        bk = nb6()
                        for kc in range(8):
                            mm(banks[bk][:, 0:TT], wo[:, kc, osl], mg[:, kc, :], kc == 0, kc == 7, [Bwo, Bmg], [Bbank[bk]])
                        tt("dve", xt[p][:, oc, :], banks[bk][:, 0:TT], xt[p][:, oc, :], ALU.add, [Bbank[bk], Bxt[p]], [Bxt[p]])
                    S.dma("sp", xr_v[:, :, a0:a0 + TT], xt[p][:], [Bxt[p]], [Bxr[tt_]])
                    norm_tile(None, xt[p], Bxt[p], PP_LN2, l, lambda c: uT[:, c, a0:a0 + TT], uRt, TT,
                              banks[6], Bbank[6], sq, Bsq, rs, Brs)
            S.barrier()

        def phase_ffn_up(l):
            with ExitStack() as ps:
                NG = 6
                wu = [sb("fu_w%d" % i, [128, 8, 2, 512], BF16, ps) for i in range(2)]
                Bwu = [Buf(), Buf()]
                raw = [[sb("fu_raw%d%d" % (s_, i), [128, 514], F32, ps) for i in range(2)] for s_ in range(2)]
                Braw = [[Buf(), Buf()], [Buf(), Buf()]]
                Bhal = [[Buf(), Buf()], [Buf(), Buf()]]
                ct = [[sb("fu_ct%d%d" % (s_, i), [128, 512], F32, ps) for i in range(2)] for s_ in range(2)]
                Bct = [[Buf(), Buf()], [Buf(), Buf()]]
                sa = [sb("fu_sa%d" % i, [128, 512], F32, ps) for i in range(2)]
                Bsa = [Buf(), Buf()]
                ab = [sb("fu_ab%d" % i, [128, 512], BF16, ps) for i in range(4)]
                Bab = [Buf() for _ in range(4)]
                wu_l = w_up[l].rearrange("(c p) f -> p c f", p=128)

                def load_g(gi):
                    sl = gi % 2
                    n = min(4, 22 - gi * 4) * 128
                    wload(wu[sl][:, :, 0, 0:n], wu_l[:, :, gi * 512:gi * 512 + n], Bwu[sl])
                    wload(wu[sl][:, :, 1, 0:n], wu_l[:, :, DFF + gi * 512:DFF + gi * 512 + n], Bwu[sl])

                pend = {"v": None, "k": 0}

                def finish(t, j, par):
                    act(sa[par][:, :], ct[0][par][:, :], AF.Silu, [Bct[0][par]], [Bsa[par]])
                    ai = pend["k"] % 4
                    pend["k"] += 1
                    tt("pool", ab[ai][:, :], sa[par][:, :], ct[1][par][:, :], ALU.mult, [Bsa[par], Bct[1][par]], [Bab[ai]])
                    S.dma("sp", act_d[t, :, j, :], ab[ai][:, :], [Bab[ai]], [Bact[t]])

                load_g(0)
                for gi in range(NG):
                    sl = gi % 2
                    if gi + 1 < NG:
                        load_g(gi + 1)
                    for jj in range(min(4, 22 - gi * 4)):
                        j = gi * 4 + jj
                        for s_ in range(2):
                            memset("dve", raw[s_][0][:, 0:2], 0.0, [Bhal[s_][0]])
                        for t in range(NT):
                            t0 = t * 512
                            par = t % 2
                            uRt = uR(t0, t0 + 512)
                            for s_ in range(2):
                                bk = rr["bank"] % 4
                                rr["bank"] += 1
                                ch = j if s_ == 0 else 22 + j
                                for kc in range(8):
                                    mm(banks[bk][:, :], wu[sl][:, kc, s_, jj * 128:(jj + 1) * 128], uT[:, kc, t0:t0 + 512],
                                       kc == 0, kc == 7, [Bwu[sl]] + uRt, [Bbank[bk]])
                                tcopy("act", raw[s_][par][:, 2:514], banks[bk][:, :], [Bbank[bk]], [Braw[s_][par]])
                                tcopy("act", raw[s_][1 - par][:, 0:2], banks[bk][:, 510:512], [Bbank[bk]], [Bhal[s_][1 - par]])
                                wcol = PP_FCW + ch * 3
                                if FFN_ACT_TAP:
                                    act(ct[s_][par][:, :], banks[bk][:, :], AF.Identity, [Bbank[bk], Bpp], [Bct[s_][par]],
                                        bias=ppc(l, PP_FCB + ch), scale=ppc(l, wcol + 2))
                                else:
                                    ts("dve", ct[s_][par][:, :], raw[s_][par][:, 2:514], ppc(l, wcol + 2), ppc(l, PP_FCB + ch),
                                       ALU.mult, ALU.add, [Braw[s_][par], Bpp], [Bct[s_][par]])
                                for kk in (1, 0):
                                    stt(ct[s_][par][:, :], raw[s_][par][:, kk:kk + 512], ppc(l, wcol + kk), ct[s_][par][:, :],
                                        ALU.mult, ALU.add, [Braw[s_][par], Bhal[s_][par], Bct[s_][par], Bpp], [Bct[s_][par]])
                            if pend["v"] is not None:
                                finish(*pend["v"])
                            pend["v"] = (t, j, par)
                finish(*pend["v"])
            S.barrier()

        def phase_ffn_down(l, last):
            TT = 256
            with ExitStack() as ps:
                wd = sb("fd_w", [128, 22, 1024], BF16, ps)
                Bwd = Buf()
                wload(wd[:, 0:11, :], w_down[l, 0:1408].rearrange("(c p) f -> p c f", p=128), Bwd)
                wload(wd[:, 11:22, :], w_down[l, 1408:2816].rearrange("(c p) f -> p c f", p=128), Bwd)
                at = [sb("fd_at%d" % i, [128, 22, 512], BF16, ps) for i in range(2)]
                Bat = [Buf(), Buf()]
                xt = [sb("fd_xt%d" % i, [128, 8, TT], F32, ps) for i in range(2)]
                Bxt = [Buf(), Buf()]
                ot = [sb("fd_ot%d" % i, [128, 8, TT], F32, ps) for i in range(2)] if last else None
                Bot = [Buf(), Buf()]
                sq = [sb("fd_sq%d" % i, [128, 512], BF16, ps) for i in range(2)]
                Bsq = [Buf(), Buf()]
                rs = sb("fd_rs", [128, 512], F32, ps)
                Brs = Buf()
                ntt = SEQ // TT

                def load_a(t):
                    S.dma("sp", at[t % 2][:], act_d[t], [Bact[t]], [Bat[t % 2]])

                def load_x(tt_):
                    a0 = tt_ * TT
                    S.dma("sp", xt[tt_ % 2][:], xr_v[:, :, a0:a0 + TT], [Bxr[tt_]], [Bxt[tt_ % 2]])

                load_a(0)
                load_x(0)
                for tt_ in range(ntt):
                    p = tt_ % 2
                    a0 = tt_ * TT
                    t5, off = divmod(a0, 512)
                    if off == 0 and t5 + 1 < NT:
                        load_a(t5 + 1)
                    if tt_ + 1 < ntt:
                        load_x(tt_ + 1)
                    for oc in range(8):
                        bk = rr["bank"] % 6
                        rr["bank"] += 1
                        for kc in range(22):
                            mm(banks[bk][:, 0:TT], wd[:, kc, oc * 128:(oc + 1) * 128], at[t5 % 2][:, kc, off:off + TT],
                               kc == 0, kc == 21, [Bwd, Bat[t5 % 2]], [Bbank[bk]])
                        tt("dve", xt[p][:, oc, :], banks[bk][:, 0:TT], xt[p][:, oc, :], ALU.add, [Bbank[bk], Bxt[p]], [Bxt[p]])
                    uRt = uR(a0, a0 + TT)
                    if not last:
                        S.dma("sp", xr_v[:, :, a0:a0 + TT], xt[p][:], [Bxt[p]], [Bxr[tt_]])
                        norm_tile(None, xt[p], Bxt[p], PP_LN1, l + 1, lambda c: uT[:, c, a0:a0 + TT], uRt, TT,
                                  banks[6], Bbank[6], sq, Bsq, rs, Brs)
                    else:
                        if debug:
                            S.dma("sp", xr_v[:, :, a0:a0 + TT], xt[p][:], [Bxt[p]], [Bxr[tt_]])
                        norm_tile(None, xt[p], Bxt[p], PP_FG, l, lambda c: ot[p][:, c, :], [Bot[p]], TT,
                                  banks[6], Bbank[6], sq, Bsq, rs, Brs)
                        S.dma("sp", outT_v[:, :, a0:a0 + TT], ot[p][:], [Bot[p]], [Bout])
            S.barrier()

        phase_norm0()
        for l in range(depth):
            for name, fn in (("attn", lambda: phase_attn(l)), ("poolz", lambda: phase_poolz(l)),
                             ("ssd", lambda: phase_ssd(l)), ("merge", lambda: phase_merge(l)),
                             ("ffn_up", lambda: phase_ffn_up(l)),
                             ("ffn_down", lambda: phase_ffn_down(l, l == depth - 1))):
                if stop["flag"]:
                    break
                if name not in SKIP:
                    fn()
                done(l, name)
            if stop["flag"]:
                break
        if stop["flag"]:
            with ExitStack() as ps:
                z = sb("dbg_z", [128, 8, 512], F32, ps)
                Bz_ = Buf()
                memset("dve", z[:], 0.0, [Bz_])
                for t in range(NT):
                    S.dma("sp", outT_v[:, :, t * 512:(t + 1) * 512], z[:], [Bz_], [Bout])

        S.emit(lambda name: es.enter_context(nc.semaphore(name)))
    nc._sched_stats = {e: len(S.streams[e]) for e in ENGS}
    nc._sched_stats["waits"] = S.nwait
    return nc


def _t5_bucket(dist):
    dist = np.asarray(dist, dtype=np.int64)
    max_exact = 16
    nf = np.maximum(dist, 1).astype(np.float32)
    large = max_exact + (np.log(nf / np.float32(max_exact)) / np.float32(math.log(2048 / max_exact))
                         * np.float32(32 - max_exact)).astype(np.int32)
    large = np.minimum(large, 31)
    return np.where(dist < max_exact, dist, large)


def _bias_tables(rel_bias):
    out = np.full((128, 18, 2, 128), NEG, dtype=np.float32)
    j = np.arange(128)[:, None]
    i = np.arange(128)[None, :]
    for g, (win, d) in enumerate(GROUPS):
        rel_c = i - j
        rel_p = i + 128 - j
        for hh in range(6):
            h = g * 6 + hh
            bc = rel_bias[_t5_bucket(np.clip(rel_c, 0, None) * d), h]
            out[:, h, 0, :] = np.where(rel_c >= 0, bc, NEG)
            bp = rel_bias[_t5_bucket(np.clip(rel_p, 0, None) * d), h]
            out[:, h, 1, :] = np.where(rel_p <= 128, bp, NEG)
    return out.reshape(128, 18, 256)


def _constants():
    c = np.zeros((128, NCST), dtype=np.float32)
    c[:, C_ID:C_ID + 128] = np.eye(128, dtype=np.float32)
    c[:, C_ONE:C_ONE + 128] = 1.0
    a = np.arange(128)
    c[:, C_TRI:C_TRI + 128] = (a[:, None] <= a[None, :]).astype(np.float32)
    tp = a[:, None]
    t = a[None, :]
    for gi, w in enumerate(POOLW):
        cur = ((t - tp >= 0) & (t - tp < w)).astype(np.float32) / w - (t == tp).astype(np.float32)
        prev = ((t + 128 - tp) < w).astype(np.float32) / w
        cnt = np.minimum(t + 1, w).astype(np.float32)
        cur0 = ((t - tp >= 0) & (t - tp < w)).astype(np.float32) / cnt - (t == tp).astype(np.float32)
        for kind, m in enumerate((cur, prev, cur0)):
            o = C_BAND + (kind * 4 + gi) * 128
            c[:, o:o + 128] = m
    return c


def _pack_params(inp, depth):
    pp = np.zeros((128, depth, NPP), dtype=np.float32)

    def fm(v, n):
        return np.asarray(v, dtype=np.float32).reshape(n, 128).T

    for l in range(depth):
        pp[:, l, PP_LN1:PP_LN1 + 8] = fm(inp["ln1_g"][l], 8)
        pp[:, l, PP_LN2:PP_LN2 + 8] = fm(inp["ln2_g"][l], 8)
        pp[:, l, PP_BG:PP_BG + 24] = fm(inp["b_gate"][l], 24)
        pp[:, l, PP_PSC:PP_PSC + 8] = fm(inp["pool_scale"][l], 8)
        cw = np.asarray(inp["ssd_conv_w"][l], dtype=np.float32)
        pp[:, l, PP_SCW:PP_SCW + 48] = cw.reshape(4, 12, 128).transpose(2, 1, 0).reshape(128, 48)
        pp[:, l, PP_SCB:PP_SCB + 12] = fm(inp["ssd_conv_b"][l], 12)
        pp[:, l, PP_NW:PP_NW + 8] = fm(inp["ssd_norm_w"][l], 8)
        pp[:, l, PP_DSK:PP_DSK + 8] = fm(np.repeat(np.asarray(inp["ssd_d"][l], dtype=np.float32), 64), 8)
        fw = np.asarray(inp["ffn_conv_w"][l], dtype=np.float32)
        pp[:, l, PP_FCW:PP_FCW + 132] = fw.reshape(3, 44, 128).transpose(2, 1, 0).reshape(128, 132)
        pp[:, l, PP_FCB:PP_FCB + 44] = fm(inp["ffn_conv_b"][l], 44)
        pp[:, l, PP_DTB:PP_DTB + 16] = np.asarray(inp["ssd_dt_bias"][l], dtype=np.float32)[None, :]
        pp[:, l, PP_ALOG:PP_ALOG + 16] = np.asarray(inp["ssd_a_log"][l], dtype=np.float32)[None, :]
        pp[:, l, PP_FG:PP_FG + 8] = fm(inp["final_g"], 8)
    return pp


_CACHE = {}


def make_in_maps(inp, depth, cores):
    f = lambda a: np.ascontiguousarray(np.asarray(a, dtype=np.float32))
    shared = {
        "w_in": f(inp["w_in"][:depth]), "w_a": f(inp["w_a"][:depth]), "pool_w": f(inp["pool_w"][:depth]),
        "w_b": f(inp["w_b"][:depth]), "w_c": f(inp["w_c"][:depth]), "w_o": f(inp["w_o"][:depth]),
        "w_up": f(inp["ffn_w_up"][:depth]), "w_down": f(inp["ffn_w_down"][:depth]),
        "pp": _pack_params(inp, depth), "cst": _constants(), "biasT": _bias_tables(f(inp["rel_bias"])),
    }
    x = np.asarray(inp["x"], dtype=np.float32)
    maps = []
    for b in cores:
        m = dict(shared)
        m["xT"] = np.ascontiguousarray(x[b].T)
        maps.append(m)
    return maps


def kernel(**inputs):
    if "nc" not in _CACHE:
        _CACHE["nc"] = build_program(DEPTH)
    nc = _CACHE["nc"]
    in_maps = make_in_maps(inputs, DEPTH, list(range(NCORES)))
    res = run_bass_kernel_spmd(nc, in_maps, core_ids=list(range(NCORES)))
    out = np.stack([np.ascontiguousarray(r["outT"].T) for r in res.results], axis=0)
    return out.astype(np.float32)
```

```python
import math
import numpy as np
from contextlib import ExitStack
import concourse.bass as bass
import concourse.mybir as mybir
from concourse.bass_utils import run_bass_kernel_spmd

F32 = mybir.dt.float32
BF16 = mybir.dt.bfloat16
AF = mybir.ActivationFunctionType
ALU = mybir.AluOpType

D = 1024
SEQ = 4096
DEPTH = 4
NCORES = 8
IN_W = 10128
Q0, K0, V0, P0, Z0, X0, DT0, G0 = 0, 1152, 2304, 3456, 4480, 5504, 7040, 7056
DFF = 2816
EPS = 1e-6
GROUPS = ((128, 1), (512, 4), (2048, 16))
POOLW = (2, 4, 8, 16)
NEG = -30000.0
import os
SSD_LEVEL = int(os.environ.get('SSD_LEVEL', '9'))
SKIP = os.environ.get('SKIP', '').split(',')
FFN_ACT_TAP = int(os.environ.get('FFN_ACT_TAP', '1'))
OPLIMIT = int(os.environ.get('OPLIMIT', '1000000000'))

PP_LN1, PP_LN2, PP_BG, PP_PSC, PP_SCW, PP_SCB, PP_NW, PP_DSK, PP_FCW, PP_FCB, PP_DTB, PP_ALOG, PP_FG = (
    0, 8, 16, 40, 48, 96, 108, 116, 124, 256, 300, 316, 332)
NPP = 340
C_ID, C_ONE, C_TRI, C_BAND = 0, 128, 256, 384
NCST = 384 + 12 * 128

ENGS = ("pe", "act", "dve", "pool", "sp")


class Buf:
    __slots__ = ("name", "w", "rs")

    def __init__(self, name=""):
        self.name = name
        self.w = None
        self.rs = []


class Op:
    __slots__ = ("eng", "idx", "fn", "deps", "dma", "sem", "val", "inc", "waits", "presem")

    def __init__(self, eng, idx, fn, dma):
        self.eng = eng
        self.idx = idx
        self.fn = fn
        self.deps = {}
        self.dma = dma
        self.sem = None
        self.val = 0
        self.inc = False
        self.waits = []
        self.presem = None


class Sched:
    def __init__(self, nc, n_dma_sems=40, raw_gap=3):
        self.nc = nc
        self.streams = {e: [] for e in ENGS}
        self.n_dma_sems = n_dma_sems
        self.raw_gap = raw_gap
        self.bar = None
        self.bar_seen = set()
        self.region = False
        self.rcount = 0

    def _add_dep(self, op, d, raw):
        if d is None or d is op:
            return
        if d.eng == op.eng and not d.dma:
            if op.eng == "pe" or not raw or op.dma:
                return
            if op.idx - d.idx >= self.raw_gap:
                return
        if d.dma:
            op.deps[("dma", d.eng, d.idx)] = d
        else:
            cur = op.deps.get(d.eng)
            if cur is None or cur.idx < d.idx:
                op.deps[d.eng] = d

    def barrier(self):
        self.bar = [st[-1] for st in self.streams.values() if st]
        self.bar_seen = set()

    def op(self, eng, fn, reads=(), writes=(), dma=False):
        if self.region:
            self.rcount += 1
            if self.rcount > OPLIMIT:
                return None
        st = self.streams[eng]
        o = Op(eng, len(st), fn, dma)
        if self.bar is not None and eng not in self.bar_seen:
            self.bar_seen.add(eng)
            for d in self.bar:
                if d.eng != eng or d.dma:
                    self._add_dep(o, d, False)
        for b in reads:
            self._add_dep(o, b.w, True)
        for b in writes:
            self._add_dep(o, b.w, False)
            for r in b.rs:
                self._add_dep(o, r, False)
        for b in writes:
            b.w = o
            b.rs = []
        for b in reads:
            if b.w is not o:
                b.rs.append(o)
        st.append(o)
        return o

    def dma(self, eng, out, in_, reads=(), writes=(), **kw):
        return self.op(eng, lambda e: e.dma_start(out=out, in_=in_, **kw), reads, writes, dma=True)

    def emit(self, sem_ctx):
        nc = self.nc
        esem = {e: sem_ctx("c_" + e) for e in ENGS if e != "sp"}
        dsem = {}
        for e in ENGS:
            if any(o.dma for o in self.streams[e]):
                dsem[e] = [sem_ctx("d_%s_%d" % (e, i)) for i in range(self.n_dma_sems)]
        for e in ENGS:
            for o in self.streams[e]:
                for d in o.deps.values():
                    if not d.dma:
                        d.inc = True
        for e in ENGS:
            cnt = 0
            dcnt = [0] * self.n_dma_sems
            k = 0
            for o in self.streams[e]:
                if o.dma:
                    s = k % self.n_dma_sems
                    k += 1
                    o.presem = (dsem[e][s], dcnt[s])
                    dcnt[s] += 16
                    o.sem = dsem[e][s]
                    o.val = dcnt[s]
                elif o.inc:
                    cnt += 1
                    o.sem = esem[e]
                    o.val = cnt
        nwait = 0
        for e in ENGS:
            known = {}
            for o in self.streams[e]:
                ws = []
                if o.dma and o.presem[1] > 0:
                    s, v = o.presem
                    if known.get(id(s), 0) < v:
                        known[id(s)] = v
                        ws.append((s, v))
                for d in o.deps.values():
                    if known.get(id(d.sem), 0) < d.val:
                        known[id(d.sem)] = d.val
                        ws.append((d.sem, d.val))
                o.waits = ws
                nwait += len(ws)
        self.nwait = nwait

        def run(eng_name, eh):
            final = {}
            for o in self.streams[eng_name]:
                for s, v in o.waits:
                    eh.wait_ge(s, v)
                ins = o.fn(eh)
                if o.dma:
                    ins.then_inc(o.sem, 16)
                    final[id(o.sem)] = (o.sem, o.val)
                elif o.inc:
                    ins.then_inc(o.sem, 1)
            for s, v in final.values():
                eh.wait_ge(s, v)

        with nc.Block() as block:
            if self.streams["pe"]:
                @block.tensor
                def _(eh):
                    run("pe", eh)
            if self.streams["act"]:
                @block.scalar
                def _(eh):
                    run("act", eh)
            if self.streams["dve"]:
                @block.vector
                def _(eh):
                    run("dve", eh)
            if self.streams["pool"]:
                @block.gpsimd
                def _(eh):
                    run("pool", eh)
            if self.streams["sp"]:
                @block.sync
                def _(eh):
                    run("sp", eh)


def build_program(depth=DEPTH, debug=False, upto=None):
    nc = bass.Bass("TRN2", target_bir_lowering=False)
    S = Sched(nc)
    NT = SEQ // 512
    dbg_kind = "ExternalOutput" if debug else "Internal"

    def dram_in(name, shape, dt=F32):
        return nc.dram_tensor(name, list(shape), dt, kind="ExternalInput").ap()

    xT = dram_in("xT", [D, SEQ])
    w_in = dram_in("w_in", [depth, D, IN_W])
    w_a = dram_in("w_a", [depth, 384, D])
    pool_w = dram_in("pool_w", [depth, 4, 256, 256])
    w_b = dram_in("w_b", [depth, D, D])
    w_c = dram_in("w_c", [depth, D, D])
    w_o = dram_in("w_o", [depth, D, D])
    w_up = dram_in("w_up", [depth, D, 2 * DFF])
    w_down = dram_in("w_down", [depth, DFF, D])
    pp_d = dram_in("pp", [128, depth, NPP])
    cst_d = dram_in("cst", [128, NCST])
    bias_d = dram_in("biasT", [128, 18, 256])
    outT = nc.dram_tensor("outT", [D, SEQ], F32, kind="ExternalOutput").ap()

    xr = nc.dram_tensor("xr", [D, SEQ], F32, kind=dbg_kind).ap()
    ao_d = nc.dram_tensor("ao_d", [3, 128, SEQ], BF16, kind=dbg_kind).ap()
    zs_d = nc.dram_tensor("zs_d", [NT, 128, 8, 512], BF16, kind=dbg_kind).ap()
    mac_d = nc.dram_tensor("mac_d", [NT, 128, 8, 512], BF16, kind=dbg_kind).ap()
    yn_d = nc.dram_tensor("yn_d", [NT, 128, 8, 512], BF16, kind=dbg_kind).ap()
    act_d = nc.dram_tensor("act_d", [NT, 128, 22, 512], BF16, kind=dbg_kind).ap()

    xT_v = xT.rearrange("(c p) t -> p c t", p=128)
    xr_v = xr.rearrange("(c p) t -> p c t", p=128)
    outT_v = outT.rearrange("(c p) t -> p c t", p=128)

    Bxr = [Buf() for _ in range(16)]
    Bao = Buf()
    Bzs = [Buf() for _ in range(NT)]
    Bmac = [Buf() for _ in range(NT)]
    Byn = [Buf() for _ in range(NT)]
    Bact = [Buf() for _ in range(NT)]
    Bout = Buf()

    es = ExitStack()
    with es:
        uid = {"n": 0}

        def sb(name, shape, dt, stack=es):
            uid["n"] += 1
            return stack.enter_context(nc.sbuf_tensor("s%d_%s" % (uid["n"], name), list(shape), dt))

        uT = sb("uT", [128, 8, SEQ], BF16)
        BuT = [Buf() for _ in range(16)]
        pp = sb("pp", [128, depth, NPP], F32)
        Bpp = Buf()
        cstf = sb("cstf", [128, 256], F32)
        cstb = sb("cstb", [128, NCST], BF16)
        Bcst = Buf()
        banks = [es.enter_context(nc.psum_tensor("bank%d" % i, [128, 512], F32)) for i in range(8)]
        Bbank = [Buf() for _ in range(8)]

        def uR(t0, t1):
            return BuT[t0 // 256:(t1 + 255) // 256]

        def mm(out, lhsT, rhs, start, stop, R, W):
            S.op("pe", lambda e: e.matmul(out, lhsT=lhsT, rhs=rhs, start=start, stop=stop), R, W)

        def transp(out, in_, R, W):
            S.op("pe", lambda e: e.transpose(out, in_, cstb[:, C_ID:C_ID + 128]), R, W)

        def act(out, in_, func, R, W, bias=None, scale=None):
            kw = {}
            if bias is not None:
                kw["bias"] = bias
            if scale is not None:
                kw["scale"] = scale
            S.op("act", lambda e: e.activation(out=out, in_=in_, func=func, **kw), R, W)

        def tcopy(eng, out, in_, R, W):
            if eng == "act":
                act(out, in_, AF.Copy, R, W)
            else:
                S.op(eng, lambda e: e.tensor_copy(out=out, in_=in_), R, W)

        def tt(eng, out, in0, in1, op, R, W):
            S.op(eng, lambda e: e.tensor_tensor(out=out, in0=in0, in1=in1, op=op), R, W)

        def ts(eng, out, in0, s1, s2, op0, op1, R, W):
            if op1 is None:
                S.op(eng, lambda e: e.tensor_scalar(out=out, in0=in0, scalar1=s1, scalar2=None, op0=op0), R, W)
            else:
                S.op(eng, lambda e: e.tensor_scalar(out=out, in0=in0, scalar1=s1, scalar2=s2, op0=op0, op1=op1), R, W)

        def stt(out, in0, scalar, in1, op0, op1, R, W):
            S.op("dve", lambda e: e.scalar_tensor_tensor(out=out, in0=in0, scalar=scalar, in1=in1, op0=op0, op1=op1), R, W)

        def memset(eng, ap, val, W):
            S.op(eng, lambda e: e.memset(ap, val), (), W)

        def wload(dst, src, B):
            S.dma("pool", dst, src, (), [B])

        rr = {"ev": 0, "bank": 0}

        def ev_eng():
            rr["ev"] += 1
            return "act" if rr["ev"] % 2 else "dve"

        S.dma("sp", pp[:], pp_d, (), [Bpp])
        S.dma("sp", cstf[:], cst_d[:, C_ONE:C_ONE + 256], (), [Bcst])
        S.dma("pool", cstb[:], cst_d, (), [Bcst])
        ident = cstb[:, C_ID:C_ID + 128]
        ones_b = cstb[:, C_ONE:C_ONE + 128]
        ones_f = cstf[:, 0:128]
        tri_f = cstf[:, 128:256]

        def band(kind, gi):
            o = C_BAND + (kind * 4 + gi) * 128
            return cstb[:, o:o + 128]

        def ppc(l, col, n=1):
            return pp[:, l, col:col + n]

        def norm_tile(st, xt, Bx, gcol0, l, dst_fn, Bdst, TT, nb, Bnb, sq, Bsq, rs, Brs):
            for c in range(8):
                act(sq[c % 2][:, 0:TT], xt[:, c, :], AF.Square, [Bx], [Bsq[c % 2]])
                mm(nb[:, 0:TT], ones_b, sq[c % 2][:, 0:TT], c == 0, c == 7, [Bsq[c % 2], Bcst], [Bnb])
            act(rs[:, 0:TT], nb[:, 0:TT], AF.Ln, [Bnb], [Brs], bias=EPS, scale=1.0 / D)
            act(rs[:, 0:TT], rs[:, 0:TT], AF.Exp, [Brs], [Brs], scale=-0.5)
            for c in range(8):
                stt(dst_fn(c), xt[:, c, :], ppc(l, gcol0 + c), rs[:, 0:TT], ALU.mult, ALU.mult,
                    [Bx, Brs, Bpp], Bdst)

        stop = {"flag": False}

        def done(l, name):
            if upto is not None and (l, name) == tuple(upto):
                stop["flag"] = True

        def phase_norm0():
            with ExitStack() as ps:
                xt = [sb("n0_xt%d" % i, [128, 8, 512], F32, ps) for i in range(2)]
                Bxt = [Buf() for _ in range(2)]
                sq = [sb("n0_sq%d" % i, [128, 512], BF16, ps) for i in range(2)]
                Bsq = [Buf(), Buf()]
                rs = sb("n0_rs", [128, 512], F32, ps)
                Brs = Buf()
                for t in range(NT):
                    t0 = t * 512
                    S.dma("sp", xt[t % 2][:], xT_v[:, :, t0:t0 + 512], (), [Bxt[t % 2]])
                    norm_tile(None, xt[t % 2], Bxt[t % 2], PP_LN1, 0,
                              lambda c: uT[:, c, t0:t0 + 512], uR(t0, t0 + 512), 512,
                              banks[0], Bbank[0], sq, Bsq, rs, Brs)
            S.barrier()

        def phase_attn(l):
            with ExitStack() as ps:
                bias_sb = sb("at_bias", [128, 18, 256], F32, ps)
                Bbias = Buf()
                S.dma("sp", bias_sb[:], bias_d, (), [Bbias])
                acc = sb("at_acc", [128, 2, SEQ], F32, ps)
                Bacc = Buf()
                qT = [sb("at_q%d" % i, [128, SEQ], BF16, ps) for i in range(2)]
                kT = [sb("at_k%d" % i, [128, SEQ], BF16, ps) for i in range(2)]
                vS = [sb("at_v%d" % i, [128, 32, 128], BF16, ps) for i in range(2)]
                wq = [sb("at_w%d" % i, [128, 8, 3, 128], BF16, ps) for i in range(2)]
                Bq = [Buf(), Buf()]
                Bk = [Buf(), Buf()]
                Bv = [Buf(), Buf()]
                Bw = [Buf(), Buf()]
                tmp = [sb("at_tmp%d" % i, [128, 256], F32, ps) for i in range(4)]
                pT = [sb("at_p%d" % i, [128, 256], BF16, ps) for i in range(4)]
                Btmp = [Buf() for _ in range(4)]
                BpT = [Buf() for _ in range(4)]
                osb = sb("at_o", [128, SEQ], BF16, ps)
                Bosb = Buf()
                w_l = w_in[l].rearrange("(c p) f -> p c f", p=128)
                it = 0
                combos = [(j, g) for j in range(3) for g in range(3)]

                def load_w(idx):
                    j, g = combos[idx]
                    sl = idx % 2
                    fo = (g * 6 + 2 * j) * 64
                    for wi, base in enumerate((Q0, K0, V0)):
                        wload(wq[sl][:, :, wi, :], w_l[:, :, base + fo:base + fo + 128], Bw[sl])

                load_w(0)
                for idx, (j, g) in enumerate(combos):
                    sl = idx % 2
                    if idx + 1 < len(combos):
                        load_w(idx + 1)
                    win, d = GROUPS[g]
                    L = SEQ // d
                    nbr = L // 128
                    for t in range(NT):
                        t0 = t * 512
                        for wi in range(2):
                            bk = rr["bank"] % 2
                            rr["bank"] += 1
                            for kc in range(8):
                                mm(banks[bk][:, :], wq[sl][:, kc, wi, :], uT[:, kc, t0:t0 + 512],
                                   kc == 0, kc == 7, [Bw[sl]] + uR(t0, t0 + 512), [Bbank[bk]])
                            dst_t = qT[sl] if wi == 0 else kT[sl]
                            Bd = Bq[sl] if wi == 0 else Bk[sl]
                            if d == 1:
                                src = banks[bk][:, :]
                                dst = dst_t[:, t0:t0 + 512]
                            else:
                                src = banks[bk][:, :].rearrange("p (m r) -> p r m", r=d)
                                dst = dst_t[:, :].rearrange("p (r l) -> p r l", r=d)[:, :, t0 // d:(t0 + 512) // d]
                            if wi == 0:
                                act(dst, src, AF.Copy, [Bbank[bk]], [Bd], scale=0.125)
                            else:
                                tcopy("dve", dst, src, [Bbank[bk]], [Bd])
                    uv = None
                    if d > 1:
                        uv = [uT[:, kc, :].rearrange("p (n i r) -> p r n i", r=d, i=128) for kc in range(8)]
                    for b4 in range(8):
                        bk = rr["bank"] % 2
                        rr["bank"] += 1
                        for bb in range(4):
                            b = b4 * 4 + bb
                            r, n = divmod(b, nbr)
                            for kc in range(8):
                                lhs = uT[:, kc, b * 128:(b + 1) * 128] if d == 1 else uv[kc][:, r, n, :]
                                mm(banks[bk][:, bb * 128:(bb + 1) * 128], lhs, wq[sl][:, kc, 2, :],
                                   kc == 0, kc == 7, [Bw[sl]] + BuT, [Bbank[bk]])
                        tcopy(ev_eng(), vS[sl][:, b4 * 4:(b4 + 1) * 4, :],
                              banks[bk][:, :].rearrange("p (b f) -> p b f", b=4), [Bbank[bk]], [Bv[sl]])
                    if d > 1:
                        accv = acc[:, :, :].rearrange("p a (n i r) -> p a r n i", r=d, i=128)
                    def emit_S(b):
                        r, n = divmod(b, nbr)
                        hp = n > 0
                        for hh in range(2):
                            sbk = 2 + (b % 2) * 2 + hh
                            ps_ = slice(hh * 64, (hh + 1) * 64)
                            mm(banks[sbk][:, 0:128], kT[sl][ps_, b * 128:(b + 1) * 128],
                               qT[sl][ps_, b * 128:(b + 1) * 128], True, True, [Bq[sl], Bk[sl]], [Bbank[sbk]])
                            if hp:
                                mm(banks[sbk][:, 128:256], kT[sl][ps_, (b - 1) * 128:b * 128],
                                   qT[sl][ps_, b * 128:(b + 1) * 128], True, True, [Bq[sl], Bk[sl]], [Bbank[sbk]])

                    def emit_add(b):
                        r, n = divmod(b, nbr)
                        ncol = 256 if n > 0 else 128
                        for hh in range(2):
                            sbk = 2 + (b % 2) * 2 + hh
                            bi = (b % 2) * 2 + hh
                            hg = g * 6 + 2 * j + hh
                            tt("dve", tmp[bi][:, 0:ncol], banks[sbk][:, 0:ncol], bias_sb[:, hg, 0:ncol], ALU.add,
                               [Bbank[sbk], Bbias], [Btmp[bi]])

                    def emit_exp(b):
                        r, n = divmod(b, nbr)
                        ncol = 256 if n > 0 else 128
                        for hh in range(2):
                            bi = (b % 2) * 2 + hh
                            act(pT[bi][:, 0:ncol], tmp[bi][:, 0:ncol], AF.Exp, [Btmp[bi]], [BpT[bi]])

                    def emit_PV(b):
                        r, n = divmod(b, nbr)
                        hp = n > 0
                        nzb = 6 + (b % 2)
                        for hh in range(2):
                            bi = (b % 2) * 2 + hh
                            ps_ = slice(hh * 64, (hh + 1) * 64)
                            mm(banks[nzb][ps_, 0:128], vS[sl][:, b, ps_], pT[bi][:, 0:128], True, not hp,
                               [Bv[sl], BpT[bi]], [Bbank[nzb]])
                            if hp:
                                mm(banks[nzb][ps_, 0:128], vS[sl][:, b - 1, ps_], pT[bi][:, 128:256], False, True,
                                   [Bv[sl], BpT[bi]], [Bbank[nzb]])
                            mm(banks[nzb][ps_, 128:256], ones_b[:, 0:64], pT[bi][:, 0:128], True, not hp,
                               [Bcst, BpT[bi]], [Bbank[nzb]])
                            if hp:
                                mm(banks[nzb][ps_, 128:256], ones_b[:, 0:64], pT[bi][:, 128:256], False, True,
                                   [Bcst, BpT[bi]], [Bbank[nzb]])

                    def emit_evac(b):
                        r, n = divmod(b, nbr)
                        nzb = 6 + (b % 2)
                        src = banks[nzb][:, 0:256].rearrange("p (a q) -> p a q", a=2)
                        if d == 1:
                            dst = acc[:, :, b * 128:(b + 1) * 128]
                        else:
                            dst = accv[:, :, r, n, :]
                        if g == 0:
                            tcopy("act", dst, src, [Bbank[nzb]], [Bacc])
                        else:
                            tt("dve", dst, src, dst, ALU.add, [Bbank[nzb], Bacc], [Bacc])

                    for k in range(32 + 4):
                        if k < 32:
                            emit_S(k)
                        if 0 <= k - 1 < 32:
                            emit_add(k - 1)
                        if 0 <= k - 2 < 32:
                            emit_exp(k - 2)
                        if 0 <= k - 3 < 32:
                            emit_PV(k - 3)
                        if 0 <= k - 4 < 32:
                            emit_evac(k - 4)
                    if g == 2:
                        for hf in range(4):
                            cs = slice(hf * 1024, (hf + 1) * 1024)
                            S.op("dve", lambda e, cs=cs: e.reciprocal(out=acc[:, 1, cs], in_=acc[:, 1, cs]), [Bacc], [Bacc])
                            tt("dve", osb[:, cs], acc[:, 0, cs], acc[:, 1, cs], ALU.mult, [Bacc], [Bosb])
                        S.dma("sp", ao_d[j], osb[:], [Bosb], [Bao])
            S.barrier()

        def phase_poolz(l):
            with ExitStack() as ps:
                Wp = sb("pz_wp", [128, 8, 1024], BF16, ps)
                Wz = sb("pz_wz", [128, 8, 1024], BF16, ps)
                pw = sb("pz_pw", [128, 4, 2, 256], BF16, ps)
                wb = sb("pz_wb", [128, 8, 1024], BF16, ps)
                Wg = sb("pz_wg", [128, 8, 1024], BF16, ps)
                BWp, BWz, Bpw, Bwb, BWg = [Buf() for _ in range(5)]
                w_l = w_in[l].rearrange("(c p) f -> p c f", p=128)
                wload(Wp[:], w_l[:, :, P0:P0 + 1024], BWp)
                for g in range(4):
                    wload(pw[:, g, :, :], pool_w[l, g].rearrange("(cc p) d -> p cc d", p=128), Bpw)
                wload(wb[:], w_b[l].rearrange("(c p) f -> p c f", p=128), Bwb)
                wload(Wg[:], w_l[:, :, G0 + 1024:G0 + 2048], BWg)
                wload(Wz[:], w_l[:, :, Z0:Z0 + 1024], BWz)
                pin = [sb("pz_pin%d" % i, [128, 1024], BF16, ps) for i in range(2)]
                Bpin = [Buf(), Buf()]
                dT = sb("pz_dT", [128, 8, 512], BF16, ps)
                BdT = Buf()
                yp = sb("pz_yp", [128, 8, 512], BF16, ps)
                Byp = Buf()
                g1 = [sb("pz_g%d" % i, [128, 512], F32, ps) for i in range(2)]
                Bg1 = [Buf(), Buf()]
                mac = [sb("pz_mac%d" % i, [128, 8, 512], BF16, ps) for i in range(2)]
                Bm = [Buf(), Buf()]
                zst = [sb("pz_zs%d" % i, [128, 8, 512], BF16, ps) for i in range(2)]
                Bz = [Buf(), Buf()]

                def nb4():
                    bk = rr["bank"] % 4
                    rr["bank"] += 1
                    return bk

                for t in range(NT):
                    t0 = t * 512
                    uRt = uR(t0, t0 + 512)
                    for blk in range(4):
                        b = t * 4 + blk
                        tok = b * 128
                        for half in range(2):
                            bk = nb4()
                            for kc in range(8):
                                mm(banks[bk][:, :], uT[:, kc, tok:tok + 128], Wp[:, kc, half * 512:(half + 1) * 512],
                                   kc == 0, kc == 7, [BWp] + uRt, [Bbank[bk]])
                            tcopy(ev_eng(), pin[b % 2][:, half * 512:(half + 1) * 512], banks[bk][:, :],
                                  [Bbank[bk]], [Bpin[b % 2]])
                        for half in range(2):
                            dbk = 4 + half
                            for c4 in range(4):
                                c = half * 4 + c4
                                gi = c // 2
                                mm(banks[dbk][:, c4 * 128:(c4 + 1) * 128], pin[b % 2][:, c * 128:(c + 1) * 128],
                                   band(2 if b == 0 else 0, gi), True, b == 0, [Bpin[b % 2], Bcst], [Bbank[dbk]])
                                if b > 0:
                                    mm(banks[dbk][:, c4 * 128:(c4 + 1) * 128], pin[(b - 1) % 2][:, c * 128:(c + 1) * 128],
                                       band(1, gi), False, True, [Bpin[(b - 1) % 2], Bcst], [Bbank[dbk]])
                            tcopy(ev_eng(), dT[:, half * 4:(half + 1) * 4, blk * 128:(blk + 1) * 128],
                                  banks[dbk][:, :].rearrange("p (c q) -> p c q", c=4), [Bbank[dbk]], [BdT])
                    for oc in range(8):
                        g, dc = divmod(oc, 2)
                        bk = nb4()
                        for cc in range(2):
                            mm(banks[bk][:, :], pw[:, g, cc, dc * 128:(dc + 1) * 128], dT[:, g * 2 + cc, :],
                               cc == 0, cc == 1, [Bpw, BdT], [Bbank[bk]])
                        act(yp[:, oc, :], banks[bk][:, :], AF.Identity, [Bbank[bk], Bpp], [Byp], scale=ppc(l, PP_PSC + oc))
                    for oc in range(8):
                        bka = nb4()
                        for kc in range(8):
                            mm(banks[bka][:, :], wb[:, kc, oc * 128:(oc + 1) * 128], yp[:, kc, :], kc == 0, kc == 7,
                               [Bwb, Byp], [Bbank[bka]])
                        bkg = nb4()
                        for kc in range(8):
                            mm(banks[bkg][:, :], Wg[:, kc, oc * 128:(oc + 1) * 128], uT[:, kc, t0:t0 + 512], kc == 0, kc == 7,
                               [BWg] + uRt, [Bbank[bkg]])
                        act(g1[oc % 2][:, :], banks[bkg][:, :], AF.Sigmoid, [Bbank[bkg], Bpp], [Bg1[oc % 2]],
                            bias=ppc(l, PP_BG + 8 + oc))
                        tt("dve", mac[t % 2][:, oc, :], banks[bka][:, :], g1[oc % 2][:, :], ALU.mult,
                           [Bbank[bka], Bg1[oc % 2]], [Bm[t % 2]])
                    S.dma("sp", mac_d[t], mac[t % 2][:], [Bm[t % 2]], [Bmac[t]])
                    for oc in range(8):
                        bk = nb4()
                        for kc in range(8):
                            mm(banks[bk][:, :], Wz[:, kc, oc * 128:(oc + 1) * 128], uT[:, kc, t0:t0 + 512], kc == 0, kc == 7,
                               [BWz] + uRt, [Bbank[bk]])
                        act(zst[t % 2][:, oc, :], banks[bk][:, :], AF.Silu, [Bbank[bk]], [Bz[t % 2]])
                    S.dma("sp", zs_d[t], zst[t % 2][:], [Bz[t % 2]], [Bzs[t]])
            S.barrier()

        def phase_ssd(l):
            with ExitStack() as ps:
                Wx = sb("sd_wx", [128, 8, 1536], BF16, ps)
                Wd = sb("sd_wd", [128, 8, 16], BF16, ps)
                BWx, BWd = Buf(), Buf()
                w_l = w_in[l].rearrange("(c p) f -> p c f", p=128)
                wload(Wx[:], w_l[:, :, X0:X0 + 1536], BWx)
                wload(Wd[:], w_l[:, :, DT0:DT0 + 16], BWd)
                zt = sb("sd_z", [128, 8, 512], BF16, ps)
                Bzt = Buf()
                xraw = sb("sd_xr", [128, 12, 515], BF16, ps)
                hal = sb("sd_hal", [128, 12, 3], BF16, ps)
                Bxraw = Buf()
                Bhalo = Buf()
                Bhal = Buf()
                ctmp = [sb("sd_ct%d" % i, [128, 512], F32, ps) for i in range(2)]
                Bct = [Buf(), Buf()]
                xcT2 = [sb("sd_xc%d" % i, [128, 12, 512], BF16, ps) for i in range(2)]
                BxcT2 = [Buf(), Buf()]
                yT = sb("sd_y", [128, 8, 512], BF16, ps)
                ByT = Buf()
                sq = [sb("sd_sq%d" % i, [128, 512], BF16, ps) for i in range(2)]
                Bsq = [Buf(), Buf()]
                rstd = sb("sd_rs", [128, 2, 512], F32, ps)
                Brs = Buf()
                A_bc = sb("sd_A", [128, 16], F32, ps)
                BA = Buf()

                def two(name, shape, dt):
                    return [sb("%s%d" % (name, i), shape, dt, ps) for i in range(2)]

                dtp = two("sd_dtp", [128, 4, 16], F32)
                dt_ = two("sd_dt", [128, 4, 16], F32)
                da = two("sd_da", [128, 4, 16], F32)
                acs = two("sd_acs", [128, 4, 16], F32)
                nacs = two("sd_nacs", [128, 4, 16], F32)
                dec = two("sd_dec", [128, 4, 16], F32)
                w2 = two("sd_w2", [128, 4, 16], F32)
                cd = two("sd_cd", [128, 4, 16], F32)
                Bsm = [Buf(), Buf()]
                dtri = two("sd_dtri", [128, 2, 4, 128], BF16)
                Bdabc = [Buf(), Buf()]
                da_h = two("sd_dah", [128, 4, 16], BF16)
                da_l = two("sd_dal", [128, 4, 16], F32)
                xs_tok = sb("sd_xst", [128, 1024], BF16, ps)
                Bxst = Buf()
                xc_tok = two("sd_xct", [128, 1024], BF16)
                xdec = two("sd_xdec", [128, 1024], BF16)
                Bxct, Bxdec = [Buf(), Buf()], [Buf(), Buf()]
                Btok = two("sd_btok", [128, 256], BF16)
                BBtok = [Buf(), Buf()]
                cbm = [two("sd_cbm%d_" % i, [128, 128], F32) for i in range(2)]
                Bcbm = [[Buf(), Buf()], [Buf(), Buf()]]
                E1 = [sb("sd_e1%d" % i, [128, 512], F32, ps) for i in range(3)]
                BE1 = [Buf() for _ in range(3)]
                OD = two("sd_od", [128, 512], F32)
                Gm = two("sd_g", [128, 512], BF16)
                Cod = two("sd_cod", [128, 512], BF16)
                BOD, BG, BCod = [[Buf(), Buf()] for _ in range(3)]
                prev_f = sb("sd_prevf", [128, 1024], F32, ps)
                prev_b2 = [sb("sd_prevb%d" % i, [128, 1024], BF16, ps) for i in range(2)]
                Bpf = Buf()
                Bpb2 = [Buf(), Buf()]

                act(A_bc[:], ppc(l, PP_ALOG, 16), AF.Exp, [Bpp], [BA])
                ts("dve", A_bc[:], A_bc[:], -1.0, None, ALU.mult, None, [BA], [BA])
                memset("dve", prev_f[:], 0.0, [Bpf])
                memset("dve", prev_b2[0][:], 0.0, [Bpb2[0]])
                memset("dve", hal[:], 0.0, [Bhal])

                tb16 = banks[7][:, :].bitcast(BF16)
                misc = banks[2]
                mb16 = banks[2][:, :].bitcast(BF16)

                def conv(t):
                    t0 = t * 512
                    uRt = uR(t0, t0 + 512)
                    xcT = xcT2[t % 2]
                    BxcT = BxcT2[t % 2]
                    tcopy("pool", xraw[:, :, 0:3], hal[:], [Bhal], [Bhalo])
                    for c in range(12):
                        bk = rr["bank"] % 2
                        rr["bank"] += 1
                        for kc in range(8):
                            mm(banks[bk][:, :], Wx[:, kc, c * 128:(c + 1) * 128], uT[:, kc, t0:t0 + 512], kc == 0, kc == 7,
                               [BWx] + uRt, [Bbank[bk]])
                        tcopy("act", xraw[:, c, 3:515], banks[bk][:, :], [Bbank[bk]], [Bxraw])
                        tcopy("act", hal[:, c, :], banks[bk][:, 509:512], [Bbank[bk]], [Bhal])
                        ci = c % 2
                        wcol = PP_SCW + c * 4
                        ts("dve", ctmp[ci][:, :], xraw[:, c, 3:515], ppc(l, wcol + 3), ppc(l, PP_SCB + c),
                           ALU.mult, ALU.add, [Bxraw, Bpp], [Bct[ci]])
                        for k in (2, 1, 0):
                            stt(ctmp[ci][:, :], xraw[:, c, k:k + 512], ppc(l, wcol + k), ctmp[ci][:, :],
                                ALU.mult, ALU.add, [Bxraw, Bhalo, Bct[ci], Bpp], [Bct[ci]])
                        act(xcT[:, c, :], ctmp[ci][:, :], AF.Silu, [Bct[ci]], [BxcT])

                class _NS:
                    pass

                def make_tile(t):
                    t0 = t * 512
                    uRt = uR(t0, t0 + 512)
                    xcT = xcT2[t % 2]
                    BxcT = BxcT2[t % 2]

                    tp = t % 2
                    A0 = banks[0]

                    def stageA():
                        for ci in range(4):
                            tk0 = t0 + ci * 128
                            for kc in range(8):
                                mm(A0[:, ci * 16:(ci + 1) * 16], uT[:, kc, tk0:tk0 + 128], Wd[:, kc, :], kc == 0, kc == 7,
                                   [BWd] + uRt, [Bbank[0]])
                        v3 = lambda ap: ap.rearrange("p (c h) -> p c h", c=4)
                        tt("dve", dtp[tp][:], v3(A0[:, 0:64]), ppc(l, PP_DTB, 16).unsqueeze(1).to_broadcast([128, 4, 16]),
                           ALU.add, [Bbank[0], Bpp], [Bsm[tp]])
                        act(dtp[tp][:], dtp[tp][:], AF.Exp, [Bsm[tp]], [Bsm[tp]])
                        act(dt_[tp][:], dtp[tp][:], AF.Ln, [Bsm[tp]], [Bsm[tp]], bias=1.0)
                        tt("dve", da[tp][:], dt_[tp][:], A_bc[:].unsqueeze(1).to_broadcast([128, 4, 16]), ALU.mult,
                           [Bsm[tp], BA], [Bsm[tp]])
                        tcopy("dve", da_h[tp][:], da[tp][:], [Bsm[tp]], [Bsm[tp]])
                        tcopy("dve", da[tp][:], da_h[tp][:], [Bsm[tp]], [Bsm[tp]])
                        for ci in range(4):
                            mm(A0[:, 64 + ci * 16:64 + (ci + 1) * 16], tri_f, da[tp][:, ci, :], True, True, [Bcst, Bsm[tp]], [Bbank[0]])
                        for ci in range(4):
                            mm(A0[:, 128 + ci * 16:128 + (ci + 1) * 16], ones_f, da[tp][:, ci, :], True, True, [Bcst, Bsm[tp]], [Bbank[0]])
                        tcopy("dve", acs[tp][:], v3(A0[:, 64:128]), [Bbank[0]], [Bsm[tp]])
                        ts("dve", nacs[tp][:], v3(A0[:, 64:128]), -1.0, None, ALU.mult, None, [Bbank[0]], [Bsm[tp]])
                        tt("dve", dec[tp][:], v3(A0[:, 128:192]), acs[tp][:], ALU.subtract, [Bbank[0], Bsm[tp]], [Bsm[tp]])
                        act(dec[tp][:], dec[tp][:], AF.Exp, [Bsm[tp]], [Bsm[tp]])
                        act(cd[tp][:], v3(A0[:, 128:192]), AF.Exp, [Bbank[0]], [Bsm[tp]])
                        tt("dve", w2[tp][:], dt_[tp][:], dec[tp][:], ALU.mult, [Bsm[tp]], [Bsm[tp]])

                    def stageB(ci):
                        sl = ci % 2
                        cs = slice(ci * 128, ci * 128 + 128)
                        for c in range(8):
                            transp(tb16[:, c * 128:(c + 1) * 128], xcT[:, c, cs], [BxcT, Bcst], [Bbank[7]])
                        for g2 in range(2):
                            transp(mb16[:, 768 + g2 * 128:768 + (g2 + 1) * 128], xcT[:, 8 + g2, cs], [BxcT, Bcst], [Bbank[2]])
                        tcopy("act", Btok[sl][:], mb16[:, 768:1024], [Bbank[2]], [BBtok[sl]])
                        tcopy("act", xs_tok[:], tb16[:, :], [Bbank[7]], [Bxst])
                        tt("dve", xc_tok[sl][:].rearrange("p (h q) -> p h q", h=16), xs_tok[:].rearrange("p (h q) -> p h q", h=16),
                           dt_[tp][:, ci, :].unsqueeze(2).to_broadcast([128, 16, 64]), ALU.mult, [Bxst, Bsm[tp]], [Bxct[sl]])
                        tt("pool", xdec[sl][:].rearrange("p (h q) -> p h q", h=16), xs_tok[:].rearrange("p (h q) -> p h q", h=16),
                           w2[tp][:, ci, :].unsqueeze(2).to_broadcast([128, 16, 64]), ALU.mult, [Bxst, Bsm[tp]], [Bxdec[sl]])
                        for g2 in range(2):
                            mm(misc[:, g2 * 128:(g2 + 1) * 128], xcT[:, 8 + g2, cs], xcT[:, 10 + g2, cs], True, True,
                               [BxcT], [Bbank[2]])
                            tt("dve", cbm[sl][g2][:, :], misc[:, g2 * 128:(g2 + 1) * 128], tri_f, ALU.mult,
                               [Bbank[2], Bcst], [Bcbm[sl][g2]])

                    def states(ci, g2):
                        sl = ci % 2
                        mm(banks[1][:, :], Btok[sl][:, g2 * 128:(g2 + 1) * 128], xdec[sl][:, g2 * 512:(g2 + 1) * 512],
                           True, True, [BBtok[sl], Bxdec[sl]], [Bbank[1]])

                    def stage3a(ci):
                        for g2 in range(2):
                            states(ci, g2)
                            pg = prev_f[:, g2 * 512:(g2 + 1) * 512]
                            tt("dve", pg.rearrange("p (h q) -> p h q", h=8), pg.rearrange("p (h q) -> p h q", h=8),
                               cd[tp][:, ci, g2 * 8:(g2 + 1) * 8].unsqueeze(2).to_broadcast([128, 8, 64]), ALU.mult,
                               [Bpf, Bsm[tp]], [Bpf])
                            tt("dve", pg, banks[1][:, :], pg, ALU.add, [Bbank[1], Bpf], [Bpf])

                    def stage3b(ci):
                        gc = t * 4 + ci
                        tcopy("act", prev_b2[(gc + 1) % 2][:], prev_f[:], [Bpf], [Bpb2[(gc + 1) % 2]])

                    def s0(k):
                        ci, q4 = divmod(k, 4)
                        rb = k % 2
                        rbk = 3 + k % 3
                        hsl = slice(q4 * 4, (q4 + 1) * 4)
                        tri_bc = tri_f.unsqueeze(1).to_broadcast([128, 4, 128])
                        tt("dve", dtri[rb][:, 0, :, :], tri_bc, da[tp][:, ci, hsl].unsqueeze(2).to_broadcast([128, 4, 128]),
                           ALU.mult, [Bsm[tp], Bcst], [Bdabc[rb]])
                        mm(banks[rbk][:, :], ones_b, dtri[rb][:, 0, :, :].rearrange("p h q -> p (h q)"),
                           True, True, [Bdabc[rb], Bcst], [Bbank[rbk]])

                    def s1(k):
                        ci, q4 = divmod(k, 4)
                        rbk = 3 + k % 3
                        e = k % 3
                        for h4 in range(4):
                            h = q4 * 4 + h4
                            act(E1[e][:, h4 * 128:(h4 + 1) * 128], banks[rbk][:, h4 * 128:(h4 + 1) * 128], AF.Exp,
                                [Bbank[rbk], Bsm[tp]], [BE1[e]], bias=nacs[tp][:, ci, h:h + 1])

                    def s2(k):
                        rbk = 3 + k % 3
                        act(OD[k % 2][:, :], banks[rbk][:, :], AF.Exp, [Bbank[rbk]], [BOD[k % 2]])

                    def s3(k):
                        ci, q4 = divmod(k, 4)
                        sl = ci % 2
                        e = k % 3
                        rb = k % 2
                        g2 = q4 // 2
                        cs = slice(ci * 128, ci * 128 + 128)
                        stt(Gm[rb][:].rearrange("p (h q) -> p h q", h=4), E1[e][:].rearrange("p (h q) -> p h q", h=4), 1.0,
                            cbm[sl][g2][:, :].unsqueeze(1).to_broadcast([128, 4, 128]), ALU.min, ALU.mult,
                            [BE1[e], Bcbm[sl][g2]], [BG[rb]])
                        tt("dve", Cod[rb][:].rearrange("p (h q) -> p h q", h=4), OD[rb][:].rearrange("p (h q) -> p h q", h=4),
                           xcT[:, 10 + g2, cs].unsqueeze(1).to_broadcast([128, 4, 128]), ALU.mult,
                           [BOD[rb], BxcT], [BCod[rb]])

                    def s4(k):
                        ci, q4 = divmod(k, 4)
                        sl = ci % 2
                        rb = k % 2
                        for h4 in range(4):
                            h = q4 * 4 + h4
                            hs = slice(h * 64, (h + 1) * 64)
                            yc0 = rb * 256 + (h4 // 2) * 128
                            yo = banks[6][(h % 2) * 64:(h % 2 + 1) * 64, yc0:yc0 + 128]
                            mm(yo, xc_tok[sl][:, hs], Gm[rb][:, h4 * 128:(h4 + 1) * 128], True, False, [Bxct[sl], BG[rb]], [Bbank[6]])
                            gc = t * 4 + ci
                            mm(yo, prev_b2[gc % 2][:, hs], Cod[rb][:, h4 * 128:(h4 + 1) * 128], False, True,
                               [Bpb2[gc % 2], BCod[rb]], [Bbank[6]])

                    def s5(k):
                        ci, q4 = divmod(k, 4)
                        rb = k % 2
                        cs = slice(ci * 128, ci * 128 + 128)
                        for pi in range(2):
                            pair = q4 * 2 + pi
                            yc0 = rb * 256 + pi * 128
                            stt(yT[:, pair, cs], xcT[:, pair, cs], ppc(l, PP_DSK + pair), banks[6][:, yc0:yc0 + 128],
                                ALU.mult, ALU.add, [BxcT, Bbank[6], Bpp], [ByT])

                    def tail():
                        tt("pool", yT[:, :, :], yT[:, :, :], zt[:, :, :], ALU.mult, [ByT, Bzt], [ByT])
                        for g2 in range(2):
                            nbk = g2
                            for i in range(4):
                                c = g2 * 4 + i
                                act(sq[c % 2][:, :], yT[:, c, :], AF.Square, [ByT], [Bsq[c % 2]])
                                mm(banks[nbk][:, :], ones_b, sq[c % 2][:, :], i == 0, i == 3, [Bsq[c % 2], Bcst], [Bbank[nbk]])
                            act(rstd[:, g2, :], banks[nbk][:, :], AF.Ln, [Bbank[nbk]], [Brs], bias=EPS, scale=1.0 / 512)
                            act(rstd[:, g2, :], rstd[:, g2, :], AF.Exp, [Brs], [Brs], scale=-0.5)
                        for c in range(8):
                            stt(yT[:, c, :], yT[:, c, :], ppc(l, PP_NW + c), rstd[:, c // 4, :], ALU.mult, ALU.mult,
                                [ByT, Brs, Bpp], [ByT])
                        S.dma("sp", yn_d[t], yT[:], [ByT], [Byn[t]])

                    ns = _NS()
                    ns.stageA, ns.stageB, ns.stage3a, ns.stage3b = stageA, stageB, stage3a, stage3b
                    ns.s0, ns.s1, ns.s2, ns.s3, ns.s4, ns.s5, ns.tail = s0, s1, s2, s3, s4, s5, tail
                    return ns

                conv(0)
                T = [make_tile(t) for t in range(NT)]
                S.dma("sp", zt[:], zs_d[0], [Bzs[0]], [Bzt])
                T[0].stageA()
                T[0].stageB(0)
                for t in range(NT):
                    X = T[t]
                    for k in range(16 + 5):
                        if k < 16:
                            X.s0(k)
                        if 0 <= k - 1 < 16:
                            X.s1(k - 1)
                        if 0 <= k - 2 < 16:
                            X.s2(k - 2)
                        if 0 <= k - 3 < 16:
                            X.s3(k - 3)
                        if 0 <= k - 4 < 16:
                            X.s4(k - 4)
                        if 0 <= k - 5 < 16:
                            X.s5(k - 5)
                        if k % 4 == 3 and k // 4 < 4:
                            c = k // 4
                            X.stage3a(c)
                            X.stage3b(c)
                            if c + 1 < 4:
                                X.stageB(c + 1)
                        if k == 2:
                            if t > 0:
                                T[t - 1].tail()
                                S.dma("sp", zt[:], zs_d[t], [Bzs[t]], [Bzt])
                            if t + 1 < NT:
                                conv(t + 1)
                        if k == 9 and t + 1 < NT:
                            T[t + 1].stageA()
                        if k == 17 and t + 1 < NT:
                            T[t + 1].stageB(0)
                T[NT - 1].tail()
            S.barrier()

        def phase_merge(l):
            TT = 256
            with ExitStack() as ps:
                wa = sb("mg_wa", [128, 3, 1024], BF16, ps)
                Wg0 = sb("mg_wg0", [128, 8, 1024], BF16, ps)
                Wg2 = sb("mg_wg2", [128, 8, 1024], BF16, ps)
                wc = sb("mg_wc", [128, 8, 1024], BF16, ps)
                wo = sb("mg_wo", [128, 8, 1024], BF16, ps)
                Bwa, BWg0, BWg2, Bwc, Bwo = [Buf() for _ in range(5)]
                w_l = w_in[l].rearrange("(c p) f -> p c f", p=128)
                wload(wa[:], w_a[l].rearrange("(c p) f -> p c f", p=128), Bwa)
                wload(Wg0[:], w_l[:, :, G0:G0 + 1024], BWg0)
                wload(wc[:], w_c[l].rearrange("(c p) f -> p c f", p=128), Bwc)
                wload(Wg2[:], w_l[:, :, G0 + 2048:G0 + 3072], BWg2)
                wload(wo[:], w_o[l].rearrange("(c p) f -> p c f", p=128), Bwo)
                ao = [sb("mg_ao%d" % i, [128, 3, TT], BF16, ps) for i in range(2)]
                yn = [sb("mg_yn%d" % i, [128, 8, TT], BF16, ps) for i in range(2)]
                mc = [sb("mg_mc%d" % i, [128, 8, TT], BF16, ps) for i in range(2)]
                xt = [sb("mg_xt%d" % i, [128, 8, TT], F32, ps) for i in range(2)]
                Bao_t, Byn_t, Bmc_t, Bxt = [[Buf(), Buf()] for _ in range(4)]
                mg = sb("mg_mg", [128, 8, TT], BF16, ps)
                Bmg = Buf()
                gs = [sb("mg_gs%d" % i, [128, TT], F32, ps) for i in range(4)]
                Bgs = [Buf() for _ in range(4)]
                t1 = [sb("mg_t1%d" % i, [128, TT], F32, ps) for i in range(2)]
                t2 = [sb("mg_t2%d" % i, [128, TT], F32, ps) for i in range(2)]
                Bt1, Bt2 = [Buf(), Buf()], [Buf(), Buf()]
                sq = [sb("mg_sq%d" % i, [128, 512], BF16, ps) for i in range(2)]
                Bsq = [Buf(), Buf()]
                rs = sb("mg_rs", [128, 512], F32, ps)
                Brs = Buf()
                xsrc = xT_v if l == 0 else xr_v
                ao_v = ao_d.rearrange("j p t -> p j t")

                def nb6():
                    bk = rr["bank"] % 6
                    rr["bank"] += 1
                    return bk

                def loads(tt_):
                    p = tt_ % 2
                    a0 = tt_ * TT
                    t5, off = divmod(a0, 512)
                    S.dma("sp", ao[p][:], ao_v[:, :, a0:a0 + TT], [Bao], [Bao_t[p]])
                    S.dma("sp", yn[p][:], yn_d[t5][:, :, off:off + TT], [Byn[t5]], [Byn_t[p]])
                    S.dma("sp", mc[p][:], mac_d[t5][:, :, off:off + TT], [Bmac[t5]], [Bmc_t[p]])
                    S.dma("sp", xt[p][:], xsrc[:, :, a0:a0 + TT], [Bxr[tt_]], [Bxt[p]])

                ntt = SEQ // TT
                loads(0)
                for tt_ in range(ntt):
                    p = tt_ % 2
                    a0 = tt_ * TT
                    if tt_ + 1 < ntt:
                        loads(tt_ + 1)
                    uRt = uR(a0, a0 + TT)
                    for oc in range(8):
                        osl = slice(oc * 128, (oc + 1) * 128)
                        i2 = oc % 2
                        bka = nb6()
                        for kc in range(3):
                            mm(banks[bka][:, 0:TT], wa[:, kc, osl], ao[p][:, kc, :], kc == 0, kc == 2, [Bwa, Bao_t[p]], [Bbank[bka]])
                        bkg = nb6()
                        for kc in range(8):
                            mm(banks[bkg][:, 0:TT], Wg0[:, kc, osl], uT[:, kc, a0:a0 + TT], kc == 0, kc == 7, [BWg0] + uRt, [Bbank[bkg]])
                        act(gs[i2][:, :], banks[bkg][:, 0:TT], AF.Sigmoid, [Bbank[bkg], Bpp], [Bgs[i2]], bias=ppc(l, PP_BG + oc))
                        tt("dve", t1[i2][:, :], banks[bka][:, 0:TT], gs[i2][:, :], ALU.mult, [Bbank[bka], Bgs[i2]], [Bt1[i2]])
                        bkc = nb6()
                        for kc in range(8):
                            mm(banks[bkc][:, 0:TT], wc[:, kc, osl], yn[p][:, kc, :], kc == 0, kc == 7, [Bwc, Byn_t[p]], [Bbank[bkc]])
                        bkg2 = nb6()
                        for kc in range(8):
                            mm(banks[bkg2][:, 0:TT], Wg2[:, kc, osl], uT[:, kc, a0:a0 + TT], kc == 0, kc == 7, [BWg2] + uRt, [Bbank[bkg2]])
                        act(gs[2 + i2][:, :], banks[bkg2][:, 0:TT], AF.Sigmoid, [Bbank[bkg2], Bpp], [Bgs[2 + i2]],
                            bias=ppc(l, PP_BG + 16 + oc))
                        tt("dve", t2[i2][:, :], banks[bkc][:, 0:TT], gs[2 + i2][:, :], ALU.mult, [Bbank[bkc], Bgs[2 + i2]], [Bt2[i2]])
                        tt("pool", t1[i2][:, :], t1[i2][:, :], mc[p][:, oc, :], ALU.add, [Bt1[i2], Bmc_t[p]], [Bt1[i2]])
                        tt("pool", mg[:, oc, :], t1[i2][:, :], t2[i2][:, :], ALU.add, [Bt1[i2], Bt2[i2]], [Bmg])
                    for oc in range(8):
                        osl = slice(oc * 128, (oc + 1) * 128)
                        bk = nb6()
                        for kc in range(8):
                            mm(banks[bk][:, 0:TT], wo[:, kc, osl], mg[:, kc, :], kc == 0, kc == 7, [Bwo, Bmg], [Bbank[bk]])
                        tt("dve", xt[p][:, oc, :], banks[bk][:, 0:TT], xt[p][:, oc, :], ALU.add, [Bbank[bk], Bxt[p]], [Bxt[p]])
                    S.dma("sp", xr_v[:, :, a0:a0 + TT], xt[p][:], [Bxt[p]], [Bxr[tt_]])
                    norm_tile(None, xt[p], Bxt[p], PP_LN2, l, lambda c: uT[:, c, a0:a0 + TT], uRt, TT,
                              banks[6], Bbank[6], sq, Bsq, rs, Brs)
            S.barrier()

        def phase_ffn_up(l):
            with ExitStack() as ps:
                NG = 6
                wu = [sb("fu_w%d" % i, [128, 8, 2, 512], BF16, ps) for i in range(2)]
                Bwu = [Buf(), Buf()]
                raw = [[sb("fu_raw%d%d" % (s_, i), [128, 514], F32, ps) for i in range(2)] for s_ in range(2)]
                Braw = [[Buf(), Buf()], [Buf(), Buf()]]
                Bhal = [[Buf(), Buf()], [Buf(), Buf()]]
                ct = [[sb("fu_ct%d%d" % (s_, i), [128, 512], F32, ps) for i in range(2)] for s_ in range(2)]
                Bct = [[Buf(), Buf()], [Buf(), Buf()]]
                sa = [sb("fu_sa%d" % i, [128, 512], F32, ps) for i in range(2)]
                Bsa = [Buf(), Buf()]
                ab = [sb("fu_ab%d" % i, [128, 512], BF16, ps) for i in range(4)]
                Bab = [Buf() for _ in range(4)]
                wu_l = w_up[l].rearrange("(c p) f -> p c f", p=128)

                def load_g(gi):
                    sl = gi % 2
                    n = min(4, 22 - gi * 4) * 128
                    wload(wu[sl][:, :, 0, 0:n], wu_l[:, :, gi * 512:gi * 512 + n], Bwu[sl])
                    wload(wu[sl][:, :, 1, 0:n], wu_l[:, :, DFF + gi * 512:DFF + gi * 512 + n], Bwu[sl])

                pend = {"v": None, "k": 0}

                def finish(t, j, par):
                    act(sa[par][:, :], ct[0][par][:, :], AF.Silu, [Bct[0][par]], [Bsa[par]])
                    ai = pend["k"] % 4
                    pend["k"] += 1
                    tt("pool", ab[ai][:, :], sa[par][:, :], ct[1][par][:, :], ALU.mult, [Bsa[par], Bct[1][par]], [Bab[ai]])
                    S.dma("sp", act_d[t, :, j, :], ab[ai][:, :], [Bab[ai]], [Bact[t]])

                load_g(0)
                for gi in range(NG):
                    sl = gi % 2
                    if gi + 1 < NG:
                        load_g(gi + 1)
                    for jj in range(min(4, 22 - gi * 4)):
                        j = gi * 4 + jj
                        for s_ in range(2):
                            memset("dve", raw[s_][0][:, 0:2], 0.0, [Bhal[s_][0]])
                        for t in range(NT):
                            t0 = t * 512
                            par = t % 2
                            uRt = uR(t0, t0 + 512)
                            for s_ in range(2):
                                bk = rr["bank"] % 4
                                rr["bank"] += 1
                                ch = j if s_ == 0 else 22 + j
                                for kc in range(8):
                                    mm(banks[bk][:, :], wu[sl][:, kc, s_, jj * 128:(jj + 1) * 128], uT[:, kc, t0:t0 + 512],
                                       kc == 0, kc == 7, [Bwu[sl]] + uRt, [Bbank[bk]])
                                tcopy("act", raw[s_][par][:, 2:514], banks[bk][:, :], [Bbank[bk]], [Braw[s_][par]])
                                tcopy("act", raw[s_][1 - par][:, 0:2], banks[bk][:, 510:512], [Bbank[bk]], [Bhal[s_][1 - par]])
                                wcol = PP_FCW + ch * 3
                                if FFN_ACT_TAP:
                                    act(ct[s_][par][:, :], banks[bk][:, :], AF.Identity, [Bbank[bk], Bpp], [Bct[s_][par]],
                                        bias=ppc(l, PP_FCB + ch), scale=ppc(l, wcol + 2))
                                else:
                                    ts("dve", ct[s_][par][:, :], raw[s_][par][:, 2:514], ppc(l, wcol + 2), ppc(l, PP_FCB + ch),
                                       ALU.mult, ALU.add, [Braw[s_][par], Bpp], [Bct[s_][par]])
                                for kk in (1, 0):
                                    stt(ct[s_][par][:, :], raw[s_][par][:, kk:kk + 512], ppc(l, wcol + kk), ct[s_][par][:, :],
                                        ALU.mult, ALU.add, [Braw[s_][par], Bhal[s_][par], Bct[s_][par], Bpp], [Bct[s_][par]])
                            if pend["v"] is not None:
                                finish(*pend["v"])
                            pend["v"] = (t, j, par)
                finish(*pend["v"])
            S.barrier()

        def phase_ffn_down(l, last):
            TT = 256
            with ExitStack() as ps:
                wd = sb("fd_w", [128, 22, 1024], BF16, ps)
                Bwd = Buf()
                wload(wd[:, 0:11, :], w_down[l, 0:1408].rearrange("(c p) f -> p c f", p=128), Bwd)
                wload(wd[:, 11:22, :], w_down[l, 1408:2816].rearrange("(c p) f -> p c f", p=128), Bwd)
                at = [sb("fd_at%d" % i, [128, 22, 512], BF16, ps) for i in range(2)]
                Bat = [Buf(), Buf()]
                xt = [sb("fd_xt%d" % i, [128, 8, TT], F32, ps) for i in range(2)]
                Bxt = [Buf(), Buf()]
                ot = [sb("fd_ot%d" % i, [128, 8, TT], F32, ps) for i in range(2)] if last else None
                Bot = [Buf(), Buf()]
                sq = [sb("fd_sq%d" % i, [128, 512], BF16, ps) for i in range(2)]
                Bsq = [Buf(), Buf()]
                rs = sb("fd_rs", [128, 512], F32, ps)
                Brs = Buf()
                ntt = SEQ // TT

                def load_a(t):
                    S.dma("sp", at[t % 2][:], act_d[t], [Bact[t]], [Bat[t % 2]])

                def load_x(tt_):
                    a0 = tt_ * TT
                    S.dma("sp", xt[tt_ % 2][:], xr_v[:, :, a0:a0 + TT], [Bxr[tt_]], [Bxt[tt_ % 2]])

                load_a(0)
                load_x(0)
                for tt_ in range(ntt):
                    p = tt_ % 2
                    a0 = tt_ * TT
                    t5, off = divmod(a0, 512)
                    if off == 0 and t5 + 1 < NT:
                        load_a(t5 + 1)
                    if tt_ + 1 < ntt:
                        load_x(tt_ + 1)
                    for oc in range(8):
                        bk = rr["bank"] % 6
                        rr["bank"] += 1
                        for kc in range(22):
                            mm(banks[bk][:, 0:TT], wd[:, kc, oc * 128:(oc + 1) * 128], at[t5 % 2][:, kc, off:off + TT],
                               kc == 0, kc == 21, [Bwd, Bat[t5 % 2]], [Bbank[bk]])
                        tt("dve", xt[p][:, oc, :], banks[bk][:, 0:TT], xt[p][:, oc, :], ALU.add, [Bbank[bk], Bxt[p]], [Bxt[p]])
                    uRt = uR(a0, a0 + TT)
                    if not last:
                        S.dma("sp", xr_v[:, :, a0:a0 + TT], xt[p][:], [Bxt[p]], [Bxr[tt_]])
                        norm_tile(None, xt[p], Bxt[p], PP_LN1, l + 1, lambda c: uT[:, c, a0:a0 + TT], uRt, TT,
                                  banks[6], Bbank[6], sq, Bsq, rs, Brs)
                    else:
                        if debug:
                            S.dma("sp", xr_v[:, :, a0:a0 + TT], xt[p][:], [Bxt[p]], [Bxr[tt_]])
                        norm_tile(None, xt[p], Bxt[p], PP_FG, l, lambda c: ot[p][:, c, :], [Bot[p]], TT,
                                  banks[6], Bbank[6], sq, Bsq, rs, Brs)
                        S.dma("sp", outT_v[:, :, a0:a0 + TT], ot[p][:], [Bot[p]], [Bout])
            S.barrier()

        phase_norm0()
        for l in range(depth):
            for name, fn in (("attn", lambda: phase_attn(l)), ("poolz", lambda: phase_poolz(l)),
                             ("ssd", lambda: phase_ssd(l)), ("merge", lambda: phase_merge(l)),
                             ("ffn_up", lambda: phase_ffn_up(l)),
                             ("ffn_down", lambda: phase_ffn_down(l, l == depth - 1))):
                if stop["flag"]:
                    break
                if name not in SKIP:
                    fn()
                done(l, name)
            if stop["flag"]:
                break
        if stop["flag"]:
            with ExitStack() as ps:
                z = sb("dbg_z", [128, 8, 512], F32, ps)
                Bz_ = Buf()
                memset("dve", z[:], 0.0, [Bz_])
                for t in range(NT):
                    S.dma("sp", outT_v[:, :, t * 512:(t + 1) * 512], z[:], [Bz_], [Bout])

        S.emit(lambda name: es.enter_context(nc.semaphore(name)))
    nc._sched_stats = {e: len(S.streams[e]) for e in ENGS}
    nc._sched_stats["waits"] = S.nwait
    return nc


def _t5_bucket(dist):
    dist = np.asarray(dist, dtype=np.int64)
    max_exact = 16
    nf = np.maximum(dist, 1).astype(np.float32)
    large = max_exact + (np.log(nf / np.float32(max_exact)) / np.float32(math.log(2048 / max_exact))
                         * np.float32(32 - max_exact)).astype(np.int32)
    large = np.minimum(large, 31)
    return np.where(dist < max_exact, dist, large)


def _bias_tables(rel_bias):
    out = np.full((128, 18, 2, 128), NEG, dtype=np.float32)
    j = np.arange(128)[:, None]
    i = np.arange(128)[None, :]
    for g, (win, d) in enumerate(GROUPS):
        rel_c = i - j
        rel_p = i + 128 - j
        for hh in range(6):
            h = g * 6 + hh
            bc = rel_bias[_t5_bucket(np.clip(rel_c, 0, None) * d), h]
            out[:, h, 0, :] = np.where(rel_c >= 0, bc, NEG)
            bp = rel_bias[_t5_bucket(np.clip(rel_p, 0, None) * d), h]
            out[:, h, 1, :] = np.where(rel_p <= 128, bp, NEG)
    return out.reshape(128, 18, 256)


def _constants():
    c = np.zeros((128, NCST), dtype=np.float32)
    c[:, C_ID:C_ID + 128] = np.eye(128, dtype=np.float32)
    c[:, C_ONE:C_ONE + 128] = 1.0
    a = np.arange(128)
    c[:, C_TRI:C_TRI + 128] = (a[:, None] <= a[None, :]).astype(np.float32)
    tp = a[:, None]
    t = a[None, :]
    for gi, w in enumerate(POOLW):
        cur = ((t - tp >= 0) & (t - tp < w)).astype(np.float32) / w - (t == tp).astype(np.float32)
        prev = ((t + 128 - tp) < w).astype(np.float32) / w
        cnt = np.minimum(t + 1, w).astype(np.float32)
        cur0 = ((t - tp >= 0) & (t - tp < w)).astype(np.float32) / cnt - (t == tp).astype(np.float32)
        for kind, m in enumerate((cur, prev, cur0)):
            o = C_BAND + (kind * 4 + gi) * 128
            c[:, o:o + 128] = m
    return c


def _pack_params(inp, depth):
    pp = np.zeros((128, depth, NPP), dtype=np.float32)

    def fm(v, n):
        return np.asarray(v, dtype=np.float32).reshape(n, 128).T

    for l in range(depth):
        pp[:, l, PP_LN1:PP_LN1 + 8] = fm(inp["ln1_g"][l], 8)
        pp[:, l, PP_LN2:PP_LN2 + 8] = fm(inp["ln2_g"][l], 8)
        pp[:, l, PP_BG:PP_BG + 24] = fm(inp["b_gate"][l], 24)
        pp[:, l, PP_PSC:PP_PSC + 8] = fm(inp["pool_scale"][l], 8)
        cw = np.asarray(inp["ssd_conv_w"][l], dtype=np.float32)
        pp[:, l, PP_SCW:PP_SCW + 48] = cw.reshape(4, 12, 128).transpose(2, 1, 0).reshape(128, 48)
        pp[:, l, PP_SCB:PP_SCB + 12] = fm(inp["ssd_conv_b"][l], 12)
        pp[:, l, PP_NW:PP_NW + 8] = fm(inp["ssd_norm_w"][l], 8)
        pp[:, l, PP_DSK:PP_DSK + 8] = fm(np.repeat(np.asarray(inp["ssd_d"][l], dtype=np.float32), 64), 8)
        fw = np.asarray(inp["ffn_conv_w"][l], dtype=np.float32)
        pp[:, l, PP_FCW:PP_FCW + 132] = fw.reshape(3, 44, 128).transpose(2, 1, 0).reshape(128, 132)
        pp[:, l, PP_FCB:PP_FCB + 44] = fm(inp["ffn_conv_b"][l], 44)
        pp[:, l, PP_DTB:PP_DTB + 16] = np.asarray(inp["ssd_dt_bias"][l], dtype=np.float32)[None, :]
        pp[:, l, PP_ALOG:PP_ALOG + 16] = np.asarray(inp["ssd_a_log"][l], dtype=np.float32)[None, :]
        pp[:, l, PP_FG:PP_FG + 8] = fm(inp["final_g"], 8)
    return pp


_CACHE = {}


def make_in_maps(inp, depth, cores):
    f = lambda a: np.ascontiguousarray(np.asarray(a, dtype=np.float32))
    shared = {
        "w_in": f(inp["w_in"][:depth]), "w_a": f(inp["w_a"][:depth]), "pool_w": f(inp["pool_w"][:depth]),
        "w_b": f(inp["w_b"][:depth]), "w_c": f(inp["w_c"][:depth]), "w_o": f(inp["w_o"][:depth]),
        "w_up": f(inp["ffn_w_up"][:depth]), "w_down": f(inp["ffn_w_down"][:depth]),
        "pp": _pack_params(inp, depth), "cst": _constants(), "biasT": _bias_tables(f(inp["rel_bias"])),
    }
    x = np.asarray(inp["x"], dtype=np.float32)
    maps = []
    for b in cores:
        m = dict(shared)
        m["xT"] = np.ascontiguousarray(x[b].T)
        maps.append(m)
    return maps


def kernel(**inputs):
    if "nc" not in _CACHE:
        _CACHE["nc"] = build_program(DEPTH)
    nc = _CACHE["nc"]
    in_maps = make_in_maps(inputs, DEPTH, list(range(NCORES)))
    res = run_bass_kernel_spmd(nc, in_maps, core_ids=list(range(NCORES)))
    out = np.stack([np.ascontiguousarray(r["outT"].T) for r in res.results], axis=0)
    return out.astype(np.float32)
```

```python
import math
import numpy as np
from contextlib import ExitStack
import concourse.bass as bass
import concourse.mybir as mybir
from concourse.bass_utils import run_bass_kernel_spmd

F32 = mybir.dt.float32
BF16 = mybir.dt.bfloat16
AF = mybir.ActivationFunctionType
ALU = mybir.AluOpType

D = 1024
SEQ = 4096
DEPTH = 4
NCORES = 8
IN_W = 10128
Q0, K0, V0, P0, Z0, X0, DT0, G0 = 0, 1152, 2304, 3456, 4480, 5504, 7040, 7056
DFF = 2816
EPS = 1e-6
GROUPS = ((128, 1), (512, 4), (2048, 16))
POOLW = (2, 4, 8, 16)
NEG = -30000.0
import os
SSD_LEVEL = int(os.environ.get('SSD_LEVEL', '9'))
SKIP = os.environ.get('SKIP', '').split(',')
FFN_ACT_TAP = int(os.environ.get('FFN_ACT_TAP', '1'))
OPLIMIT = int(os.environ.get('OPLIMIT', '1000000000'))

PP_LN1, PP_LN2, PP_BG, PP_PSC, PP_SCW, PP_SCB, PP_NW, PP_DSK, PP_FCW, PP_FCB, PP_DTB, PP_ALOG, PP_FG = (
    0, 8, 16, 40, 48, 96, 108, 116, 124, 256, 300, 316, 332)
NPP = 340
C_ID, C_ONE, C_TRI, C_BAND = 0, 128, 256, 384
NCST = 384 + 12 * 128

ENGS = ("pe", "act", "dve", "pool", "sp")


class Buf:
    __slots__ = ("name", "w", "rs")

    def __init__(self, name=""):
        self.name = name
        self.w = None
        self.rs = []


class Op:
    __slots__ = ("eng", "idx", "fn", "deps", "dma", "sem", "val", "inc", "waits", "presem")

    def __init__(self, eng, idx, fn, dma):
        self.eng = eng
        self.idx = idx
        self.fn = fn
        self.deps = {}
        self.dma = dma
        self.sem = None
        self.val = 0
        self.inc = False
        self.waits = []
        self.presem = None


class Sched:
    def __init__(self, nc, n_dma_sems=40, raw_gap=3):
        self.nc = nc
        self.streams = {e: [] for e in ENGS}
        self.n_dma_sems = n_dma_sems
        self.raw_gap = raw_gap
        self.bar = None
        self.bar_seen = set()
        self.region = False
        self.rcount = 0

    def _add_dep(self, op, d, raw):
        if d is None or d is op:
            return
        if d.eng == op.eng and not d.dma:
            if op.eng == "pe" or not raw or op.dma:
                return
            if op.idx - d.idx >= self.raw_gap:
                return
        if d.dma:
            op.deps[("dma", d.eng, d.idx)] = d
        else:
            cur = op.deps.get(d.eng)
            if cur is None or cur.idx < d.idx:
                op.deps[d.eng] = d

    def barrier(self):
        self.bar = [st[-1] for st in self.streams.values() if st]
        self.bar_seen = set()

    def op(self, eng, fn, reads=(), writes=(), dma=False):
        if self.region:
            self.rcount += 1
            if self.rcount > OPLIMIT:
                return None
        st = self.streams[eng]
        o = Op(eng, len(st), fn, dma)
        if self.bar is not None and eng not in self.bar_seen:
            self.bar_seen.add(eng)
            for d in self.bar:
                if d.eng != eng or d.dma:
                    self._add_dep(o, d, False)
        for b in reads:
            self._add_dep(o, b.w, True)
        for b in writes:
            self._add_dep(o, b.w, False)
            for r in b.rs:
                self._add_dep(o, r, False)
        for b in writes:
            b.w = o
            b.rs = []
        for b in reads:
            if b.w is not o:
                b.rs.append(o)
        st.append(o)
        return o

    def dma(self, eng, out, in_, reads=(), writes=(), **kw):
        return self.op(eng, lambda e: e.dma_start(out=out, in_=in_, **kw), reads, writes, dma=True)

    def emit(self, sem_ctx):
        nc = self.nc
        esem = {e: sem_ctx("c_" + e) for e in ENGS if e != "sp"}
        dsem = {}
        for e in ENGS:
            if any(o.dma for o in self.streams[e]):
                dsem[e] = [sem_ctx("d_%s_%d" % (e, i)) for i in range(self.n_dma_sems)]
        for e in ENGS:
            for o in self.streams[e]:
                for d in o.deps.values():
                    if not d.dma:
                        d.inc = True
        for e in ENGS:
            cnt = 0
            dcnt = [0] * self.n_dma_sems
            k = 0
            for o in self.streams[e]:
                if o.dma:
                    s = k % self.n_dma_sems
                    k += 1
                    o.presem = (dsem[e][s], dcnt[s])
                    dcnt[s] += 16
                    o.sem = dsem[e][s]
                    o.val = dcnt[s]
                elif o.inc:
                    cnt += 1
                    o.sem = esem[e]
                    o.val = cnt
        nwait = 0
        for e in ENGS:
            known = {}
            for o in self.streams[e]:
                ws = []
                if o.dma and o.presem[1] > 0:
                    s, v = o.presem
                    if known.get(id(s), 0) < v:
                        known[id(s)] = v
                        ws.append((s, v))
                for d in o.deps.values():
                    if known.get(id(d.sem), 0) < d.val:
                        known[id(d.sem)] = d.val
                        ws.append((d.sem, d.val))
                o.waits = ws
                nwait += len(ws)
        self.nwait = nwait

        def run(eng_name, eh):
            final = {}
            for o in self.streams[eng_name]:
                for s, v in o.waits:
                    eh.wait_ge(s, v)
                ins = o.fn(eh)
                if o.dma:
                    ins.then_inc(o.sem, 16)
                    final[id(o.sem)] = (o.sem, o.val)
                elif o.inc:
                    ins.then_inc(o.sem, 1)
            for s, v in final.values():
                eh.wait_ge(s, v)

        with nc.Block() as block:
            if self.streams["pe"]:
                @block.tensor
                def _(eh):
                    run("pe", eh)
            if self.streams["act"]:
                @block.scalar
                def _(eh):
                    run("act", eh)
            if self.streams["dve"]:
                @block.vector
                def _(eh):
                    run("dve", eh)
            if self.streams["pool"]:
                @block.gpsimd
                def _(eh):
                    run("pool", eh)
            if self.streams["sp"]:
                @block.sync
                def _(eh):
                    run("sp", eh)


def build_program(depth=DEPTH, debug=False, upto=None):
    nc = bass.Bass("TRN2", target_bir_lowering=False)
    S = Sched(nc)
    NT = SEQ // 512
    dbg_kind = "ExternalOutput" if debug else "Internal"

    def dram_in(name, shape, dt=F32):
        return nc.dram_tensor(name, list(shape), dt, kind="ExternalInput").ap()

    xT = dram_in("xT", [D, SEQ])
    w_in = dram_in("w_in", [depth, D, IN_W])
    w_a = dram_in("w_a", [depth, 384, D])
    pool_w = dram_in("pool_w", [depth, 4, 256, 256])
    w_b = dram_in("w_b", [depth, D, D])
    w_c = dram_in("w_c", [depth, D, D])
    w_o = dram_in("w_o", [depth, D, D])
    w_up = dram_in("w_up", [depth, D, 2 * DFF])
    w_down = dram_in("w_down", [depth, DFF, D])
    pp_d = dram_in("pp", [128, depth, NPP])
    cst_d = dram_in("cst", [128, NCST])
    bias_d = dram_in("biasT", [128, 18, 256])
    outT = nc.dram_tensor("outT", [D, SEQ], F32, kind="ExternalOutput").ap()

    xr = nc.dram_tensor("xr", [D, SEQ], F32, kind=dbg_kind).ap()
    ao_d = nc.dram_tensor("ao_d", [3, 128, SEQ], BF16, kind=dbg_kind).ap()
    zs_d = nc.dram_tensor("zs_d", [NT, 128, 8, 512], BF16, kind=dbg_kind).ap()
    mac_d = nc.dram_tensor("mac_d", [NT, 128, 8, 512], BF16, kind=dbg_kind).ap()
    yn_d = nc.dram_tensor("yn_d", [NT, 128, 8, 512], BF16, kind=dbg_kind).ap()
    act_d = nc.dram_tensor("act_d", [NT, 128, 22, 512], BF16, kind=dbg_kind).ap()

    xT_v = xT.rearrange("(c p) t -> p c t", p=128)
    xr_v = xr.rearrange("(c p) t -> p c t", p=128)
    outT_v = outT.rearrange("(c p) t -> p c t", p=128)

    Bxr = [Buf() for _ in range(16)]
    Bao = Buf()
    Bzs = [Buf() for _ in range(NT)]
    Bmac = [Buf() for _ in range(NT)]
    Byn = [Buf() for _ in range(NT)]
    Bact = [Buf() for _ in range(NT)]
    Bout = Buf()

    es = ExitStack()
    with es:
        uid = {"n": 0}

        def sb(name, shape, dt, stack=es):
            uid["n"] += 1
            return stack.enter_context(nc.sbuf_tensor("s%d_%s" % (uid["n"], name), list(shape), dt))

        uT = sb("uT", [128, 8, SEQ], BF16)
        BuT = [Buf() for _ in range(16)]
        pp = sb("pp", [128, depth, NPP], F32)
        Bpp = Buf()
        cstf = sb("cstf", [128, 256], F32)
        cstb = sb("cstb", [128, NCST], BF16)
        Bcst = Buf()
        banks = [es.enter_context(nc.psum_tensor("bank%d" % i, [128, 512], F32)) for i in range(8)]
        Bbank = [Buf() for _ in range(8)]

        def uR(t0, t1):
            return BuT[t0 // 256:(t1 + 255) // 256]

        def mm(out, lhsT, rhs, start, stop, R, W):
            S.op("pe", lambda e: e.matmul(out, lhsT=lhsT, rhs=rhs, start=start, stop=stop), R, W)

        def transp(out, in_, R, W):
            S.op("pe", lambda e: e.transpose(out, in_, cstb[:, C_ID:C_ID + 128]), R, W)

        def act(out, in_, func, R, W, bias=None, scale=None):
            kw = {}
            if bias is not None:
                kw["bias"] = bias
            if scale is not None:
                kw["scale"] = scale
            S.op("act", lambda e: e.activation(out=out, in_=in_, func=func, **kw), R, W)

        def tcopy(eng, out, in_, R, W):
            if eng == "act":
                act(out, in_, AF.Copy, R, W)
            else:
                S.op(eng, lambda e: e.tensor_copy(out=out, in_=in_), R, W)

        def tt(eng, out, in0, in1, op, R, W):
            S.op(eng, lambda e: e.tensor_tensor(out=out, in0=in0, in1=in1, op=op), R, W)

        def ts(eng, out, in0, s1, s2, op0, op1, R, W):
            if op1 is None:
                S.op(eng, lambda e: e.tensor_scalar(out=out, in0=in0, scalar1=s1, scalar2=None, op0=op0), R, W)
            else:
                S.op(eng, lambda e: e.tensor_scalar(out=out, in0=in0, scalar1=s1, scalar2=s2, op0=op0, op1=op1), R, W)

        def stt(out, in0, scalar, in1, op0, op1, R, W):
            S.op("dve", lambda e: e.scalar_tensor_tensor(out=out, in0=in0, scalar=scalar, in1=in1, op0=op0, op1=op1), R, W)

        def memset(eng, ap, val, W):
            S.op(eng, lambda e: e.memset(ap, val), (), W)

        def wload(dst, src, B):
            S.dma("pool", dst, src, (), [B])

        rr = {"ev": 0, "bank": 0}

        def ev_eng():
            rr["ev"] += 1
            return "act" if rr["ev"] % 2 else "dve"

        S.dma("sp", pp[:], pp_d, (), [Bpp])
        S.dma("sp", cstf[:], cst_d[:, C_ONE:C_ONE + 256], (), [Bcst])
        S.dma("pool", cstb[:], cst_d, (), [Bcst])
        ident = cstb[:, C_ID:C_ID + 128]
        ones_b = cstb[:, C_ONE:C_ONE + 128]
        ones_f = cstf[:, 0:128]
        tri_f = cstf[:, 128:256]

        def band(kind, gi):
            o = C_BAND + (kind * 4 + gi) * 128
            return cstb[:, o:o + 128]

        def ppc(l, col, n=1):
            return pp[:, l, col:col + n]

        def norm_tile(st, xt, Bx, gcol0, l, dst_fn, Bdst, TT, nb, Bnb, sq, Bsq, rs, Brs):
            for c in range(8):
                act(sq[c % 2][:, 0:TT], xt[:, c, :], AF.Square, [Bx], [Bsq[c % 2]])
                mm(nb[:, 0:TT], ones_b, sq[c % 2][:, 0:TT], c == 0, c == 7, [Bsq[c % 2], Bcst], [Bnb])
            act(rs[:, 0:TT], nb[:, 0:TT], AF.Ln, [Bnb], [Brs], bias=EPS, scale=1.0 / D)
            act(rs[:, 0:TT], rs[:, 0:TT], AF.Exp, [Brs], [Brs], scale=-0.5)
            for c in range(8):
                stt(dst_fn(c), xt[:, c, :], ppc(l, gcol0 + c), rs[:, 0:TT], ALU.mult, ALU.mult,
                    [Bx, Brs, Bpp], Bdst)

        stop = {"flag": False}

        def done(l, name):
            if upto is not None and (l, name) == tuple(upto):
                stop["flag"] = True

        def phase_norm0():
            with ExitStack() as ps:
                xt = [sb("n0_xt%d" % i, [128, 8, 512], F32, ps) for i in range(2)]
                Bxt = [Buf() for _ in range(2)]
                sq = [sb("n0_sq%d" % i, [128, 512], BF16, ps) for i in range(2)]
                Bsq = [Buf(), Buf()]
                rs = sb("n0_rs", [128, 512], F32, ps)
                Brs = Buf()
                for t in range(NT):
                    t0 = t * 512
                    S.dma("sp", xt[t % 2][:], xT_v[:, :, t0:t0 + 512], (), [Bxt[t % 2]])
                    norm_tile(None, xt[t % 2], Bxt[t % 2], PP_LN1, 0,
                              lambda c: uT[:, c, t0:t0 + 512], uR(t0, t0 + 512), 512,
                              banks[0], Bbank[0], sq, Bsq, rs, Brs)
            S.barrier()

        def phase_attn(l):
            with ExitStack() as ps:
                bias_sb = sb("at_bias", [128, 18, 256], F32, ps)
                Bbias = Buf()
                S.dma("sp", bias_sb[:], bias_d, (), [Bbias])
                acc = sb("at_acc", [128, 2, SEQ], F32, ps)
                Bacc = Buf()
                qT = [sb("at_q%d" % i, [128, SEQ], BF16, ps) for i in range(2)]
                kT = [sb("at_k%d" % i, [128, SEQ], BF16, ps) for i in range(2)]
                vS = [sb("at_v%d" % i, [128, 32, 128], BF16, ps) for i in range(2)]
                wq = [sb("at_w%d" % i, [128, 8, 3, 128], BF16, ps) for i in range(2)]
                Bq = [Buf(), Buf()]
                Bk = [Buf(), Buf()]
                Bv = [Buf(), Buf()]
                Bw = [Buf(), Buf()]
                tmp = [sb("at_tmp%d" % i, [128, 256], F32, ps) for i in range(4)]
                pT = [sb("at_p%d" % i, [128, 256], BF16, ps) for i in range(4)]
                Btmp = [Buf() for _ in range(4)]
                BpT = [Buf() for _ in range(4)]
                osb = sb("at_o", [128, SEQ], BF16, ps)
                Bosb = Buf()
                w_l = w_in[l].rearrange("(c p) f -> p c f", p=128)
                it = 0
                combos = [(j, g) for j in range(3) for g in range(3)]

                def load_w(idx):
                    j, g = combos[idx]
                    sl = idx % 2
                    fo = (g * 6 + 2 * j) * 64
                    for wi, base in enumerate((Q0, K0, V0)):
                        wload(wq[sl][:, :, wi, :], w_l[:, :, base + fo:base + fo + 128], Bw[sl])

                load_w(0)
                for idx, (j, g) in enumerate(combos):
                    sl = idx % 2
                    if idx + 1 < len(combos):
                        load_w(idx + 1)
                    win, d = GROUPS[g]
                    L = SEQ // d
                    nbr = L // 128
                    for t in range(NT):
                        t0 = t * 512
                        for wi in range(2):
                            bk = rr["bank"] % 2
                            rr["bank"] += 1
                            for kc in range(8):
                                mm(banks[bk][:, :], wq[sl][:, kc, wi, :], uT[:, kc, t0:t0 + 512],
                                   kc == 0, kc == 7, [Bw[sl]] + uR(t0, t0 + 512), [Bbank[bk]])
                            dst_t = qT[sl] if wi == 0 else kT[sl]
                            Bd = Bq[sl] if wi == 0 else Bk[sl]
                            if d == 1:
                                src = banks[bk][:, :]
                                dst = dst_t[:, t0:t0 + 512]
                            else:
                                src = banks[bk][:, :].rearrange("p (m r) -> p r m", r=d)
                                dst = dst_t[:, :].rearrange("p (r l) -> p r l", r=d)[:, :, t0 // d:(t0 + 512) // d]
                            if wi == 0:
                                act(dst, src, AF.Copy, [Bbank[bk]], [Bd], scale=0.125)
                            else:
                                tcopy("dve", dst, src, [Bbank[bk]], [Bd])
                    uv = None
                    if d > 1:
                        uv = [uT[:, kc, :].rearrange("p (n i r) -> p r n i", r=d, i=128) for kc in range(8)]
                    for b4 in range(8):
                        bk = rr["bank"] % 2
                        rr["bank"] += 1
                        for bb in range(4):
                            b = b4 * 4 + bb
                            r, n = divmod(b, nbr)
                            for kc in range(8):
                                lhs = uT[:, kc, b * 128:(b + 1) * 128] if d == 1 else uv[kc][:, r, n, :]
                                mm(banks[bk][:, bb * 128:(bb + 1) * 128], lhs, wq[sl][:, kc, 2, :],
                                   kc == 0, kc == 7, [Bw[sl]] + BuT, [Bbank[bk]])
                        tcopy(ev_eng(), vS[sl][:, b4 * 4:(b4 + 1) * 4, :],
                              banks[bk][:, :].rearrange("p (b f) -> p b f", b=4), [Bbank[bk]], [Bv[sl]])
                    if d > 1:
                        accv = acc[:, :, :].rearrange("p a (n i r) -> p a r n i", r=d, i=128)
                    def emit_S(b):
                        r, n = divmod(b, nbr)
                        hp = n > 0
                        for hh in range(2):
                            sbk = 2 + (b % 2) * 2 + hh
                            ps_ = slice(hh * 64, (hh + 1) * 64)
                            mm(banks[sbk][:, 0:128], kT[sl][ps_, b * 128:(b + 1) * 128],
                               qT[sl][ps_, b * 128:(b + 1) * 128], True, True, [Bq[sl], Bk[sl]], [Bbank[sbk]])
                            if hp:
                                mm(banks[sbk][:, 128:256], kT[sl][ps_, (b - 1) * 128:b * 128],
                                   qT[sl][ps_, b * 128:(b + 1) * 128], True, True, [Bq[sl], Bk[sl]], [Bbank[sbk]])

                    def emit_add(b):
                        r, n = divmod(b, nbr)
                        ncol = 256 if n > 0 else 128
                        for hh in range(2):
                            sbk = 2 + (b % 2) * 2 + hh
                            bi = (b % 2) * 2 + hh
                            hg = g * 6 + 2 * j + hh
                            tt("dve", tmp[bi][:, 0:ncol], banks[sbk][:, 0:ncol], bias_sb[:, hg, 0:ncol], ALU.add,
                               [Bbank[sbk], Bbias], [Btmp[bi]])

                    def emit_exp(b):
                        r, n = divmod(b, nbr)
                        ncol = 256 if n > 0 else 128
                        for hh in range(2):
                            bi = (b % 2) * 2 + hh
                            act(pT[bi][:, 0:ncol], tmp[bi][:, 0:ncol], AF.Exp, [Btmp[bi]], [BpT[bi]])

                    def emit_PV(b):
                        r, n = divmod(b, nbr)
                        hp = n > 0
                        nzb = 6 + (b % 2)
                        for hh in range(2):
                            bi = (b % 2) * 2 + hh
                            ps_ = slice(hh * 64, (hh + 1) * 64)
                            mm(banks[nzb][ps_, 0:128], vS[sl][:, b, ps_], pT[bi][:, 0:128], True, not hp,
                               [Bv[sl], BpT[bi]], [Bbank[nzb]])
                            if hp:
                                mm(banks[nzb][ps_, 0:128], vS[sl][:, b - 1, ps_], pT[bi][:, 128:256], False, True,
                                   [Bv[sl], BpT[bi]], [Bbank[nzb]])
                            mm(banks[nzb][ps_, 128:256], ones_b[:, 0:64], pT[bi][:, 0:128], True, not hp,
                               [Bcst, BpT[bi]], [Bbank[nzb]])
                            if hp:
                                mm(banks[nzb][ps_, 128:256], ones_b[:, 0:64], pT[bi][:, 128:256], False, True,
                                   [Bcst, BpT[bi]], [Bbank[nzb]])

                    def emit_evac(b):
                        r, n = divmod(b, nbr)
                        nzb = 6 + (b % 2)
                        src = banks[nzb][:, 0:256].rearrange("p (a q) -> p a q", a=2)
                        if d == 1:
                            dst = acc[:, :, b * 128:(b + 1) * 128]
                        else:
                            dst = accv[:, :, r, n, :]
                        if g == 0:
                            tcopy("act", dst, src, [Bbank[nzb]], [Bacc])
                        else:
                            tt("dve", dst, src, dst, ALU.add, [Bbank[nzb], Bacc], [Bacc])

                    for k in range(32 + 4):
                        if k < 32:
                            emit_S(k)
                        if 0 <= k - 1 < 32:
                            emit_add(k - 1)
                        if 0 <= k - 2 < 32:
                            emit_exp(k - 2)
                        if 0 <= k - 3 < 32:
                            emit_PV(k - 3)
                        if 0 <= k - 4 < 32:
                            emit_evac(k - 4)
                    if g == 2:
                        for hf in range(4):
                            cs = slice(hf * 1024, (hf + 1) * 1024)
                            S.op("dve", lambda e, cs=cs: e.reciprocal(out=acc[:, 1, cs], in_=acc[:, 1, cs]), [Bacc], [Bacc])
                            tt("dve", osb[:, cs], acc[:, 0, cs], acc[:, 1, cs], ALU.mult, [Bacc], [Bosb])
                        S.dma("sp", ao_d[j], osb[:], [Bosb], [Bao])
            S.barrier()

        def phase_poolz(l):
            with ExitStack() as ps:
                Wp = sb("pz_wp", [128, 8, 1024], BF16, ps)
                Wz = sb("pz_wz", [128, 8, 1024], BF16, ps)
                pw = sb("pz_pw", [128, 4, 2, 256], BF16, ps)
                wb = sb("pz_wb", [128, 8, 1024], BF16, ps)
                Wg = sb("pz_wg", [128, 8, 1024], BF16, ps)
                BWp, BWz, Bpw, Bwb, BWg = [Buf() for _ in range(5)]
                w_l = w_in[l].rearrange("(c p) f -> p c f", p=128)
                wload(Wp[:], w_l[:, :, P0:P0 + 1024], BWp)
                for g in range(4):
                    wload(pw[:, g, :, :], pool_w[l, g].rearrange("(cc p) d -> p cc d", p=128), Bpw)
                wload(wb[:], w_b[l].rearrange("(c p) f -> p c f", p=128), Bwb)
                wload(Wg[:], w_l[:, :, G0 + 1024:G0 + 2048], BWg)
                wload(Wz[:], w_l[:, :, Z0:Z0 + 1024], BWz)
                pin = [sb("pz_pin%d" % i, [128, 1024], BF16, ps) for i in range(2)]
                Bpin = [Buf(), Buf()]
                dT = sb("pz_dT", [128, 8, 512], BF16, ps)
                BdT = Buf()
                yp = sb("pz_yp", [128, 8, 512], BF16, ps)
                Byp = Buf()
                g1 = [sb("pz_g%d" % i, [128, 512], F32, ps) for i in range(2)]
                Bg1 = [Buf(), Buf()]
                mac = [sb("pz_mac%d" % i, [128, 8, 512], BF16, ps) for i in range(2)]
                Bm = [Buf(), Buf()]
                zst = [sb("pz_zs%d" % i, [128, 8, 512], BF16, ps) for i in range(2)]
                Bz = [Buf(), Buf()]

                def nb4():
                    bk = rr["bank"] % 4
                    rr["bank"] += 1
                    return bk

                for t in range(NT):
                    t0 = t * 512
                    uRt = uR(t0, t0 + 512)
                    for blk in range(4):
                        b = t * 4 + blk
                        tok = b * 128
                        for half in range(2):
                            bk = nb4()
                            for kc in range(8):
                                mm(banks[bk][:, :], uT[:, kc, tok:tok + 128], Wp[:, kc, half * 512:(half + 1) * 512],
                                   kc == 0, kc == 7, [BWp] + uRt, [Bbank[bk]])
                            tcopy(ev_eng(), pin[b % 2][:, half * 512:(half + 1) * 512], banks[bk][:, :],
                                  [Bbank[bk]], [Bpin[b % 2]])
                        for half in range(2):
                            dbk = 4 + half
                            for c4 in range(4):
                                c = half * 4 + c4
                                gi = c // 2
                                mm(banks[dbk][:, c4 * 128:(c4 + 1) * 128], pin[b % 2][:, c * 128:(c + 1) * 128],
                                   band(2 if b == 0 else 0, gi), True, b == 0, [Bpin[b % 2], Bcst], [Bbank[dbk]])
                                if b > 0:
                                    mm(banks[dbk][:, c4 * 128:(c4 + 1) * 128], pin[(b - 1) % 2][:, c * 128:(c + 1) * 128],
                                       band(1, gi), False, True, [Bpin[(b - 1) % 2], Bcst], [Bbank[dbk]])
                            tcopy(ev_eng(), dT[:, half * 4:(half + 1) * 4, blk * 128:(blk + 1) * 128],
                                  banks[dbk][:, :].rearrange("p (c q) -> p c q", c=4), [Bbank[dbk]], [BdT])
                    for oc in range(8):
                        g, dc = divmod(oc, 2)
                        bk = nb4()
                        for cc in range(2):
                            mm(banks[bk][:, :], pw[:, g, cc, dc * 128:(dc + 1) * 128], dT[:, g * 2 + cc, :],
                               cc == 0, cc == 1, [Bpw, BdT], [Bbank[bk]])
                        act(yp[:, oc, :], banks[bk][:, :], AF.Identity, [Bbank[bk], Bpp], [Byp], scale=ppc(l, PP_PSC + oc))
                    for oc in range(8):
                        bka = nb4()
                        for kc in range(8):
                            mm(banks[bka][:, :], wb[:, kc, oc * 128:(oc + 1) * 128], yp[:, kc, :], kc == 0, kc == 7,
                               [Bwb, Byp], [Bbank[bka]])
                        bkg = nb4()
                        for kc in range(8):
                            mm(banks[bkg][:, :], Wg[:, kc, oc * 128:(oc + 1) * 128], uT[:, kc, t0:t0 + 512], kc == 0, kc == 7,
                               [BWg] + uRt, [Bbank[bkg]])
                        act(g1[oc % 2][:, :], banks[bkg][:, :], AF.Sigmoid, [Bbank[bkg], Bpp], [Bg1[oc % 2]],
                            bias=ppc(l, PP_BG + 8 + oc))
                        tt("dve", mac[t % 2][:, oc, :], banks[bka][:, :], g1[oc % 2][:, :], ALU.mult,
                           [Bbank[bka], Bg1[oc % 2]], [Bm[t % 2]])
                    S.dma("sp", mac_d[t], mac[t % 2][:], [Bm[t % 2]], [Bmac[t]])
                    for oc in range(8):
                        bk = nb4()
                        for kc in range(8):
                            mm(banks[bk][:, :], Wz[:, kc, oc * 128:(oc + 1) * 128], uT[:, kc, t0:t0 + 512], kc == 0, kc == 7,
                               [BWz] + uRt, [Bbank[bk]])
                        act(zst[t % 2][:, oc, :], banks[bk][:, :], AF.Silu, [Bbank[bk]], [Bz[t % 2]])
                    S.dma("sp", zs_d[t], zst[t % 2][:], [Bz[t % 2]], [Bzs[t]])
            S.barrier()

        def phase_ssd(l):
            with ExitStack() as ps:
                Wx = sb("sd_wx", [128, 8, 1536], BF16, ps)
                Wd = sb("sd_wd", [128, 8, 16], BF16, ps)
                BWx, BWd = Buf(), Buf()
                w_l = w_in[l].rearrange("(c p) f -> p c f", p=128)
                wload(Wx[:], w_l[:, :, X0:X0 + 1536], BWx)
                wload(Wd[:], w_l[:, :, DT0:DT0 + 16], BWd)
                zt = sb("sd_z", [128, 8, 512], BF16, ps)
                Bzt = Buf()
                xraw = sb("sd_xr", [128, 12, 515], BF16, ps)
                hal = sb("sd_hal", [128, 12, 3], BF16, ps)
                Bxraw = Buf()
                Bhalo = Buf()
                Bhal = Buf()
                ctmp = [sb("sd_ct%d" % i, [128, 512], F32, ps) for i in range(2)]
                Bct = [Buf(), Buf()]
                xcT2 = [sb("sd_xc%d" % i, [128, 12, 512], BF16, ps) for i in range(2)]
                BxcT2 = [Buf(), Buf()]
                yT = sb("sd_y", [128, 8, 512], BF16, ps)
                ByT = Buf()
                sq = [sb("sd_sq%d" % i, [128, 512], BF16, ps) for i in range(2)]
                Bsq = [Buf(), Buf()]
                rstd = sb("sd_rs", [128, 2, 512], F32, ps)
                Brs = Buf()
                A_bc = sb("sd_A", [128, 16], F32, ps)
                BA = Buf()

                def two(name, shape, dt):
                    return [sb("%s%d" % (name, i), shape, dt, ps) for i in range(2)]

                dtp = two("sd_dtp", [128, 4, 16], F32)
                dt_ = two("sd_dt", [128, 4, 16], F32)
                da = two("sd_da", [128, 4, 16], F32)
                acs = two("sd_acs", [128, 4, 16], F32)
                nacs = two("sd_nacs", [128, 4, 16], F32)
                dec = two("sd_dec", [128, 4, 16], F32)
                w2 = two("sd_w2", [128, 4, 16], F32)
                cd = two("sd_cd", [128, 4, 16], F32)
                Bsm = [Buf(), Buf()]
                dtri = two("sd_dtri", [128, 2, 4, 128], BF16)
                Bdabc = [Buf(), Buf()]
                da_h = two("sd_dah", [128, 4, 16], BF16)
                da_l = two("sd_dal", [128, 4, 16], F32)
                xs_tok = sb("sd_xst", [128, 1024], BF16, ps)
                Bxst = Buf()
                xc_tok = two("sd_xct", [128, 1024], BF16)
                xdec = two("sd_xdec", [128, 1024], BF16)
                Bxct, Bxdec = [Buf(), Buf()], [Buf(), Buf()]
                Btok = two("sd_btok", [128, 256], BF16)
                BBtok = [Buf(), Buf()]
                cbm = [two("sd_cbm%d_" % i, [128, 128], F32) for i in range(2)]
                Bcbm = [[Buf(), Buf()], [Buf(), Buf()]]
                E1 = [sb("sd_e1%d" % i, [128, 512], F32, ps) for i in range(3)]
                BE1 = [Buf() for _ in range(3)]
                OD = two("sd_od", [128, 512], F32)
                Gm = two("sd_g", [128, 512], BF16)
                Cod = two("sd_cod", [128, 512], BF16)
                BOD, BG, BCod = [[Buf(), Buf()] for _ in range(3)]
                prev_f = sb("sd_prevf", [128, 1024], F32, ps)
                prev_b2 = [sb("sd_prevb%d" % i, [128, 1024], BF16, ps) for i in range(2)]
                Bpf = Buf()
                Bpb2 = [Buf(), Buf()]

                dskd = sb("sd_dskd", [128, 8, 128], BF16, ps)
                Bdskd = Buf()
                for pair_ in range(8):
                    act(dskd[:, pair_, :], ident, AF.Identity, [Bcst, Bpp], [Bdskd], scale=ppc(l, PP_DSK + pair_))
                act(A_bc[:], ppc(l, PP_ALOG, 16), AF.Exp, [Bpp], [BA])
                ts("dve", A_bc[:], A_bc[:], -1.0, None, ALU.mult, None, [BA], [BA])
                memset("dve", prev_f[:], 0.0, [Bpf])
                memset("dve", prev_b2[0][:], 0.0, [Bpb2[0]])
                memset("dve", hal[:], 0.0, [Bhal])

                tb16 = banks[7][:, :].bitcast(BF16)
                misc = banks[2]
                mb16 = banks[2][:, :].bitcast(BF16)

                def conv(t):
                    t0 = t * 512
                    uRt = uR(t0, t0 + 512)
                    xcT = xcT2[t % 2]
                    BxcT = BxcT2[t % 2]
                    tcopy("pool", xraw[:, :, 0:3], hal[:], [Bhal], [Bhalo])
                    for c in range(12):
                        bk = rr["bank"] % 2
                        rr["bank"] += 1
                        for kc in range(8):
                            mm(banks[bk][:, :], Wx[:, kc, c * 128:(c + 1) * 128], uT[:, kc, t0:t0 + 512], kc == 0, kc == 7,
                               [BWx] + uRt, [Bbank[bk]])
                        tcopy("act", xraw[:, c, 3:515], banks[bk][:, :], [Bbank[bk]], [Bxraw])
                        tcopy("act", hal[:, c, :], banks[bk][:, 509:512], [Bbank[bk]], [Bhal])
                        ci = c % 2
                        wcol = PP_SCW + c * 4
                        ts("dve", ctmp[ci][:, :], xraw[:, c, 3:515], ppc(l, wcol + 3), ppc(l, PP_SCB + c),
                           ALU.mult, ALU.add, [Bxraw, Bpp], [Bct[ci]])
                        for k in (2, 1, 0):
                            stt(ctmp[ci][:, :], xraw[:, c, k:k + 512], ppc(l, wcol + k), ctmp[ci][:, :],
                                ALU.mult, ALU.add, [Bxraw, Bhalo, Bct[ci], Bpp], [Bct[ci]])
                        act(xcT[:, c, :], ctmp[ci][:, :], AF.Silu, [Bct[ci]], [BxcT])

                class _NS:
                    pass

                def make_tile(t):
                    t0 = t * 512
                    uRt = uR(t0, t0 + 512)
                    xcT = xcT2[t % 2]
                    BxcT = BxcT2[t % 2]

                    tp = t % 2
                    A0 = banks[0]

                    def stageA():
                        for ci in range(4):
                            tk0 = t0 + ci * 128
                            for kc in range(8):
                                mm(A0[:, ci * 16:(ci + 1) * 16], uT[:, kc, tk0:tk0 + 128], Wd[:, kc, :], kc == 0, kc == 7,
                                   [BWd] + uRt, [Bbank[0]])
                        v3 = lambda ap: ap.rearrange("p (c h) -> p c h", c=4)
                        tt("dve", dtp[tp][:], v3(A0[:, 0:64]), ppc(l, PP_DTB, 16).unsqueeze(1).to_broadcast([128, 4, 16]),
                           ALU.add, [Bbank[0], Bpp], [Bsm[tp]])
                        act(dtp[tp][:], dtp[tp][:], AF.Exp, [Bsm[tp]], [Bsm[tp]])
                        act(dt_[tp][:], dtp[tp][:], AF.Ln, [Bsm[tp]], [Bsm[tp]], bias=1.0)
                        tt("dve", da[tp][:], dt_[tp][:], A_bc[:].unsqueeze(1).to_broadcast([128, 4, 16]), ALU.mult,
                           [Bsm[tp], BA], [Bsm[tp]])
                        tcopy("dve", da_h[tp][:], da[tp][:], [Bsm[tp]], [Bsm[tp]])
                        tcopy("dve", da[tp][:], da_h[tp][:], [Bsm[tp]], [Bsm[tp]])
                        for ci in range(4):
                            mm(A0[:, 64 + ci * 16:64 + (ci + 1) * 16], tri_f, da[tp][:, ci, :], True, True, [Bcst, Bsm[tp]], [Bbank[0]])
                        for ci in range(4):
                            mm(A0[:, 128 + ci * 16:128 + (ci + 1) * 16], ones_f, da[tp][:, ci, :], True, True, [Bcst, Bsm[tp]], [Bbank[0]])
                        tcopy("dve", acs[tp][:], v3(A0[:, 64:128]), [Bbank[0]], [Bsm[tp]])
                        ts("dve", nacs[tp][:], v3(A0[:, 64:128]), -1.0, None, ALU.mult, None, [Bbank[0]], [Bsm[tp]])
                        tt("dve", dec[tp][:], v3(A0[:, 128:192]), acs[tp][:], ALU.subtract, [Bbank[0], Bsm[tp]], [Bsm[tp]])
                        act(dec[tp][:], dec[tp][:], AF.Exp, [Bsm[tp]], [Bsm[tp]])
                        act(cd[tp][:], v3(A0[:, 128:192]), AF.Exp, [Bbank[0]], [Bsm[tp]])
                        tt("dve", w2[tp][:], dt_[tp][:], dec[tp][:], ALU.mult, [Bsm[tp]], [Bsm[tp]])

                    def stageB(ci):
                        sl = ci % 2
                        cs = slice(ci * 128, ci * 128 + 128)
                        for c in range(8):
                            transp(tb16[:, c * 128:(c + 1) * 128], xcT[:, c, cs], [BxcT, Bcst], [Bbank[7]])
                        for g2 in range(2):
                            transp(mb16[:, 768 + g2 * 128:768 + (g2 + 1) * 128], xcT[:, 8 + g2, cs], [BxcT, Bcst], [Bbank[2]])
                        tcopy("act", Btok[sl][:], mb16[:, 768:1024], [Bbank[2]], [BBtok[sl]])
                        tcopy("act", xs_tok[:], tb16[:, :], [Bbank[7]], [Bxst])
                        tt("dve", xc_tok[sl][:].rearrange("p (h q) -> p h q", h=16), xs_tok[:].rearrange("p (h q) -> p h q", h=16),
                           dt_[tp][:, ci, :].unsqueeze(2).to_broadcast([128, 16, 64]), ALU.mult, [Bxst, Bsm[tp]], [Bxct[sl]])
                        tt("pool", xdec[sl][:].rearrange("p (h q) -> p h q", h=16), xs_tok[:].rearrange("p (h q) -> p h q", h=16),
                           w2[tp][:, ci, :].unsqueeze(2).to_broadcast([128, 16, 64]), ALU.mult, [Bxst, Bsm[tp]], [Bxdec[sl]])
                        for g2 in range(2):
                            mm(misc[:, g2 * 128:(g2 + 1) * 128], xcT[:, 8 + g2, cs], xcT[:, 10 + g2, cs], True, True,
                               [BxcT], [Bbank[2]])
                            tt("dve", cbm[sl][g2][:, :], misc[:, g2 * 128:(g2 + 1) * 128], tri_f, ALU.mult,
                               [Bbank[2], Bcst], [Bcbm[sl][g2]])

                    def states(ci, g2):
                        sl = ci % 2
                        mm(banks[1][:, :], Btok[sl][:, g2 * 128:(g2 + 1) * 128], xdec[sl][:, g2 * 512:(g2 + 1) * 512],
                           True, True, [BBtok[sl], Bxdec[sl]], [Bbank[1]])

                    def stage3a(ci):
                        for g2 in range(2):
                            states(ci, g2)
                            pg = prev_f[:, g2 * 512:(g2 + 1) * 512]
                            tt("dve", pg.rearrange("p (h q) -> p h q", h=8), pg.rearrange("p (h q) -> p h q", h=8),
                               cd[tp][:, ci, g2 * 8:(g2 + 1) * 8].unsqueeze(2).to_broadcast([128, 8, 64]), ALU.mult,
                               [Bpf, Bsm[tp]], [Bpf])
                            tt("dve", pg, banks[1][:, :], pg, ALU.add, [Bbank[1], Bpf], [Bpf])

                    def stage3b(ci):
                        gc = t * 4 + ci
                        tcopy("act", prev_b2[(gc + 1) % 2][:], prev_f[:], [Bpf], [Bpb2[(gc + 1) % 2]])

                    def s0(k):
                        ci, q4 = divmod(k, 4)
                        rb = k % 2
                        rbk = 3 + k % 3
                        hsl = slice(q4 * 4, (q4 + 1) * 4)
                        tri_bc = tri_f.unsqueeze(1).to_broadcast([128, 4, 128])
                        tt("dve", dtri[rb][:, 0, :, :], tri_bc, da[tp][:, ci, hsl].unsqueeze(2).to_broadcast([128, 4, 128]),
                           ALU.mult, [Bsm[tp], Bcst], [Bdabc[rb]])
                        mm(banks[rbk][:, :], ones_b, dtri[rb][:, 0, :, :].rearrange("p h q -> p (h q)"),
                           True, True, [Bdabc[rb], Bcst], [Bbank[rbk]])

                    def s1(k):
                        ci, q4 = divmod(k, 4)
                        rbk = 3 + k % 3
                        e = k % 3
                        for h4 in range(4):
                            h = q4 * 4 + h4
                            act(E1[e][:, h4 * 128:(h4 + 1) * 128], banks[rbk][:, h4 * 128:(h4 + 1) * 128], AF.Exp,
                                [Bbank[rbk], Bsm[tp]], [BE1[e]], bias=nacs[tp][:, ci, h:h + 1])

                    def s2(k):
                        rbk = 3 + k % 3
                        act(OD[k % 2][:, :], banks[rbk][:, :], AF.Exp, [Bbank[rbk]], [BOD[k % 2]])

                    def s3(k):
                        ci, q4 = divmod(k, 4)
                        sl = ci % 2
                        e = k % 3
                        rb = k % 2
                        g2 = q4 // 2
                        cs = slice(ci * 128, ci * 128 + 128)
                        stt(Gm[rb][:].rearrange("p (h q) -> p h q", h=4), E1[e][:].rearrange("p (h q) -> p h q", h=4), 1.0,
                            cbm[sl][g2][:, :].unsqueeze(1).to_broadcast([128, 4, 128]), ALU.min, ALU.mult,
                            [BE1[e], Bcbm[sl][g2]], [BG[rb]])
                        tt("dve", Cod[rb][:].rearrange("p (h q) -> p h q", h=4), OD[rb][:].rearrange("p (h q) -> p h q", h=4),
                           xcT[:, 10 + g2, cs].unsqueeze(1).to_broadcast([128, 4, 128]), ALU.mult,
                           [BOD[rb], BxcT], [BCod[rb]])

                    def s4(k):
                        ci, q4 = divmod(k, 4)
                        sl = ci % 2
                        rb = k % 2
                        gc = t * 4 + ci
                        cs = slice(ci * 128, ci * 128 + 128)
                        for pi in range(2):
                            pair = q4 * 2 + pi
                            yc0 = rb * 256 + pi * 128
                            mm(banks[6][:, yc0:yc0 + 128], dskd[:, pair, :], xcT[:, pair, cs], True, False,
                               [Bdskd, BxcT], [Bbank[6]])
                            for hh in range(2):
                                h = pair * 2 + hh
                                h4 = pi * 2 + hh
                                hs = slice(h * 64, (h + 1) * 64)
                                yo = banks[6][hh * 64:(hh + 1) * 64, yc0:yc0 + 128]
                                mm(yo, xc_tok[sl][:, hs], Gm[rb][:, h4 * 128:(h4 + 1) * 128], False, False, [Bxct[sl], BG[rb]], [Bbank[6]])
                                mm(yo, prev_b2[gc % 2][:, hs], Cod[rb][:, h4 * 128:(h4 + 1) * 128], False, hh == 1,
                                   [Bpb2[gc % 2], BCod[rb]], [Bbank[6]])

                    def s5(k):
                        ci, q4 = divmod(k, 4)
                        rb = k % 2
                        cs = slice(ci * 128, ci * 128 + 128)
                        act(yT[:, q4 * 2:q4 * 2 + 2, cs], banks[6][:, rb * 256:rb * 256 + 256].rearrange("p (a q) -> p a q", a=2),
                            AF.Copy, [Bbank[6]], [ByT])

                    def tail():
                        tt("pool", yT[:, :, :], yT[:, :, :], zt[:, :, :], ALU.mult, [ByT, Bzt], [ByT])
                        for g2 in range(2):
                            nbk = g2
                            for i in range(4):
                                c = g2 * 4 + i
                                act(sq[c % 2][:, :], yT[:, c, :], AF.Square, [ByT], [Bsq[c % 2]])
                                mm(banks[nbk][:, :], ones_b, sq[c % 2][:, :], i == 0, i == 3, [Bsq[c % 2], Bcst], [Bbank[nbk]])
                            act(rstd[:, g2, :], banks[nbk][:, :], AF.Ln, [Bbank[nbk]], [Brs], bias=EPS, scale=1.0 / 512)
                            act(rstd[:, g2, :], rstd[:, g2, :], AF.Exp, [Brs], [Brs], scale=-0.5)
                        for c in range(8):
                            stt(yT[:, c, :], yT[:, c, :], ppc(l, PP_NW + c), rstd[:, c // 4, :], ALU.mult, ALU.mult,
                                [ByT, Brs, Bpp], [ByT])
                        S.dma("sp", yn_d[t], yT[:], [ByT], [Byn[t]])

                    ns = _NS()
                    ns.stageA, ns.stageB, ns.stage3a, ns.stage3b = stageA, stageB, stage3a, stage3b
                    ns.s0, ns.s1, ns.s2, ns.s3, ns.s4, ns.s5, ns.tail = s0, s1, s2, s3, s4, s5, tail
                    return ns

                conv(0)
                T = [make_tile(t) for t in range(NT)]
                S.dma("sp", zt[:], zs_d[0], [Bzs[0]], [Bzt])
                T[0].stageA()
                T[0].stageB(0)
                for t in range(NT):
                    X = T[t]
                    for k in range(16 + 5):
                        if k < 16:
                            X.s0(k)
                        if 0 <= k - 1 < 16:
                            X.s1(k - 1)
                        if 0 <= k - 2 < 16:
                            X.s2(k - 2)
                        if 0 <= k - 3 < 16:
                            X.s3(k - 3)
                        if 0 <= k - 4 < 16:
                            X.s4(k - 4)
                        if 0 <= k - 5 < 16:
                            X.s5(k - 5)
                        if k % 4 == 3 and k // 4 < 4:
                            c = k // 4
                            X.stage3a(c)
                            X.stage3b(c)
                            if c + 1 < 4:
                                X.stageB(c + 1)
                        if k == 2:
                            if t > 0:
                                T[t - 1].tail()
                                S.dma("sp", zt[:], zs_d[t], [Bzs[t]], [Bzt])
                            if t + 1 < NT:
                                conv(t + 1)
                        if k == 9 and t + 1 < NT:
                            T[t + 1].stageA()
                        if k == 17 and t + 1 < NT:
                            T[t + 1].stageB(0)
                T[NT - 1].tail()
            S.barrier()

        def phase_merge(l):
            TT = 256
            with ExitStack() as ps:
                wa = sb("mg_wa", [128, 3, 1024], BF16, ps)
                Wg0 = sb("mg_wg0", [128, 8, 1024], BF16, ps)
                Wg2 = sb("mg_wg2", [128, 8, 1024], BF16, ps)
                wc = sb("mg_wc", [128, 8, 1024], BF16, ps)
                wo = sb("mg_wo", [128, 8, 1024], BF16, ps)
                Bwa, BWg0, BWg2, Bwc, Bwo = [Buf() for _ in range(5)]
                w_l = w_in[l].rearrange("(c p) f -> p c f", p=128)
                wload(wa[:], w_a[l].rearrange("(c p) f -> p c f", p=128), Bwa)
                wload(Wg0[:], w_l[:, :, G0:G0 + 1024], BWg0)
                wload(wc[:], w_c[l].rearrange("(c p) f -> p c f", p=128), Bwc)
                wload(Wg2[:], w_l[:, :, G0 + 2048:G0 + 3072], BWg2)
                wload(wo[:], w_o[l].rearrange("(c p) f -> p c f", p=128), Bwo)
                ao = [sb("mg_ao%d" % i, [128, 3, TT], BF16, ps) for i in range(2)]
                yn = [sb("mg_yn%d" % i, [128, 8, TT], BF16, ps) for i in range(2)]
                mc = [sb("mg_mc%d" % i, [128, 8, TT], BF16, ps) for i in range(2)]
                xt = [sb("mg_xt%d" % i, [128, 8, TT], F32, ps) for i in range(2)]
                Bao_t, Byn_t, Bmc_t, Bxt = [[Buf(), Buf()] for _ in range(4)]
                mg = sb("mg_mg", [128, 8, TT], BF16, ps)
                Bmg = Buf()
                gs = [sb("mg_gs%d" % i, [128, TT], F32, ps) for i in range(4)]
                Bgs = [Buf() for _ in range(4)]
                t1 = [sb("mg_t1%d" % i, [128, TT], F32, ps) for i in range(2)]
                t2 = [sb("mg_t2%d" % i, [128, TT], F32, ps) for i in range(2)]
                Bt1, Bt2 = [Buf(), Buf()], [Buf(), Buf()]
                sq = [sb("mg_sq%d" % i, [128, 512], BF16, ps) for i in range(2)]
                Bsq = [Buf(), Buf()]
                rs = sb("mg_rs", [128, 512], F32, ps)
                Brs = Buf()
                xsrc = xT_v if l == 0 else xr_v
                ao_v = ao_d.rearrange("j p t -> p j t")

                def nb6():
                    bk = rr["bank"] % 6
                    rr["bank"] += 1
                    return bk

                def loads(tt_):
                    p = tt_ % 2
                    a0 = tt_ * TT
                    t5, off = divmod(a0, 512)
                    S.dma("sp", ao[p][:], ao_v[:, :, a0:a0 + TT], [Bao], [Bao_t[p]])
                    S.dma("sp", yn[p][:], yn_d[t5][:, :, off:off + TT], [Byn[t5]], [Byn_t[p]])
                    S.dma("sp", mc[p][:], mac_d[t5][:, :, off:off + TT], [Bmac[t5]], [Bmc_t[p]])
                    S.dma("sp", xt[p][:], xsrc[:, :, a0:a0 + TT], [Bxr[tt_]], [Bxt[p]])

                ntt = SEQ // TT
                loads(0)
                for tt_ in range(ntt):
                    p = tt_ % 2
                    a0 = tt_ * TT
                    if tt_ + 1 < ntt:
                        loads(tt_ + 1)
                    uRt = uR(a0, a0 + TT)
                    for oc in range(8):
                        osl = slice(oc * 128, (oc + 1) * 128)
                        i2 = oc % 2
                        bka = nb6()
                        for kc in range(3):
                            mm(banks[bka][:, 0:TT], wa[:, kc, osl], ao[p][:, kc, :], kc == 0, kc == 2, [Bwa, Bao_t[p]], [Bbank[bka]])
                        bkg = nb6()
                        for kc in range(8):
                            mm(banks[bkg][:, 0:TT], Wg0[:, kc, osl], uT[:, kc, a0:a0 + TT], kc == 0, kc == 7, [BWg0] + uRt, [Bbank[bkg]])
                        act(gs[i2][:, :], banks[bkg][:, 0:TT], AF.Sigmoid, [Bbank[bkg], Bpp], [Bgs[i2]], bias=ppc(l, PP_BG + oc))
                        tt("dve", t1[i2][:, :], banks[bka][:, 0:TT], gs[i2][:, :], ALU.mult, [Bbank[bka], Bgs[i2]], [Bt1[i2]])
                        bkc = nb6()
                        for kc in range(8):
                            mm(banks[bkc][:, 0:TT], wc[:, kc, osl], yn[p][:, kc, :], kc == 0, kc == 7, [Bwc, Byn_t[p]], [Bbank[bkc]])
                        bkg2 = nb6()
                        for kc in range(8):
                            mm(banks[bkg2][:, 0:TT], Wg2[:, kc, osl], uT[:, kc, a0:a0 + TT], kc == 0, kc == 7, [BWg2] + uRt, [Bbank[bkg2]])
                        act(gs[2 + i2][:, :], banks[bkg2][:, 0:TT], AF.Sigmoid, [Bbank[bkg2], Bpp], [Bgs[2 + i2]],
                            bias=ppc(l, PP_BG + 16 + oc))
                        tt("dve", t2[i2][:, :], banks[bkc][:, 0:TT], gs[2 + i2][:, :], ALU.mult, [Bbank[bkc], Bgs[2 + i2]], [Bt2[i2]])
                        tt("pool", t1[i2][:, :], t1[i2][:, :], mc[p][:, oc, :], ALU.add, [Bt1[i2], Bmc_t[p]], [Bt1[i2]])
                        tt("pool", mg[:, oc, :], t1[i2][:, :], t2[i2][:, :], ALU.add, [Bt1[i2], Bt2[i2]], [Bmg])
                    for oc in range(8):
                        osl = slice(oc * 128, (oc + 1) * 128)
                        bk = nb6()
                        for kc in range(8):
                            mm(banks[bk][:, 0:TT], wo[:, kc, osl], mg[:, kc, :], kc == 0, kc == 7, [Bwo, Bmg], [Bbank[bk]])
                        tt("dve", xt[p][:, oc, :], banks[bk][:, 0:TT], xt[p][:, oc, :], ALU.add, [Bbank[bk], Bxt[p]], [Bxt[p]])
                    S.dma("sp", xr_v[:, :, a0:a0 + TT], xt[p][:], [Bxt[p]], [Bxr[tt_]])
                    norm_tile(None, xt[p], Bxt[p], PP_LN2, l, lambda c: uT[:, c, a0:a0 + TT], uRt, TT,
                              banks[6], Bbank[6], sq, Bsq, rs, Brs)
            S.barrier()

        def phase_ffn_up(l):
            with ExitStack() as ps:
                NG = 6
                wu = [sb("fu_w%d" % i, [128, 8, 2, 512], BF16, ps) for i in range(2)]
                Bwu = [Buf(), Buf()]
                raw = [[sb("fu_raw%d%d" % (s_, i), [128, 514], F32, ps) for i in range(2)] for s_ in range(2)]
                Braw = [[Buf(), Buf()], [Buf(), Buf()]]
                Bhal = [[Buf(), Buf()], [Buf(), Buf()]]
                ct = [[sb("fu_ct%d%d" % (s_, i), [128, 512], F32, ps) for i in range(2)] for s_ in range(2)]
                Bct = [[Buf(), Buf()], [Buf(), Buf()]]
                sa = [sb("fu_sa%d" % i, [128, 512], F32, ps) for i in range(2)]
                Bsa = [Buf(), Buf()]
                ab = [sb("fu_ab%d" % i, [128, 512], BF16, ps) for i in range(4)]
                Bab = [Buf() for _ in range(4)]
                wu_l = w_up[l].rearrange("(c p) f -> p c f", p=128)

                def load_g(gi):
                    sl = gi % 2
                    n = min(4, 22 - gi * 4) * 128
                    wload(wu[sl][:, :, 0, 0:n], wu_l[:, :, gi * 512:gi * 512 + n], Bwu[sl])
                    wload(wu[sl][:, :, 1, 0:n], wu_l[:, :, DFF + gi * 512:DFF + gi * 512 + n], Bwu[sl])

                pend = {"v": None, "k": 0}

                def finish(t, j, par):
                    act(sa[par][:, :], ct[0][par][:, :], AF.Silu, [Bct[0][par]], [Bsa[par]])
                    ai = pend["k"] % 4
                    pend["k"] += 1
                    tt("pool", ab[ai][:, :], sa[par][:, :], ct[1][par][:, :], ALU.mult, [Bsa[par], Bct[1][par]], [Bab[ai]])
                    S.dma("sp", act_d[t, :, j, :], ab[ai][:, :], [Bab[ai]], [Bact[t]])

                load_g(0)
                for gi in range(NG):
                    sl = gi % 2
                    if gi + 1 < NG:
                        load_g(gi + 1)
                    for jj in range(min(4, 22 - gi * 4)):
                        j = gi * 4 + jj
                        for s_ in range(2):
                            memset("dve", raw[s_][0][:, 0:2], 0.0, [Bhal[s_][0]])
                        for t in range(NT):
                            t0 = t * 512
                            par = t % 2
                            uRt = uR(t0, t0 + 512)
                            for s_ in range(2):
                                bk = rr["bank"] % 4
                                rr["bank"] += 1
                                ch = j if s_ == 0 else 22 + j
                                for kc in range(8):
                                    mm(banks[bk][:, :], wu[sl][:, kc, s_, jj * 128:(jj + 1) * 128], uT[:, kc, t0:t0 + 512],
                                       kc == 0, kc == 7, [Bwu[sl]] + uRt, [Bbank[bk]])
                                tcopy("act", raw[s_][par][:, 2:514], banks[bk][:, :], [Bbank[bk]], [Braw[s_][par]])
                                tcopy("act", raw[s_][1 - par][:, 0:2], banks[bk][:, 510:512], [Bbank[bk]], [Bhal[s_][1 - par]])
                                wcol = PP_FCW + ch * 3
                                if FFN_ACT_TAP:
                                    act(ct[s_][par][:, :], banks[bk][:, :], AF.Identity, [Bbank[bk], Bpp], [Bct[s_][par]],
                                        bias=ppc(l, PP_FCB + ch), scale=ppc(l, wcol + 2))
                                else:
                                    ts("dve", ct[s_][par][:, :], raw[s_][par][:, 2:514], ppc(l, wcol + 2), ppc(l, PP_FCB + ch),
                                       ALU.mult, ALU.add, [Braw[s_][par], Bpp], [Bct[s_][par]])
                                for kk in (1, 0):
                                    stt(ct[s_][par][:, :], raw[s_][par][:, kk:kk + 512], ppc(l, wcol + kk), ct[s_][par][:, :],
                                        ALU.mult, ALU.add, [Braw[s_][par], Bhal[s_][par], Bct[s_][par], Bpp], [Bct[s_][par]])
                            if pend["v"] is not None:
                                finish(*pend["v"])
                            pend["v"] = (t, j, par)
                finish(*pend["v"])
            S.barrier()

        def phase_ffn_down(l, last):
            TT = 256
            with ExitStack() as ps:
                wd = sb("fd_w", [128, 22, 1024], BF16, ps)
                Bwd = Buf()
                wload(wd[:, 0:11, :], w_down[l, 0:1408].rearrange("(c p) f -> p c f", p=128), Bwd)
                wload(wd[:, 11:22, :], w_down[l, 1408:2816].rearrange("(c p) f -> p c f", p=128), Bwd)
                at = [sb("fd_at%d" % i, [128, 22, 512], BF16, ps) for i in range(2)]
                Bat = [Buf(), Buf()]
                xt = [sb("fd_xt%d" % i, [128, 8, TT], F32, ps) for i in range(2)]
                Bxt = [Buf(), Buf()]
                ot = [sb("fd_ot%d" % i, [128, 8, TT], F32, ps) for i in range(2)] if last else None
                Bot = [Buf(), Buf()]
                sq = [sb("fd_sq%d" % i, [128, 512], BF16, ps) for i in range(2)]
                Bsq = [Buf(), Buf()]
                rs = sb("fd_rs", [128, 512], F32, ps)
                Brs = Buf()
                ntt = SEQ // TT

                def load_a(t):
                    S.dma("sp", at[t % 2][:], act_d[t], [Bact[t]], [Bat[t % 2]])

                def load_x(tt_):
                    a0 = tt_ * TT
                    S.dma("sp", xt[tt_ % 2][:], xr_v[:, :, a0:a0 + TT], [Bxr[tt_]], [Bxt[tt_ % 2]])

                load_a(0)
                load_x(0)
                for tt_ in range(ntt):
                    p = tt_ % 2
                    a0 = tt_ * TT
                    t5, off = divmod(a0, 512)
                    if off == 0 and t5 + 1 < NT:
                        load_a(t5 + 1)
                    if tt_ + 1 < ntt:
                        load_x(tt_ + 1)
                    for oc in range(8):
                        bk = rr["bank"] % 6
                        rr["bank"] += 1
                        for kc in range(22):
                            mm(banks[bk][:, 0:TT], wd[:, kc, oc * 128:(oc + 1) * 128], at[t5 % 2][:, kc, off:off + TT],
                               kc == 0, kc == 21, [Bwd, Bat[t5 % 2]], [Bbank[bk]])
                        tt("dve", xt[p][:, oc, :], banks[bk][:, 0:TT], xt[p][:, oc, :], ALU.add, [Bbank[bk], Bxt[p]], [Bxt[p]])
                    uRt = uR(a0, a0 + TT)
                    if not last:
                        S.dma("sp", xr_v[:, :, a0:a0 + TT], xt[p][:], [Bxt[p]], [Bxr[tt_]])
                        norm_tile(None, xt[p], Bxt[p], PP_LN1, l + 1, lambda c: uT[:, c, a0:a0 + TT], uRt, TT,
                                  banks[6], Bbank[6], sq, Bsq, rs, Brs)
                    else:
                        if debug:
                            S.dma("sp", xr_v[:, :, a0:a0 + TT], xt[p][:], [Bxt[p]], [Bxr[tt_]])
                        norm_tile(None, xt[p], Bxt[p], PP_FG, l, lambda c: ot[p][:, c, :], [Bot[p]], TT,
                                  banks[6], Bbank[6], sq, Bsq, rs, Brs)
                        S.dma("sp", outT_v[:, :, a0:a0 + TT], ot[p][:], [Bot[p]], [Bout])
            S.barrier()

        phase_norm0()
        for l in range(depth):
            for name, fn in (("attn", lambda: phase_attn(l)), ("poolz", lambda: phase_poolz(l)),
                             ("ssd", lambda: phase_ssd(l)), ("merge", lambda: phase_merge(l)),
                             ("ffn_up", lambda: phase_ffn_up(l)),
                             ("ffn_down", lambda: phase_ffn_down(l, l == depth - 1))):
                if stop["flag"]:
                    break
                if name not in SKIP:
                    fn()
                done(l, name)
            if stop["flag"]:
                break
        if stop["flag"]:
            with ExitStack() as ps:
                z = sb("dbg_z", [128, 8, 512], F32, ps)
                Bz_ = Buf()
                memset("dve", z[:], 0.0, [Bz_])
                for t in range(NT):
                    S.dma("sp", outT_v[:, :, t * 512:(t + 1) * 512], z[:], [Bz_], [Bout])

        S.emit(lambda name: es.enter_context(nc.semaphore(name)))
    nc._sched_stats = {e: len(S.streams[e]) for e in ENGS}
    nc._sched_stats["waits"] = S.nwait
    return nc


def _t5_bucket(dist):
    dist = np.asarray(dist, dtype=np.int64)
    max_exact = 16
    nf = np.maximum(dist, 1).astype(np.float32)
    large = max_exact + (np.log(nf / np.float32(max_exact)) / np.float32(math.log(2048 / max_exact))
                         * np.float32(32 - max_exact)).astype(np.int32)
    large = np.minimum(large, 31)
    return np.where(dist < max_exact, dist, large)


def _bias_tables(rel_bias):
    out = np.full((128, 18, 2, 128), NEG, dtype=np.float32)
    j = np.arange(128)[:, None]
    i = np.arange(128)[None, :]
    for g, (win, d) in enumerate(GROUPS):
        rel_c = i - j
        rel_p = i + 128 - j
        for hh in range(6):
            h = g * 6 + hh
            bc = rel_bias[_t5_bucket(np.clip(rel_c, 0, None) * d), h]
            out[:, h, 0, :] = np.where(rel_c >= 0, bc, NEG)
            bp = rel_bias[_t5_bucket(np.clip(rel_p, 0, None) * d), h]
            out[:, h, 1, :] = np.where(rel_p <= 128, bp, NEG)
    return out.reshape(128, 18, 256)


def _constants():
    c = np.zeros((128, NCST), dtype=np.float32)
    c[:, C_ID:C_ID + 128] = np.eye(128, dtype=np.float32)
    c[:, C_ONE:C_ONE + 128] = 1.0
    a = np.arange(128)
    c[:, C_TRI:C_TRI + 128] = (a[:, None] <= a[None, :]).astype(np.float32)
    tp = a[:, None]
    t = a[None, :]
    for gi, w in enumerate(POOLW):
        cur = ((t - tp >= 0) & (t - tp < w)).astype(np.float32) / w - (t == tp).astype(np.float32)
        prev = ((t + 128 - tp) < w).astype(np.float32) / w
        cnt = np.minimum(t + 1, w).astype(np.float32)
        cur0 = ((t - tp >= 0) & (t - tp < w)).astype(np.float32) / cnt - (t == tp).astype(np.float32)
        for kind, m in enumerate((cur, prev, cur0)):
            o = C_BAND + (kind * 4 + gi) * 128
            c[:, o:o + 128] = m
    return c


def _pack_params(inp, depth):
    pp = np.zeros((128, depth, NPP), dtype=np.float32)

    def fm(v, n):
        return np.asarray(v, dtype=np.float32).reshape(n, 128).T

    for l in range(depth):
        pp[:, l, PP_LN1:PP_LN1 + 8] = fm(inp["ln1_g"][l], 8)
        pp[:, l, PP_LN2:PP_LN2 + 8] = fm(inp["ln2_g"][l], 8)
        pp[:, l, PP_BG:PP_BG + 24] = fm(inp["b_gate"][l], 24)
        pp[:, l, PP_PSC:PP_PSC + 8] = fm(inp["pool_scale"][l], 8)
        cw = np.asarray(inp["ssd_conv_w"][l], dtype=np.float32)
        pp[:, l, PP_SCW:PP_SCW + 48] = cw.reshape(4, 12, 128).transpose(2, 1, 0).reshape(128, 48)
        pp[:, l, PP_SCB:PP_SCB + 12] = fm(inp["ssd_conv_b"][l], 12)
        pp[:, l, PP_NW:PP_NW + 8] = fm(inp["ssd_norm_w"][l], 8)
        pp[:, l, PP_DSK:PP_DSK + 8] = fm(np.repeat(np.asarray(inp["ssd_d"][l], dtype=np.float32), 64), 8)
        fw = np.asarray(inp["ffn_conv_w"][l], dtype=np.float32)
        pp[:, l, PP_FCW:PP_FCW + 132] = fw.reshape(3, 44, 128).transpose(2, 1, 0).reshape(128, 132)
        pp[:, l, PP_FCB:PP_FCB + 44] = fm(inp["ffn_conv_b"][l], 44)
        pp[:, l, PP_DTB:PP_DTB + 16] = np.asarray(inp["ssd_dt_bias"][l], dtype=np.float32)[None, :]
        pp[:, l, PP_ALOG:PP_ALOG + 16] = np.asarray(inp["ssd_a_log"][l], dtype=np.float32)[None, :]
        pp[:, l, PP_FG:PP_FG + 8] = fm(inp["final_g"], 8)
    return pp


_CACHE = {}


def make_in_maps(inp, depth, cores):
    f = lambda a: np.ascontiguousarray(np.asarray(a, dtype=np.float32))
    shared = {
        "w_in": f(inp["w_in"][:depth]), "w_a": f(inp["w_a"][:depth]), "pool_w": f(inp["pool_w"][:depth]),
        "w_b": f(inp["w_b"][:depth]), "w_c": f(inp["w_c"][:depth]), "w_o": f(inp["w_o"][:depth]),
        "w_up": f(inp["ffn_w_up"][:depth]), "w_down": f(inp["ffn_w_down"][:depth]),
        "pp": _pack_params(inp, depth), "cst": _constants(), "biasT": _bias_tables(f(inp["rel_bias"])),
    }
    x = np.asarray(inp["x"], dtype=np.float32)
    maps = []
    for b in cores:
        m = dict(shared)
        m["xT"] = np.ascontiguousarray(x[b].T)
        maps.append(m)
    return maps


def kernel(**inputs):
    if "nc" not in _CACHE:
        _CACHE["nc"] = build_program(DEPTH)
    nc = _CACHE["nc"]
    in_maps = make_in_maps(inputs, DEPTH, list(range(NCORES)))
    res = run_bass_kernel_spmd(nc, in_maps, core_ids=list(range(NCORES)))
    out = np.stack([np.ascontiguousarray(r["outT"].T) for r in res.results], axis=0)
    return out.astype(np.float32)
```
